# Optimizing a Trainium2 kernel written in Bass

```python
import jax, jax.numpy as jnp
from jax import lax
import numpy as np

D_MODEL = 1024
BATCH = 16
SEQ = 2048
DEPTH = 2
DEC_BATCH = 32
DEC_SEQ = 64
PAST_LEN = 4096

CHUNK = 64
GMLP_CHUNK = 128
GMLP_GROUPS = 4
GMLP_WIDTH = D_MODEL
GMLP_GROUP_DIM = GMLP_WIDTH // GMLP_GROUPS
HGRN_EXPAND = 128
HGRN_HEADS = D_MODEL // HGRN_EXPAND
HGRN_DK = HGRN_EXPAND
HGRN_DV = D_MODEL // HGRN_HEADS
HGRN_QK_WIDTH = HGRN_HEADS * HGRN_DK
HGRN_WIDTH = HGRN_HEADS * HGRN_DV
HGRN_BLOCK = 16
N_BRANCH = 2
IN_SIZES = (GMLP_WIDTH, GMLP_WIDTH, GMLP_WIDTH, HGRN_QK_WIDTH, HGRN_QK_WIDTH, HGRN_WIDTH, HGRN_WIDTH, D_MODEL, D_MODEL)
IN_COLS = sum(IN_SIZES)
EPS = 1e-6

kernel_name = "gmlp_hgrn2_gated_streaming_step"


def rms_norm(x, g):
    xf = x.astype(jnp.float32)
    r = lax.rsqrt(jnp.mean(xf * xf, axis=-1, keepdims=True) + EPS)
    return (xf * r).astype(x.dtype) * g


def layer_norm(x, g, b):
    xf = x.astype(jnp.float32)
    mu = jnp.mean(xf, axis=-1, keepdims=True)
    var = jnp.mean(jnp.square(xf - mu), axis=-1, keepdims=True)
    return ((xf - mu) * lax.rsqrt(var + EPS)).astype(x.dtype) * g + b


def gmlp_spatial(u, v, w_s, b_s):
    B, T, _ = v.shape
    n = -(-T // GMLP_CHUNK)
    pad = n * GMLP_CHUNK - T
    vp = jnp.pad(v, ((0, 0), (0, pad), (0, 0))).reshape(B, n, GMLP_CHUNK, GMLP_GROUPS, GMLP_GROUP_DIM)
    mask = jnp.tril(jnp.ones((GMLP_CHUNK, GMLP_CHUNK), dtype=bool))
    w = jnp.where(mask[None], w_s, jnp.zeros_like(w_s))
    s = jnp.einsum('gts,bnsgc->bntgc', w, vp) + b_s.T[None, None, :, :, None]
    s = s.reshape(B, n * GMLP_CHUNK, GMLP_WIDTH)[:, :T]
    return u * s


def hgrn2_recurrence(q, k, i, log_f, s0):
    B, T = q.shape[:2]
    n = -(-T // HGRN_BLOCK)
    pad = n * HGRN_BLOCK - T

    def blocks(a):
        a = jnp.pad(a, ((0, 0), (0, pad), (0, 0), (0, 0)))
        return a.reshape(B, n, HGRN_BLOCK, a.shape[2], a.shape[3]).transpose(1, 0, 3, 2, 4)

    qb, kb, ib, fb = blocks(q), blocks(k), blocks(i), blocks(log_f)
    mask = jnp.tril(jnp.ones((HGRN_BLOCK, HGRN_BLOCK), dtype=bool))[None, None, :, :, None]

    def step(S, blk):
        qc, kc, ic, fc = blk
        b = jnp.cumsum(fc, axis=2)
        b_last = b[:, :, -1:, :]
        o_inter = jnp.einsum('bhtk,bhkv->bhtv', qc * jnp.exp(b), S)
        rel = jnp.where(mask, b[:, :, :, None, :] - b[:, :, None, :, :], -jnp.inf)
        a = jnp.einsum('bhtk,bhsk,bhtsk->bhts', qc, kc, jnp.exp(rel))
        o_intra = jnp.einsum('bhts,bhsv->bhtv', a, ic)
        S_new = jnp.exp(b_last[:, :, 0, :])[..., None] * S + jnp.einsum('bhsk,bhsv->bhkv', kc * jnp.exp(b_last - b), ic)
        return S_new, o_inter + o_intra

    S, o = lax.scan(step, s0, (qb, kb, ib, fb))
    o = o.transpose(1, 0, 3, 2, 4).reshape(B, n * HGRN_BLOCK, HGRN_HEADS, HGRN_DV)[:, :T]
    return o, S


def trunk_layer(x, c, s0, lb, w_ada, b_ada, norm_g, w_in, ln_v_g, ln_v_b, w_s, b_s, gnorm_g, w_pa, w_pb, w_o):
    B, T, _ = x.shape
    mod = (jax.nn.silu(c) @ w_ada + b_ada)[:, None, :]
    shift, scale, gate = jnp.split(mod, 3, axis=-1)
    h = rms_norm(x, norm_g) * (1 + scale) + shift
    proj = h @ w_in
    edges, acc = [], 0
    for s in IN_SIZES[:-1]:
        acc += s
        edges.append(acc)
    u, v, z_a, q, f, i, z_b, g_a, g_b = jnp.split(proj, edges, axis=-1)

    u = jax.nn.gelu(u)
    v = layer_norm(jax.nn.gelu(v), ln_v_g, ln_v_b)
    y_a = gmlp_spatial(u, v, w_s, b_s) * jax.nn.silu(z_a)

    qh = jax.nn.silu(q).astype(jnp.float32).reshape(B, T, HGRN_HEADS, HGRN_DK)
    zf = f.astype(jnp.float32).reshape(B, T, HGRN_HEADS, HGRN_DK)
    lbh = lb.reshape(HGRN_HEADS, HGRN_DK)
    log_f = jnp.logaddexp(jnp.log(lbh), jnp.log1p(-lbh) + jax.nn.log_sigmoid(zf))
    k = (1.0 - lbh) * jax.nn.sigmoid(-zf)
    ih = i.astype(jnp.float32).reshape(B, T, HGRN_HEADS, HGRN_DV)
    o, s_new = hgrn2_recurrence(qh, k, ih, log_f, s0.astype(jnp.float32))
    o = rms_norm(o.astype(x.dtype), gnorm_g.reshape(HGRN_HEADS, HGRN_DV)).reshape(B, T, HGRN_WIDTH)
    y_b = o * jax.nn.silu(z_b)

    m = jax.nn.sigmoid(g_a) * (y_a @ w_pa) + jax.nn.sigmoid(g_b) * (y_b @ w_pb)
    x = x + gate * (m @ w_o)
    return x, s_new.astype(x.dtype), v


def setup_inputs(seed: int = 0) -> dict:
    key = jax.random.key(seed)
    ks = jax.random.split(key, 24)
    nrm = jax.random.normal
    D = D_MODEL
    return {
        "x_prompt": nrm(ks[0], (BATCH, SEQ, D), jnp.float32),
        "x_sample": nrm(ks[1], (DEC_BATCH, DEC_SEQ, D), jnp.float32),
        "state_hgrn": 0.5 * nrm(ks[2], (DEPTH, DEC_BATCH, HGRN_HEADS, HGRN_DK, HGRN_DV), jnp.float32),
        "c_prompt": nrm(ks[3], (BATCH, D), jnp.float32),
        "c_sample": nrm(ks[4], (DEC_BATCH, D), jnp.float32),
        "w_ada": 0.5 * D ** -0.5 * nrm(ks[5], (DEPTH, D, 3 * D), jnp.float32),
        "b_ada": 0.01 * nrm(ks[6], (DEPTH, 3 * D), jnp.float32),
        "norm_g": 1.0 + 0.05 * nrm(ks[7], (DEPTH, D), jnp.float32),
        "w_in": D ** -0.5 * nrm(ks[8], (DEPTH, D, IN_COLS), jnp.float32),
        "ln_v_g": 1.0 + 0.05 * nrm(ks[9], (DEPTH, GMLP_WIDTH), jnp.float32),
        "ln_v_b": 0.01 * nrm(ks[10], (DEPTH, GMLP_WIDTH), jnp.float32),
        "w_s": 0.5 * GMLP_CHUNK ** -0.5 * nrm(ks[11], (DEPTH, GMLP_GROUPS, GMLP_CHUNK, GMLP_CHUNK), jnp.float32),
        "b_s": 1.0 + 0.1 * nrm(ks[12], (DEPTH, GMLP_GROUPS, GMLP_CHUNK), jnp.float32),
        "lb_raw": nrm(ks[13], (DEPTH, HGRN_QK_WIDTH), jnp.float32),
        "gnorm_g": 1.0 + 0.05 * nrm(ks[14], (DEPTH, HGRN_WIDTH), jnp.float32),
        "w_pa": GMLP_WIDTH ** -0.5 * nrm(ks[15], (DEPTH, GMLP_WIDTH, D), jnp.float32),
        "w_pb": HGRN_WIDTH ** -0.5 * nrm(ks[16], (DEPTH, HGRN_WIDTH, D), jnp.float32),
        "w_o": D ** -0.5 * nrm(ks[17], (DEPTH, D, D), jnp.float32),
        "final_g": 1.0 + 0.05 * nrm(ks[18], (D,), jnp.float32),
    }


def reference(x_prompt, x_sample, state_hgrn, c_prompt, c_sample, w_ada, b_ada, norm_g, w_in, ln_v_g, ln_v_b,
              w_s, b_s, lb_raw, gnorm_g, w_pa, w_pb, w_o, final_g):
    lbs = jnp.cumsum(jax.nn.softmax(lb_raw.astype(jnp.float32), axis=0), axis=0)
    lbs = lbs - lbs[0:1]
    xp, xs = x_prompt, x_sample
    s_zero = jnp.zeros((x_prompt.shape[0], HGRN_HEADS, HGRN_DK, HGRN_DV), jnp.float32)
    sp_list, ss_list, vs_list = [], [], []
    for l in range(DEPTH):
        wl = (w_ada[l], b_ada[l], norm_g[l], w_in[l], ln_v_g[l], ln_v_b[l], w_s[l], b_s[l], gnorm_g[l],
              w_pa[l], w_pb[l], w_o[l])
        xp, sp, _ = trunk_layer(xp, c_prompt, s_zero, lbs[l], *wl)
        xs, ss, vs = trunk_layer(xs, c_sample, state_hgrn[l], lbs[l], *wl)
        sp_list.append(sp)
        ss_list.append(ss)
        vs_list.append(vs)
    y_prompt = rms_norm(xp, final_g)
    y_sample = rms_norm(xs, final_g)
    new_state_hgrn_prompt = jnp.stack(sp_list)
    new_state_hgrn_sample = jnp.stack(ss_list)
    new_v_gmlp_sample = jnp.stack(vs_list)
    return (y_prompt, y_sample, new_state_hgrn_prompt, new_state_hgrn_sample, new_v_gmlp_sample)
```

```python
import numpy as np
import ml_dtypes
from contextlib import ExitStack
import concourse.bass as bass
import concourse.mybir as mybir
from concourse.bass_utils import run_bass_kernel_spmd

F32 = mybir.dt.float32
BF16 = mybir.dt.bfloat16
AF = mybir.ActivationFunctionType
ALU = mybir.AluOpType
ENGS = ("pe", "act", "dve", "pool", "sp")
EPS = 1e-6
D = 1024
NCOL = 15360
C_U, C_V, C_ZA, C_Q, C_F, C_I, C_ZB = 0, 1024, 2048, 3072, 4096, 5120, 6144
C_ADA, C_MRG, C_O = 7168, 10240, 14336


class Buf:
    __slots__ = ("name", "w", "r", "dsem", "dcnt")

    def __init__(self, name):
        self.name = name
        self.w = None
        self.r = []
        self.dsem = None
        self.dcnt = 0


class Prog:
    def __init__(self, nc, sems):
        self.nc = nc
        self.free_sems = list(sems)
        self.esem = {e: self.free_sems.pop() for e in ("pe", "act", "dve", "pool")}
        self.cnt = {e: 0 for e in ("pe", "act", "dve", "pool")}
        self.known = {e: {} for e in ENGS}
        self.q = {e: [] for e in ENGS}
        self.tag = ""
        self.pe_log = []

    def _need(self, eng, toks):
        best = {}
        for t in toks:
            if t is None:
                continue
            sem, val = t
            if self.known[eng].get(id(sem), 0) >= val:
                continue
            if best.get(id(sem), (None, 0))[1] < val:
                best[id(sem)] = (sem, val)
        out = []
        for sem, val in best.values():
            self.known[eng][id(sem)] = val
            out.append((sem, val))
        return out

    @staticmethod
    def _deps(reads, writes):
        toks = []
        for b in reads:
            toks.append(b.w)
        for b in writes:
            toks.append(b.w)
            toks.extend(b.r)
        return toks

    @staticmethod
    def _commit(tok, reads, writes):
        for b in reads:
            if b not in writes:
                b.r.append(tok)
                if len(b.r) > 64:
                    best = {}
                    for s, v in b.r:
                        if best.get(id(s), (None, 0))[1] < v:
                            best[id(s)] = (s, v)
                    b.r = list(best.values())
        for b in writes:
            b.w = tok
            b.r = []

    @staticmethod
    def _flat(bufs):
        out = []
        for b in bufs:
            if isinstance(b, (list, tuple)):
                out.extend(Prog._flat(b))
            elif b not in out:
                out.append(b)
        return out

    def op(self, eng, fn, reads=(), writes=(), inc=True):
        reads, writes = self._flat(reads), self._flat(writes)
        deps = self._deps(reads, writes)
        if eng == "pe":
            deps = [t for t in deps if t is not None and t[0] is not self.esem["pe"]]
        waits = self._need(eng, deps)
        if inc:
            self.cnt[eng] += 1
            tok = (self.esem[eng], self.cnt[eng])
        else:
            tok = (self.esem[eng], self.cnt[eng] + 1)
        self._commit(tok, reads, writes)
        self.q[eng].append((fn, waits, self.esem[eng] if inc else None))
        if eng == "pe":
            self.pe_log.append((self.tag, [(getattr(w[0], "name", str(w[0])), w[1]) for w in waits]))
        return tok

    def dma(self, qeng, out_ap, in_ap, reads=(), writes=(), sembuf=None):
        reads, writes = self._flat(reads), self._flat(writes)
        sb = sembuf if sembuf is not None else (writes[0] if writes else reads[0])
        if sb.dsem is None:
            sb.dsem = {}
        if qeng not in sb.dsem:
            sb.dsem[qeng] = [self.free_sems.pop(), 0]
        ent = sb.dsem[qeng]
        waits = self._need(qeng, self._deps(reads, writes))
        ent[1] += 16
        tok = (ent[0], ent[1])
        self._commit(tok, reads, writes)

        def fn(e, out_ap=out_ap, in_ap=in_ap):
            return e.dma_start(out=out_ap, in_=in_ap)
        self.q[qeng].append((fn, waits, (ent[0], 16)))
        return tok

    def wait_tokens(self, eng, toks):
        waits = self._need(eng, toks)
        if waits:
            self.q[eng].append((None, waits, None))

    def emit(self, block):
        hmap = {"pe": "tensor", "act": "scalar", "dve": "vector", "pool": "gpsimd", "sp": "sync"}

        def make(eng):
            def body(e):
                for fn, waits, inc in self.q[eng]:
                    if fn is None:
                        for sem, val in waits:
                            e.wait_ge(sem, val)
                        continue
                    if eng == "pe":
                        for sem, val in waits:
                            e.wait_ge(sem, val)
                        ins = fn(e)
                    else:
                        for sem, val in waits[1:]:
                            e.wait_ge(sem, val)
                        ins = fn(e)
                        if waits:
                            ins._wait_ge(waits[0][0], waits[0][1])
                    if inc is not None:
                        if isinstance(inc, tuple):
                            ins.then_inc(inc[0], inc[1])
                        else:
                            ins.then_inc(inc, 1)
            return body
        for eng in ENGS:
            getattr(block, hmap[eng])(make(eng))


def default_tiles():
    tiles = []
    for p in range(2):
        for k in range(4):
            tiles.append(dict(kind="p", tok0=p * 2048 + k * 512, T=512, seq=p, first=(k == 0), last=(k == 3)))
    tiles.append(dict(kind="s", tok0=4096, T=256, seq=None, first=True, last=True))
    return tiles


def build_nc(tiles, ntok, nprompt=2):
    nc = bass.Bass("TRN2", target_bir_lowering=False)
    dram_in = lambda name, shape, dt=F32: nc.dram_tensor(name, list(shape), dt, kind="ExternalInput").ap()
    dram_out = lambda name, shape, dt=F32: nc.dram_tensor(name, list(shape), dt, kind="ExternalOutput").ap()
    xin = dram_in("xin", [ntok, D])
    cT = dram_in("cT", [128, 8, 6])
    s0 = dram_in("s0", [2, 4, 8, 128, 128])
    wall = dram_in("wall", [2, D, NCOL])
    pvx = dram_in("pvx", [128, 2, 32, 6])
    pvg = dram_in("pvg", [128, 24])
    bct = dram_in("bct", [2, 2, 128, D])
    lbr = dram_in("lbr", [2, 128, D])
    wst = dram_in("wst", [2, 2, 128, 4, 128])
    bsr = dram_in("bsr", [2, 2, 128, 512])
    identb_d = dram_in("identb", [128, 128], BF16)
    identf_d = dram_in("identf", [128, 128])
    mru_d = dram_in("mru", [128, 256])
    mask8_d = dram_in("mask8", [128, 512])
    tril_d = dram_in("tril", [128, 128])
    esel_d = dram_in("esel", [128, 64])
    y = dram_out("y", [ntok, D])
    sp_out = dram_out("sp_out", [2, nprompt, 8, 128, 128])
    ss_out = dram_out("ss_out", [2, 4, 8, 128, 128])
    v_out = dram_out("v_out", [2, 256, D])
    wsc = nc.dram_tensor("wsc", [2, NCOL // 512, 128, 8 * 512], BF16).ap()

    with ExitStack() as es:
        def sb(name, shape, dt=F32):
            return es.enter_context(nc.sbuf_tensor("sb_" + name, list(shape), dt))
        xT = sb("xT", [128, 8, 512]); BxTc = [Buf(f"xT{c}") for c in range(8)]
        hT = sb("hT", [128, 8, 512], BF16); BhTc = [Buf(f"hT{c}") for c in range(8)]
        bufA = sb("bufA", [128, 4, 1024]); BAt = [Buf(f"bufA{t}") for t in range(4)]
        guT = sb("guT", [128, 8, 512], BF16); Bguc = [Buf(f"gu{c}") for c in range(8)]
        vtok = sb("vtok", [128, 4, 1024], BF16); Bvt = Buf("vtok")
        itok = sb("itok", [128, 4, 1024], BF16); Bit = Buf("itok")
        ktT = sb("ktT", [128, 4, 1024], BF16); BkT = Buf("ktT")
        qsT = sb("qsT", [128, 8, 512], BF16); Bqs = Buf("qs")
        qtl = sb("qtl", [128, 8, 512], BF16); Bqtc = [Buf(f"qtl{c}") for c in range(8)]
        qdc = sb("qdc", [128, 8, 512], BF16); Bqdc = [Buf(f"qdc{c}") for c in range(8)]
        szb = sb("szb", [128, 8, 512], BF16); Bsz = Buf("szb")
        NTF = 8
        tf = [sb(f"tf{i}", [128, 512]) for i in range(NTF)]; Btf = [Buf(f"tf{i}") for i in range(NTF)]
        ft = [sb(f"ft{i}", [128, 1024]) for i in range(2)]
        Bfh = [Buf(f"fh{i}") for i in range(4)]
        Bft = [[Bfh[0], Bfh[1]], [Bfh[2], Bfh[3]]]
        NW = 3
        wb = [sb(f"wb{i}", [128, 8, 512], BF16) for i in range(NW)]; Bwb = [Buf(f"wb{i}") for i in range(NW)]
        S = sb("S", [128, 2, 8, 128]); BS = [[Buf(f"S{l}_{h}") for h in range(8)] for l in range(2)]
        Sb = sb("Sb", [128, 2, 8, 128], BF16); BSb = [[Buf(f"Sb{l}_{h}") for h in range(8)] for l in range(2)]
        BSst = [Buf("Sst0"), Buf("Sst1")]
        Bvo = [Buf("vo0"), Buf("vo1")]
        bcg = sb("bcg", [128, D]); bcb = sb("bcb", [128, D]); Bbc = Buf("bc")
        lbb = sb("lbb", [128, D]); oml = sb("oml", [128, D]); Blb = Buf("lb")
        WTs = sb("WTs", [128, 2, 2, 4, 128], BF16); BWT = Buf("WTs")
        bsbc = sb("bsbc", [128, 512]); Bbs = Buf("bsbc")
        mask8 = sb("mask8", [128, 512]); mru = sb("mru", [128, 256]); tril = sb("tril", [128, 128])
        identb = sb("identb", [128, 128], BF16); identf = sb("identf", [128, 128]); onesf = sb("onesf", [128, 128])
        esel = sb("esel", [128, 64])
        Bc = Buf("consts")
        mod = sb("mod", [128, 2, 24, 6]); Bmodl = [Buf("mod0"), Buf("mod1")]
        pvxs = sb("pvxs", [128, 2, 32, 6]); pvgs = sb("pvgs", [128, 24])
        cTs = sb("cTs", [128, 8, 6]); scb = sb("scb", [128, 8, 6], BF16); Bsc = Buf("scb")
        ATs = sb("ATs", [128, 4, 512], BF16); BAT = [Buf("ATs0"), Buf("ATs1")]
        U = sb("U", [128, 2, 8, 128]); BU = [Buf("U0"), Buf("U1")]
        ec = sb("ec", [128, 2, 8, 8]); Bec = Buf("ec")
        st6 = sb("st6", [128, 4, 2, 2, 3]); mv = sb("mv", [128, 4, 2]); rsv = sb("rsv", [128, 4]); Bst = Buf("st")
        pbank = [es.enter_context(nc.psum_tensor(f"pb{i}", [128, 512], F32)) for i in range(8)]
        Bpb = [Buf(f"pb{i}") for i in range(8)]
        sems = [es.enter_context(nc.semaphore(f"s{i}")) for i in range(60)]
        block = es.enter_context(nc.Block())
        P = Prog(nc, sems)
        st = dict(pb=0, tf=0, ft=0, wb=0, ev=0, tile=0)

        held = set()
        bank_stamp = [0] * 8

        def nb(hold=False):
            cands = [i for i in range(8) if i not in held]
            i = min(cands, key=lambda j: bank_stamp[j])
            st["pb"] += 1
            bank_stamp[i] = st["pb"]
            if hold:
                held.add(i)
            return pbank[i], Bpb[i]

        def release(bank):
            i = pbank.index(bank)
            held.discard(i)
            st["pb"] += 1
            bank_stamp[i] = st["pb"]

        held_tf = set()

        def ntf(hold=False):
            i = st["tf"]
            while i in held_tf:
                i = (i + 1) % NTF
            st["tf"] = (i + 1) % NTF
            if hold:
                held_tf.add(i)
            return tf[i], Btf[i]

        def release_tf(buf):
            held_tf.discard(tf.index(buf))

        def nft():
            i = st["ft"]; st["ft"] = (i + 1) % 2
            return ft[i], Bft[i]

        Bscr = {}

        def wload(l, col0, ncols=512, keep=True):
            i = st["wb"]; st["wb"] = (i + 1) % NW
            key = (l, col0)
            if key in Bscr:
                assert ncols == 512 and col0 % 512 == 0
                P.dma("sp", wb[i][:, :, :].rearrange("p a b -> p (a b)"), wsc[l, col0 // 512], reads=[Bscr[key]], writes=[Bwb[i]])
            else:
                src = wall[l].rearrange("(kc p) n -> p kc n", p=128)[:, :, col0:col0 + ncols]
                P.dma("pool", wb[i][:, :, :ncols], src, writes=[Bwb[i]])
                if keep and (st["tile"] > 0 or (col0 // 512) % 2 == 0):
                    Bscr[key] = Buf(f"scr{l}_{col0}")
                    assert ncols == 512 and col0 % 512 == 0
                    P.dma("sp", wsc[l, col0 // 512], wb[i][:, :, :].rearrange("p a b -> p (a b)"), reads=[Bwb[i]], writes=[Bscr[key]], sembuf=Bwb[i])
            return wb[i], Bwb[i]

        def mm(out, lhsT, rhs, start, stop, reads, writes, inc=True, tp=None):
            if tp is None:
                P.op("pe", lambda e: e.matmul(out, lhsT=lhsT, rhs=rhs, start=start, stop=stop), reads, writes, inc)
            else:
                P.op("pe", lambda e: e.matmul(out, lhsT=lhsT, rhs=rhs, start=start, stop=stop, tile_position=tp), reads, writes, inc)

        def act(out, in_, func, reads, writes, scale=1.0, bias=0.0):
            P.op("act", lambda e: e.activation(out=out, in_=in_, func=func, bias=bias, scale=scale), reads, writes)

        def evac(out, in_, reads, writes):
            st["ev"] ^= 1
            if st["ev"]:
                P.op("act", lambda e: e.activation(out=out, in_=in_, func=AF.Copy), reads, writes)
            else:
                P.op("dve", lambda e: e.tensor_copy(out=out, in_=in_), reads, writes)

        def tt(out, in0, in1, op, reads, writes, eng="dve"):
            P.op(eng, lambda e: e.tensor_tensor(out=out, in0=in0, in1=in1, op=op), reads, writes)

        def ts(out, in0, s1, s2, op0, op1, reads, writes, eng="dve"):
            if s2 is None:
                P.op(eng, lambda e: e.tensor_scalar(out=out, in0=in0, scalar1=s1, scalar2=None, op0=op0), reads, writes)
            else:
                P.op(eng, lambda e: e.tensor_scalar(out=out, in0=in0, scalar1=s1, scalar2=s2, op0=op0, op1=op1), reads, writes)

        def stt(out, in0, scalar, in1, op0, op1, reads, writes, eng="dve"):
            P.op(eng, lambda e: e.scalar_tensor_tensor(out=out, in0=in0, scalar=scalar, in1=in1, op0=op0, op1=op1), reads, writes)

        for dst, src in ((identb, identb_d), (identf, identf_d), (mru, mru_d), (mask8, mask8_d), (tril, tril_d), (esel, esel_d)):
            P.dma("sp", dst[:], src[:, :], writes=[Bc])
        P.dma("sp", pvxs[:], pvx[:, :, :, :], writes=[Bc])
        P.dma("sp", pvgs[:], pvg[:, :], writes=[Bc])
        P.dma("sp", cTs[:], cT[:, :, :], writes=[Bc])
        P.op("dve", lambda e: e.memset(onesf[:], 1.0), writes=[Bc])
        P.op("dve", lambda e: e.memset(ATs[:], 0.0), writes=[BAT[0], BAT[1]])
        mask83 = mask8[:, :].rearrange("p (h t) -> p h t", h=8)
        for var in range(2):
            for l in range(2):
                t_, tB = nft()
                P.dma("sp", t_[:, 0:512].rearrange("p (g t) -> p g t", g=4), wst[var, l], writes=[tB])
                for g in range(4):
                    tt(WTs[:, var, l, g, :], t_[:, g * 128:(g + 1) * 128], tril[:], ALU.mult, [tB, Bc], [BWT])
        t0_, t0B = nft(); t1_, t1B = nft()
        P.dma("sp", t0_[:], lbr[0], writes=[t0B])
        P.dma("sp", t1_[:], lbr[1], writes=[t1B])
        tt(t1_[:], t1_[:], t0_[:], ALU.subtract, [t0B, t1B], [t1B])
        act(lbb[:], t1_[:], AF.Sigmoid, [t1B], [Blb])
        ts(oml[:], lbb[:], -1.0, 1.0, ALU.mult, ALU.add, [Blb], [Blb])
        act(scb[:], cTs[:], AF.Silu, [Bc], [Bsc])
        def mod_block(l, blk, pm, pmB):
            w_, wB = wload(l, C_ADA + blk * 512, keep=False)
            for j in range(4):
                jc = blk * 4 + j
                for kc in range(8):
                    mm(pm[:, jc * 6:jc * 6 + 6], w_[:, kc, j * 128:(j + 1) * 128], scb[:, kc, :],
                       kc == 0, kc == 7, [wB, Bsc], [pmB], inc=(kc == 7))

        def mod_finish(l, pm, pmB):
            release(pm)
            tt(mod[:, l].rearrange("p a b -> p (a b)"), pm[:, 0:144],
               pvxs[:, l, 0:24, :].rearrange("p a b -> p (a b)"), ALU.add, [pmB, Bc], [Bmodl[l]])
            stt(mod[:, l, 8:16, :], mod[:, l, 8:16, :], 1.0, pvxs[:, l, 24:32, :], ALU.add, ALU.mult, [Bmodl[l], Bc], [Bmodl[l]])

        pm0 = nb(hold=True)
        for blk in range(6):
            mod_block(0, blk, *pm0)
        mod_finish(0, *pm0)
        deferred = []
        pm1 = nb(hold=True)
        for blk in range(6):
            deferred.append(lambda blk=blk: mod_block(1, blk, *pm1))
        deferred.append(lambda: mod_finish(1, *pm1))

        def hook():
            if deferred:
                deferred.pop(0)()

        def stat_begin():
            return nb(hold=True)

        def stat_sq(c, T):
            q_, qB = ntf()
            act(q_[:, :T], xT[:, c, :T], AF.Square, [BxTc[c]], [qB])
            return (c, q_, qB)

        def stat_mm(pend, T, pn, pnB):
            c, q_, qB = pend
            mm(pn[:, :T], onesf[:], q_[:, :T], c == 0, c == 7, [qB, Bc], [pnB])

        def stat_finish(T, pn, pnB):
            release(pn)
            a_, aB = ntf()
            act(a_[:, :T], pn[:, :T], AF.Ln, [pnB], [aB], scale=1.0 / D, bias=EPS)
            r_, rB = ntf(hold=True)
            act(r_[:, :T], a_[:, :T], AF.Exp, [aB], [rB], scale=-0.5)
            return r_, rB

        def layer(l, tile, rstat, after_hgrn=None):
            T = tile["T"]; NB = T // 128; NCH = T // 64
            if l == 1:
                while deferred:
                    hook()
            var = 0 if tile["kind"] == "p" else 1
            if tile["kind"] == "p":
                segs = [(0, T, tile["seq"])]
            else:
                segs = [(i * 64, (i + 1) * 64, 2 + i) for i in range(4)]
            P.dma("pool", bcg[:], bct[l, 0], writes=[Bbc])
            P.dma("pool", bcb[:], bct[l, 1], writes=[Bbc])
            P.dma("pool", bsbc[:], bsr[var, l], writes=[Bbs])

            def s0_load(c):
                si = c % 2
                P.dma("pool", S[:, si], s0[l, c].rearrange("h k v -> k h v"), writes=[*BS[si]])

            if tile["kind"] == "s":
                s0_load(0)
                s0_load(1)
            P.tag = "S1"
            r_, rB = rstat
            for c in range(8):
                n_, nB = ntf()
                tt(n_[:, :T], xT[:, c, :T], r_[:, :T], ALU.mult, [BxTc[c], rB], [nB])
                for (c0, c1, sq) in segs:
                    act(hT[:, c, c0:c1], n_[:, c0:c1], AF.Identity, [nB, Bmodl[l]], [BhTc[c]],
                        scale=mod[:, l, 8 + c, sq:sq + 1], bias=mod[:, l, c, sq:sq + 1])
            release_tf(r_)
            P.tag = "S2v"
            hook()
            gvt = bufA
            for half in range(2):
                w_, wB = wload(l, C_V + half * 512)
                if half == 0:
                    bks = [nb() for tb in range(NB)]
                    for kc in range(8):
                        for tb in range(NB):
                            mm(bks[tb][0][:, :], hT[:, kc, tb * 128:(tb + 1) * 128], w_[:, kc, :], kc == 0, kc == 7,
                               [BhTc[kc], wB], [bks[tb][1]], inc=(kc == 7))
                    for tb in range(NB):
                        act(gvt[:, tb, 0:512], bks[tb][0][:, :], AF.Gelu_apprx_tanh, [bks[tb][1]], [BAt[tb]])
                    continue
                for tb in range(NB):
                    pv, pvB = nb()
                    for kc in range(8):
                        mm(pv[:, :], hT[:, kc, tb * 128:(tb + 1) * 128], w_[:, kc, :], kc == 0, kc == 7,
                           [*BhTc, wB], [pvB], inc=(kc == 7))
                    act(gvt[:, tb, half * 512:(half + 1) * 512], pv[:, :], AF.Gelu_apprx_tanh, [pvB], [BAt[tb]])
            for tb in range(NB):
                for half in range(2):
                    P.op("dve", lambda e, tb=tb, half=half: e.bn_stats(st6[:, tb, half], gvt[:, tb, half * 512:(half + 1) * 512]),
                         [BAt[tb]], [Bst])
                P.op("dve", lambda e, tb=tb: e.bn_aggr(mv[:, tb, :], st6[:, tb].rearrange("p k o j -> p (k o) j")), [Bst], [Bst])
            act(rsv[:, :NB], mv[:, :NB, 1], AF.Ln, [Bst], [Bst], scale=1.0, bias=EPS)
            act(rsv[:, :NB], rsv[:, :NB], AF.Exp, [Bst], [Bst], scale=-0.5)
            for tb in range(NB):
                stt(gvt[:, tb, :], gvt[:, tb, :], mv[:, tb, 0:1], bcg[:], ALU.subtract, ALU.mult, [BAt[tb], Bst, Bbc], [BAt[tb]])
                if var == 0:
                    stt(vtok[:, tb, :], gvt[:, tb, :], rsv[:, tb:tb + 1], bcb[:], ALU.mult, ALU.add, [BAt[tb], Bst, Bbc], [Bvt])
                else:
                    stt(gvt[:, tb, :], gvt[:, tb, :], rsv[:, tb:tb + 1], bcb[:], ALU.mult, ALU.add, [BAt[tb], Bst, Bbc], [BAt[tb]])
                    P.dma("pool", v_out[l, tb * 128:(tb + 1) * 128, :], gvt[:, tb, :], reads=[BAt[tb]], sembuf=Bvo[tb])
                    P.op("dve", lambda e, tb=tb: e.tensor_copy(out=vtok[:, tb, :], in_=gvt[:, tb, :]), [BAt[tb]], [Bvt])
            P.tag = "Hq"
            hook()

            def silu_proj(col, dst, dB, half, later=None):
                w_, wB = wload(l, col + half * 512)
                for j in range(4):
                    h = half * 4 + j
                    pq, pqB = nb(hold=(later is not None))
                    for kc in range(8):
                        mm(pq[:, :T], w_[:, kc, j * 128:(j + 1) * 128], hT[:, kc, :T], kc == 0, kc == 7,
                           [*BhTc, wB], [pqB], inc=(kc == 7))

                    def ev(h=h, pq=pq, pqB=pqB):
                        release(pq)
                        act(dst[:, h, :T], pq[:, :T], AF.Silu, [pqB], [dB])
                    if later is None:
                        ev()
                    else:
                        later.append(ev)

            P.tag = "Hf"
            hook()
            logf = bufA

            def i_proj(half):
                w_, wB = wload(l, C_I + half * 512)
                for tb in range(NB):
                    pi_, piB = nb()
                    for kc in range(8):
                        mm(pi_[:, :], hT[:, kc, tb * 128:(tb + 1) * 128], w_[:, kc, :], kc == 0, kc == 7,
                           [*BhTc, wB], [piB], inc=(kc == 7))
                    evac(itok[:, tb, half * 512:(half + 1) * 512], pi_[:, :], [piB], [Bit])

            fh = [(ft[i // 2][:, (i % 2) * 512:(i % 2 + 1) * 512], Bfh[i]) for i in range(4)]

            def f_proj(half):
                w_, wB = wload(l, C_F + half * 512)
                for tb in range(NB):
                    f_, fB = fh[tb]
                    pf, pfB = nb()
                    for kc in range(8):
                        mm(pf[:, :], hT[:, kc, tb * 128:(tb + 1) * 128], w_[:, kc, :], kc == 0, kc == 7,
                           [*BhTc, wB], [pfB], inc=(kc == 7))
                    act(f_, pf[:, :], AF.Sigmoid, [pfB], [fB])

            def f_post_a(half):
                hs = slice(half * 512, (half + 1) * 512)
                for tb in range(NB):
                    f_, fB = fh[tb]
                    if l == 1:
                        tt(f_, f_, oml[:, hs], ALU.mult, [fB, Blb], [fB])
                        tt(f_, f_, lbb[:, hs], ALU.add, [fB, Blb], [fB])
                    act(logf[:, tb, hs], f_, AF.Ln, [fB], [BAt[tb]])
                    ts(f_, f_, -1.0, 1.0, ALU.mult, ALU.add, [fB], [fB])

            def f_post_b(half):
                hs = slice(half * 512, (half + 1) * 512)
                for tb in range(NB):
                    f_, fB = fh[tb]
                    pr, prB = nb()
                    mm(pr[:, :], mru[:, 0:128], logf[:, tb, hs], True, True, [Bc, BAt[tb]], [prB])
                    e_, eB = ntf()
                    act(e_[:, :], pr[:, :], AF.Exp, [prB], [eB], scale=-1.0)
                    tt(ktT[:, tb, hs], f_, e_[:, :], ALU.mult, [fB, eB], [BkT])

            for half in range(2):
                f_proj(half)
                i_proj(half)
                f_post_a(half)
                later = []
                silu_proj(C_ZB, szb, Bsz, half, later=later)
                f_post_b(half)
                for ev in later:
                    ev()
            for half in range(2):
                silu_proj(C_Q, qsT, Bqs, half)
            P.tag = "HRB"
            hook()
            for h in range(8):
                erb_t, erbB = ft[h % 2], Bft[h % 2]
                erb = erb_t[:, 0:NB * 256].rearrange("p (n x) -> p n x", x=256)
                pRC = [nb() for _ in range((NB + 1) // 2)]
                for tb in range(NB):
                    pq_, pqB_ = pRC[tb // 2]
                    mm(pq_[:, (tb % 2) * 256:(tb % 2 + 1) * 256], logf[:, tb, h * 128:(h + 1) * 128], mru[:, 0:256], True, True,
                       [BAt[tb], Bc], [pqB_])
                for i2, (pq_, pqB_) in enumerate(pRC):
                    nb2 = min(2, NB - 2 * i2)
                    act(erb_t[:, i2 * 512:i2 * 512 + nb2 * 256], pq_[:, 0:nb2 * 256], AF.Exp, [pqB_], [erbB[i2] if NB > 2 else erbB])
                q3 = lambda t: t[:, h, :T].rearrange("p (n x) -> p n x", x=128)
                tt(q3(qtl), q3(qsT), erb[:, :, 0:128], ALU.mult, [Bqs, erbB], [Bqtc[h]])
                tt(q3(qdc), q3(qsT), erb[:, :, 128:256], ALU.mult, [Bqs, erbB], [Bqdc[h]])
                src = erb_t[:, 0:NB * 256].rearrange("p (n w c x) -> p w n c x", w=2, c=2, x=64)[:, :, :, :, 63]
                dst = ec[:, :, h, 0:NCH].rearrange("p w (n c) -> p w n c", c=2)
                P.op("dve", lambda e, src=src, dst=dst: e.tensor_copy(out=dst, in_=src), [erbB], [Bec])
            P.tag = "Hkt"
            hook()
            ktl = guT
            for h in range(8):
                pt, ptB = nb()
                ptb = pt[:, :].bitcast(BF16)
                for tb in range(NB):
                    P.op("pe", lambda e, h=h, tb=tb, ptb=ptb: e.transpose(ptb[:, tb * 128:(tb + 1) * 128], ktT[:, tb, h * 128:(h + 1) * 128], identb[:]),
                         [BkT, Bc], [ptB])
                P.op("dve", lambda e, h=h, ptb=ptb: e.tensor_copy(out=ktl[:, h, :T], in_=ptb[:, :T]), [ptB], [Bguc[h]])
            P.tag = "Hchunk"
            hook()
            oT = bufA[:, :, :].rearrange("p a b -> p (a b)").rearrange("p (h t) -> p h t", h=8)
            if tile["kind"] == "p" and tile["first"]:
                P.op("dve", lambda e: e.memset(S[:, l], 0.0), [], [*BS[l]])
                P.op("dve", lambda e: e.memset(Sb[:, l], 0.0), [], [*BSb[l]])

            def stage_a(c):
                tb = c // 2; r0 = (c % 2) * 64; rows = slice(r0, r0 + 64)
                cs = slice(c * 64, (c + 1) * 64); c0_ = c * 64
                tp = (0, 64) if r0 else None
                pa, paB = nb()
                for h in range(8):
                    mm(pa[rows, h * 64 + 32:(h + 1) * 64], ktl[:, h, cs], qtl[:, h, c0_ + 32:c0_ + 64], True, True,
                       [Bguc[h], Bqtc[h]], [paB], inc=False, tp=tp)
                    mm(pa[r0:r0 + 32, h * 64:h * 64 + 32], ktl[:, h, c0_:c0_ + 32], qtl[:, h, c0_:c0_ + 32], True, True,
                       [Bguc[h], Bqtc[h]], [paB], inc=(h == 7), tp=tp)
                pa3 = pa[:, :].rearrange("p (h t) -> p h t", h=8)
                at3 = ATs[:, tb, :].rearrange("p (h t) -> p h t", h=8)
                tt(at3[rows, :, 32:64], pa3[rows, :, 32:64], mask83[rows, :, 32:64], ALU.mult, [paB, Bc], [BAT[c % 2]])
                tt(at3[r0:r0 + 32, :, 0:32], pa3[r0:r0 + 32, :, 0:32], mask83[r0:r0 + 32, :, 0:32], ALU.mult, [paB, Bc], [BAT[c % 2]])
                pk = [nb(), nb()]
                for h in range(8):
                    pk_, pkB = pk[h // 4]
                    mm(pk_[:, (h % 4) * 128:(h % 4 + 1) * 128], ktT[rows, tb, h * 128:(h + 1) * 128],
                       itok[rows, tb, h * 128:(h + 1) * 128], True, True, [BkT, Bit], [pkB], inc=(h % 4 == 3))
                for h in range(8):
                    pk_, pkB = pk[h // 4]
                    act(U[:, c % 2, h, :], pk_[:, (h % 4) * 128:(h % 4 + 1) * 128], AF.Identity, [pkB, Bec], [BU[c % 2]],
                        scale=ec[:, 0, h, c:c + 1])

            def stage_b(c):
                tb = c // 2; r0 = (c % 2) * 64; rows = slice(r0, r0 + 64)
                cs = slice(c * 64, (c + 1) * 64)
                si = l if tile["kind"] == "p" else c % 2
                if tile["kind"] == "s":
                    P.op("act", lambda e: e.activation(out=Sb[:, si], in_=S[:, si], func=AF.Copy), [*BS[si]], [*BSb[si]])
                po, poB = nb()
                for h in range(8):
                    mm(po[:, h * 64:(h + 1) * 64], itok[rows, tb, h * 128:(h + 1) * 128], ATs[rows, tb, h * 64:(h + 1) * 64],
                       True, False, [Bit, BAT[c % 2]], [poB], inc=False)
                    mm(po[:, h * 64:(h + 1) * 64], Sb[:, si, h, :], qdc[:, h, cs], False, True, [BSb[si][h], Bqdc[h]], [poB], inc=True)
                evac(oT[:, :, cs], po[:, :].rearrange("p (h t) -> p h t", h=8), [poB], [*BAt])
                for h in range(8):
                    stt(S[:, si, h, :], S[:, si, h, :], ec[:, 1, h, c:c + 1], U[:, c % 2, h, :], ALU.mult, ALU.add,
                        [BS[si][h], Bec, BU[c % 2]], [BS[si][h]])
                    if tile["kind"] == "p":
                        P.op("act", lambda e, h=h: e.activation(out=Sb[:, si, h, :], in_=S[:, si, h, :], func=AF.Copy),
                             [BS[si][h]], [BSb[si][h]])
                if tile["kind"] == "s":
                    P.dma("pool", ss_out[l, c].rearrange("h k v -> k h v"), S[:, si], reads=[*BS[si]], sembuf=BSst[si])
                    if c + 2 < NCH:
                        s0_load(c + 2)

            stage_a(0)
            for c in range(NCH):
                if c + 1 < NCH:
                    stage_a(c + 1)
                stage_b(c)
            if tile["kind"] == "p" and tile["last"]:
                P.dma("pool", sp_out[l, tile["seq"]].rearrange("h k v -> k h v"), S[:, l], reads=[*BS[l]], sembuf=BSst[l])

            def gnorm_sq(h):
                q_, qB = ntf()
                tt(q_[:, :T], oT[:, h, :T], oT[:, h, :T], ALU.mult, [*BAt], [qB])
                return (h, q_, qB)

            def gnorm_mm(pend, gb):
                h, q_, qB = pend
                pn, pnB = nb(hold=True)
                mm(pn[:, :T], onesf[:], q_[:, :T], True, True, [qB, Bc], [pnB])
                gb[h] = (pn, pnB)

            def gnorm_finish(gb):
                rs_ = {}
                for h, (pn, pnB) in gb.items():
                    a_, aB = ntf()
                    release(pn)
                    act(a_[:, :T], pn[:, :T], AF.Ln, [pnB], [aB], scale=1.0 / 128, bias=EPS)
                    rs_[h] = (a_, aB)
                for h, (a_, aB) in rs_.items():
                    act(a_[:, :T], a_[:, :T], AF.Exp, [aB], [aB], scale=-0.5)
                for h, (a_, aB) in rs_.items():
                    tt(a_[:, :T], a_[:, :T], oT[:, h, :T], ALU.mult, [aB, *BAt], [aB])
                    stt(qsT[:, h, :T], a_[:, :T], pvgs[:, l * 8 + h:l * 8 + h + 1], szb[:, h, :T], ALU.mult, ALU.mult,
                        [aB, Bc, Bsz], [Bqs])

            P.tag = "S3a"
            hook()
            pq = []
            groups = [dict(), dict()]
            for half in range(2):
                w_, wB = wload(l, C_U + half * 512)
                for j in range(4):
                    fc = half * 4 + j
                    pu, puB = nb()
                    for kc in range(8):
                        mm(pu[:, :T], w_[:, kc, j * 128:(j + 1) * 128], hT[:, kc, :T], kc == 0, kc == 7,
                           [*BhTc, wB], [puB], inc=(kc == 7))
                    if len(pq) == 2:
                        pd = pq.pop(0)
                        gnorm_mm(pd, groups[pd[0] // 4])
                        if pd[0] == 3:
                            gnorm_finish(groups[0])
                    act(guT[:, fc, :T], pu[:, :T], AF.Gelu_apprx_tanh, [puB], [Bguc[fc]])
                    pq.append(gnorm_sq(fc))
            gn_tail = pq
            P.tag = "S3b"
            hook()
            for half in range(2):
                w_, wB = wload(l, C_ZA + half * 512)
                for j in range(4):
                    fc = half * 4 + j; g = fc // 2
                    pz, pzB = nb()
                    for kc in range(8):
                        mm(pz[:, :T], w_[:, kc, j * 128:(j + 1) * 128], hT[:, kc, :T], kc == 0, kc == 7,
                           [*BhTc, wB], [pzB], inc=(kc == 7))
                    if gn_tail:
                        gnorm_mm(gn_tail.pop(0), groups[1])
                    z_, zB = ntf()
                    act(z_[:, :T], pz[:, :T], AF.Silu, [pzB], [zB])
                    ps_, psB = nb()
                    for tb in range(NB):
                        mm(ps_[:, tb * 128:(tb + 1) * 128], vtok[:, tb, fc * 128:(fc + 1) * 128], WTs[:, var, l, g, :],
                           True, True, [Bvt, BWT], [psB], inc=(tb == NB - 1))
                    tt(z_[:, :T], z_[:, :T], guT[:, fc, :T], ALU.mult, [zB, Bguc[fc]], [zB])
                    s_, sB = ntf()
                    tt(s_[:, :T].rearrange("p (n t) -> p n t", t=128), ps_[:, :T].rearrange("p (n t) -> p n t", t=128),
                       bsbc[:, g * 128:(g + 1) * 128].unsqueeze(1).to_broadcast([128, NB, 128]), ALU.add, [psB, Bbs], [sB])
                    tt(guT[:, fc, :T], s_[:, :T], z_[:, :T], ALU.mult, [sB, zB], [Bguc[fc]])
            P.tag = "S5"
            gnorm_finish(groups[1])
            if after_hgrn is not None:
                after_hgrn()

            def merge_gates(dc):
                w_, wB = wload(l, C_MRG + dc * 512)
                gates = []
                for j in (2, 3):
                    pp, ppB = nb()
                    for kc in range(8):
                        mm(pp[:, :T], w_[:, kc, j * 128:(j + 1) * 128], hT[:, kc, :T], kc == 0, kc == 7,
                           [*BhTc, wB], [ppB], inc=(kc == 7))
                    g_, gB_ = ntf(hold=True)
                    act(g_[:, :T], pp[:, :T], AF.Sigmoid, [ppB], [gB_])
                    gates.append((g_, gB_))
                return w_, wB, gates

            def merge_rest(dc, pre):
                w_, wB, gates = pre
                for j, (src, sB) in ((0, (guT, Bguc)), (1, (qsT, [Bqs]))):
                    pp, ppB = nb()
                    for kc in range(8):
                        mm(pp[:, :T], w_[:, kc, j * 128:(j + 1) * 128], src[:, kc, :T], kc == 0, kc == 7,
                           [sB[kc] if len(sB) == 8 else sB[0], wB], [ppB], inc=(kc == 7))
                    g_, gB_ = gates[j]
                    tt(g_[:, :T], pp[:, :T], g_[:, :T], ALU.mult, [ppB, gB_], [gB_])
                (ga, gaB), (gb, gbB) = gates
                tt(qtl[:, dc, :T], ga[:, :T], gb[:, :T], ALU.add, [gaB, gbB], [Bqtc[dc]])
                release_tf(ga); release_tf(gb)

            pre = merge_gates(0)
            for dc in range(8):
                nxt = merge_gates(dc + 1) if dc + 1 < 8 else None
                merge_rest(dc, pre)
                pre = nxt
            P.tag = "S6"
            pnx, pnxB = stat_begin()
            pend = None
            for half in range(2):
                w_, wB = wload(l, C_O + half * 512)
                for j in range(4):
                    dcc = half * 4 + j
                    pp, ppB = nb()
                    for kc in range(8):
                        mm(pp[:, :T], w_[:, kc, j * 128:(j + 1) * 128], qtl[:, kc, :T], kc == 0, kc == 7,
                           [Bqtc[kc], wB], [ppB], inc=(kc == 7))
                    if pend is not None:
                        stat_mm(pend, T, pnx, pnxB)
                    for (c0, c1, sq) in segs:
                        stt(xT[:, dcc, c0:c1], pp[:, c0:c1], mod[:, l, 16 + dcc, sq:sq + 1], xT[:, dcc, c0:c1],
                            ALU.mult, ALU.add, [ppB, Bmodl[l], BxTc[dcc]], [BxTc[dcc]])
                    pend = stat_sq(dcc, T)
            stat_mm(pend, T, pnx, pnxB)
            return stat_finish(T, pnx, pnxB)


        def load_x(tile):
            T = tile["T"]; NB = T // 128; t0 = tile["tok0"]
            P.dma("pool", bufA[:, :NB, :], xin[t0:t0 + T].rearrange("(tb p) d -> p tb d", p=128), writes=[*BAt])

        load_x(tiles[0])
        for ti, tile in enumerate(tiles):
            st["tile"] = ti
            T = tile["T"]; NB = T // 128; t0 = tile["tok0"]
            xtok = bufA
            P.tag = "Xin"
            pn0, pn0B = stat_begin()
            pend = None
            for c in range(8):
                pt, ptB = nb()
                for tb in range(NB):
                    P.op("pe", lambda e, c=c, tb=tb, pt=pt: e.transpose(pt[:, tb * 128:(tb + 1) * 128], xtok[:, tb, c * 128:(c + 1) * 128], identf[:]),
                         [*BAt, Bc], [ptB])
                if pend is not None:
                    stat_mm(pend, T, pn0, pn0B)
                evac(xT[:, c, :T], pt[:, :T], [ptB], [BxTc[c]])
                pend = stat_sq(c, T)
            stat_mm(pend, T, pn0, pn0B)
            rs = stat_finish(T, pn0, pn0B)
            rs = layer(0, tile, rs)
            nxt = (lambda t=tiles[ti + 1]: load_x(t)) if ti + 1 < len(tiles) else None
            rs = layer(1, tile, rs, after_hgrn=nxt)
            P.tag = "Fin"
            r_, rB = rs
            for c in range(8):
                stt(xT[:, c, :T], xT[:, c, :T], pvgs[:, 16 + c:17 + c], r_[:, :T], ALU.mult, ALU.mult, [BxTc[c], Bc, rB], [BxTc[c]])
            release_tf(r_)
            for tb in range(NB):
                sg_, sgB = nft()
                for half in range(2):
                    pt, ptB = nb()
                    for j in range(4):
                        c = half * 4 + j
                        P.op("pe", lambda e, c=c, j=j, tb=tb, pt=pt: e.transpose(pt[:, j * 128:(j + 1) * 128], xT[:, c, tb * 128:(tb + 1) * 128], identf[:]),
                             [BxTc[c], Bc], [ptB])
                    evac(sg_[:, half * 512:(half + 1) * 512], pt[:, :], [ptB], [sgB[half]])
                P.dma("pool", y[t0 + tb * 128:t0 + (tb + 1) * 128, :], sg_[:], reads=[sgB], sembuf=sgB[0])
        final = []
        for b in (Bvo[0], Bvo[1], BSst[0], BSst[1], Bfh[0], Bfh[2]):
            if b.dsem is not None:
                for ent in b.dsem.values():
                    final.append((ent[0], ent[1]))
        P.wait_tokens("pool", final)
        P.emit(block)
        build_nc.last_prog = P
    return nc


def host_consts():
    identb = np.eye(128, dtype=np.float32).astype(ml_dtypes.bfloat16)
    identf = np.eye(128, dtype=np.float32)
    sp_, t_ = np.meshgrid(np.arange(128), np.arange(128), indexing="ij")
    same = (sp_ // 64) == (t_ // 64)
    sl, tl = sp_ % 64, t_ % 64
    U = (same & (sl <= tl)).astype(np.float32)
    mid = 31
    Mrel = np.where(same & (sl > mid) & (sl <= tl), 1.0, 0.0) - np.where(same & (sl <= mid) & (sl > tl), 1.0, 0.0)
    mru = np.concatenate([Mrel.astype(np.float32), U], axis=1)
    p_, t8 = np.meshgrid(np.arange(128), np.arange(512), indexing="ij")
    mask8 = ((p_ % 64) <= (t8 % 64)).astype(np.float32)
    tril = (sp_ <= t_).astype(np.float32)
    esel = np.zeros((128, 8, 8), np.float32)
    for h in range(8):
        esel[:, h, h] = 1.0
    return dict(identb=identb, identf=identf, mru=np.ascontiguousarray(mru), mask8=mask8, tril=tril,
                esel=esel.reshape(128, 64))


def host_weights(w_ada, b_ada, norm_g, w_in, ln_v_g, ln_v_b, w_s, b_s, lb_raw, gnorm_g, w_pa, w_pb, w_o, final_g):
    f = lambda a: np.asarray(a, dtype=np.float32)
    w_ada, b_ada, norm_g, w_in = f(w_ada), f(b_ada), f(norm_g), f(w_in)
    wall = np.empty((2, D, NCOL), np.float32)
    wall[:, :, 0:7168] = w_in[:, :, 0:7168]
    wall[:, :, C_ADA:C_ADA + 3072] = w_ada
    gA = w_in[:, :, 7168:8192]; gB = w_in[:, :, 8192:9216]
    for dc in range(8):
        s = slice(dc * 128, (dc + 1) * 128)
        o = C_MRG + dc * 512
        wall[:, :, o:o + 128] = f(w_pa)[:, :, s]
        wall[:, :, o + 128:o + 256] = f(w_pb)[:, :, s]
        wall[:, :, o + 256:o + 384] = gA[:, :, s]
        wall[:, :, o + 384:o + 512] = gB[:, :, s]
    wall[:, :, C_O:C_O + 1024] = f(w_o)
    fm = lambda v, n: np.ascontiguousarray(v.reshape(n, 128).T)
    pvx = np.empty((128, 2, 32, 6), np.float32)
    for l in range(2):
        pvx[:, l, 0:24, :] = fm(b_ada[l], 24)[:, :, None]
        pvx[:, l, 24:32, :] = fm(norm_g[l], 8)[:, :, None]
    pvg = np.concatenate([fm(f(gnorm_g)[0], 8), fm(f(gnorm_g)[1], 8), fm(f(final_g), 8)], axis=1)
    bct = np.empty((2, 2, 128, D), np.float32)
    bct[:, 0] = f(ln_v_g)[:, None, :]
    bct[:, 1] = f(ln_v_b)[:, None, :]
    lbr = np.ascontiguousarray(np.broadcast_to(f(lb_raw)[:, None, :], (2, 128, D)))
    w_s, b_s = f(w_s), f(b_s)
    wst = np.zeros((2, 2, 128, 4, 128), np.float32)
    wst[0] = np.transpose(w_s, (0, 3, 1, 2))
    blk = np.transpose(w_s[:, :, 0:64, 0:64], (0, 3, 1, 2))
    wst[1, :, 0:64, :, 0:64] = blk
    wst[1, :, 64:128, :, 64:128] = blk
    bsr = np.empty((2, 2, 128, 512), np.float32)
    bsr[0] = b_s.reshape(2, 1, 512)
    bsr[1] = np.concatenate([b_s[:, :, 0:64], b_s[:, :, 0:64]], axis=2).reshape(2, 1, 512)
    return dict(wall=wall, pvx=pvx, pvg=np.ascontiguousarray(pvg), bct=bct, lbr=lbr, wst=wst, bsr=bsr)


def core_inputs(core, x_prompt, x_sample, state_hgrn, c_prompt, c_sample, shared, npl=2048):
    xp = np.asarray(x_prompt[2 * core:2 * core + 2, :npl], np.float32).reshape(-1, D)
    xs = np.asarray(x_sample[4 * core:4 * core + 4], np.float32).reshape(-1, D)
    c6 = np.concatenate([np.asarray(c_prompt[2 * core:2 * core + 2], np.float32),
                         np.asarray(c_sample[4 * core:4 * core + 4], np.float32)], axis=0)
    cT = np.ascontiguousarray(c6.reshape(6, 8, 128).transpose(2, 1, 0))
    s0 = np.ascontiguousarray(np.asarray(state_hgrn[:, 4 * core:4 * core + 4], np.float32))
    m = dict(shared)
    m.update(xin=np.ascontiguousarray(np.concatenate([xp, xs], axis=0)), cT=cT, s0=s0)
    return m


def kernel(x_prompt, x_sample, state_hgrn, c_prompt, c_sample, w_ada, b_ada, norm_g, w_in, ln_v_g, ln_v_b,
           w_s, b_s, lb_raw, gnorm_g, w_pa, w_pb, w_o, final_g):
    ncores = 8
    shared = host_consts()
    shared.update(host_weights(w_ada, b_ada, norm_g, w_in, ln_v_g, ln_v_b, w_s, b_s, lb_raw, gnorm_g,
                               w_pa, w_pb, w_o, final_g))
    in_maps = [core_inputs(c, x_prompt, x_sample, state_hgrn, c_prompt, c_sample, shared) for c in range(ncores)]
    nc = build_nc(default_tiles(), 4352)
    res = run_bass_kernel_spmd(nc, in_maps, core_ids=list(range(ncores)))
    R = res.results
    y_prompt = np.concatenate([r["y"][:4096].reshape(2, 2048, D) for r in R], axis=0)
    y_sample = np.concatenate([r["y"][4096:].reshape(4, 64, D) for r in R], axis=0)
    sp = np.concatenate([r["sp_out"] for r in R], axis=1)
    ss = np.concatenate([r["ss_out"] for r in R], axis=1)
    vs = np.concatenate([r["v_out"].reshape(2, 4, 64, D) for r in R], axis=1)
    return (y_prompt.astype(np.float32), y_sample.astype(np.float32), sp.astype(np.float32),
            ss.astype(np.float32), vs.astype(np.float32))
```

```python
import numpy as np
import ml_dtypes
from contextlib import ExitStack
import concourse.bass as bass
import concourse.mybir as mybir
from concourse.bass_utils import run_bass_kernel_spmd

F32 = mybir.dt.float32
BF16 = mybir.dt.bfloat16
AF = mybir.ActivationFunctionType
ALU = mybir.AluOpType
ENGS = ("pe", "act", "dve", "pool", "sp")
EPS = 1e-6
D = 1024
NCOL = 15360
C_U, C_V, C_ZA, C_Q, C_F, C_I, C_ZB = 0, 1024, 2048, 3072, 4096, 5120, 6144
C_ADA, C_MRG, C_O = 7168, 10240, 14336


class Buf:
    __slots__ = ("name", "w", "r", "dsem", "dcnt")

    def __init__(self, name):
        self.name = name
        self.w = None
        self.r = []
        self.dsem = None
        self.dcnt = 0


class Prog:
    def __init__(self, nc, sems):
        self.nc = nc
        self.free_sems = list(sems)
        self.esem = {e: self.free_sems.pop() for e in ("pe", "act", "dve", "pool")}
        self.cnt = {e: 0 for e in ("pe", "act", "dve", "pool")}
        self.known = {e: {} for e in ENGS}
        self.q = {e: [] for e in ENGS}
        self.tag = ""
        self.pe_log = []

    def _need(self, eng, toks):
        best = {}
        for t in toks:
            if t is None:
                continue
            sem, val = t
            if self.known[eng].get(id(sem), 0) >= val:
                continue
            if best.get(id(sem), (None, 0))[1] < val:
                best[id(sem)] = (sem, val)
        out = []
        for sem, val in best.values():
            self.known[eng][id(sem)] = val
            out.append((sem, val))
        return out

    @staticmethod
    def _deps(reads, writes):
        toks = []
        for b in reads:
            toks.append(b.w)
        for b in writes:
            toks.append(b.w)
            toks.extend(b.r)
        return toks

    @staticmethod
    def _commit(tok, reads, writes):
        for b in reads:
            if b not in writes:
                b.r.append(tok)
                if len(b.r) > 64:
                    best = {}
                    for s, v in b.r:
                        if best.get(id(s), (None, 0))[1] < v:
                            best[id(s)] = (s, v)
                    b.r = list(best.values())
        for b in writes:
            b.w = tok
            b.r = []

    @staticmethod
    def _flat(bufs):
        out = []
        for b in bufs:
            if isinstance(b, (list, tuple)):
                out.extend(Prog._flat(b))
            elif b not in out:
                out.append(b)
        return out

    def op(self, eng, fn, reads=(), writes=(), inc=True):
        reads, writes = self._flat(reads), self._flat(writes)
        deps = self._deps(reads, writes)
        if eng == "pe":
            deps = [t for t in deps if t is not None and t[0] is not self.esem["pe"]]
        waits = self._need(eng, deps)
        if inc:
            self.cnt[eng] += 1
            tok = (self.esem[eng], self.cnt[eng])
        else:
            tok = (self.esem[eng], self.cnt[eng] + 1)
        self._commit(tok, reads, writes)
        self.q[eng].append((fn, waits, self.esem[eng] if inc else None))
        if eng == "pe":
            self.pe_log.append((self.tag, [(getattr(w[0], "name", str(w[0])), w[1]) for w in waits]))
        return tok

    def dma(self, qeng, out_ap, in_ap, reads=(), writes=(), sembuf=None):
        reads, writes = self._flat(reads), self._flat(writes)
        sb = sembuf if sembuf is not None else (writes[0] if writes else reads[0])
        if sb.dsem is None:
            sb.dsem = {}
        if qeng not in sb.dsem:
            sb.dsem[qeng] = [self.free_sems.pop(), 0]
        ent = sb.dsem[qeng]
        waits = self._need(qeng, self._deps(reads, writes))
        ent[1] += 16
        tok = (ent[0], ent[1])
        self._commit(tok, reads, writes)

        def fn(e, out_ap=out_ap, in_ap=in_ap):
            return e.dma_start(out=out_ap, in_=in_ap)
        self.q[qeng].append((fn, waits, (ent[0], 16)))
        return tok

    def wait_tokens(self, eng, toks):
        waits = self._need(eng, toks)
        if waits:
            self.q[eng].append((None, waits, None))

    def emit(self, block):
        hmap = {"pe": "tensor", "act": "scalar", "dve": "vector", "pool": "gpsimd", "sp": "sync"}

        def make(eng):
            def body(e):
                for fn, waits, inc in self.q[eng]:
                    if fn is None:
                        for sem, val in waits:
                            e.wait_ge(sem, val)
                        continue
                    if eng == "pe":
                        for sem, val in waits:
                            e.wait_ge(sem, val)
                        ins = fn(e)
                    else:
                        for sem, val in waits[1:]:
                            e.wait_ge(sem, val)
                        ins = fn(e)
                        if waits:
                            ins._wait_ge(waits[0][0], waits[0][1])
                    if inc is not None:
                        if isinstance(inc, tuple):
                            ins.then_inc(inc[0], inc[1])
                        else:
                            ins.then_inc(inc, 1)
            return body
        for eng in ENGS:
            getattr(block, hmap[eng])(make(eng))


def default_tiles():
    tiles = []
    for p in range(2):
        for k in range(4):
            tiles.append(dict(kind="p", tok0=p * 2048 + k * 512, T=512, seq=p, first=(k == 0), last=(k == 3)))
    tiles.append(dict(kind="s", tok0=4096, T=256, seq=None, first=True, last=True))
    return tiles


def build_nc(tiles, ntok, nprompt=2):
    nc = bass.Bass("TRN2", target_bir_lowering=False)
    dram_in = lambda name, shape, dt=F32: nc.dram_tensor(name, list(shape), dt, kind="ExternalInput").ap()
    dram_out = lambda name, shape, dt=F32: nc.dram_tensor(name, list(shape), dt, kind="ExternalOutput").ap()
    xin = dram_in("xin", [ntok, D])
    cT = dram_in("cT", [128, 8, 6])
    s0 = dram_in("s0", [2, 4, 8, 128, 128])
    wall = dram_in("wall", [2, D, NCOL])
    pvx = dram_in("pvx", [128, 2, 32, 6])
    pvg = dram_in("pvg", [128, 24])
    bct = dram_in("bct", [2, 2, 128, D])
    lbr = dram_in("lbr", [2, 128, D])
    wst = dram_in("wst", [2, 2, 128, 4, 128])
    bsr = dram_in("bsr", [2, 2, 128, 512])
    identb_d = dram_in("identb", [128, 128], BF16)
    identf_d = dram_in("identf", [128, 128])
    mru_d = dram_in("mru", [128, 256])
    mask8_d = dram_in("mask8", [128, 512])
    tril_d = dram_in("tril", [128, 128])
    esel_d = dram_in("esel", [128, 64])
    y = dram_out("y", [ntok, D])
    sp_out = dram_out("sp_out", [2, nprompt, 8, 128, 128])
    ss_out = dram_out("ss_out", [2, 4, 8, 128, 128])
    v_out = dram_out("v_out", [2, 256, D])
    wsc = nc.dram_tensor("wsc", [2, NCOL // 512, 128, 8 * 512], BF16).ap()

    with ExitStack() as es:
        def sb(name, shape, dt=F32):
            return es.enter_context(nc.sbuf_tensor("sb_" + name, list(shape), dt))
        xT = sb("xT", [128, 8, 512]); BxTc = [Buf(f"xT{c}") for c in range(8)]
        hT = sb("hT", [128, 8, 512], BF16); BhTc = [Buf(f"hT{c}") for c in range(8)]
        bufA = sb("bufA", [128, 4, 1024]); BAt = [Buf(f"bufA{t}") for t in range(4)]
        guT = sb("guT", [128, 8, 512], BF16); Bguc = [Buf(f"gu{c}") for c in range(8)]
        vtok = sb("vtok", [128, 4, 1024], BF16); Bvt = Buf("vtok")
        itok = sb("itok", [128, 4, 1024], BF16); Bit = Buf("itok")
        ktT = sb("ktT", [128, 4, 1024], BF16); BkT = Buf("ktT")
        qsT = sb("qsT", [128, 8, 512], BF16); Bqs = Buf("qs")
        qtl = sb("qtl", [128, 8, 512], BF16); Bqtc = [Buf(f"qtl{c}") for c in range(8)]
        qdc = sb("qdc", [128, 8, 512], BF16); Bqdc = [Buf(f"qdc{c}") for c in range(8)]
        szb = sb("szb", [128, 8, 512], BF16); Bsz = Buf("szb")
        NTF = 8
        tf = [sb(f"tf{i}", [128, 512]) for i in range(NTF)]; Btf = [Buf(f"tf{i}") for i in range(NTF)]
        ft = [sb(f"ft{i}", [128, 1024]) for i in range(2)]
        Bfh = [Buf(f"fh{i}") for i in range(4)]
        Bft = [[Bfh[0], Bfh[1]], [Bfh[2], Bfh[3]]]
        NW = 3
        wb = [sb(f"wb{i}", [128, 8, 512], BF16) for i in range(NW)]; Bwb = [Buf(f"wb{i}") for i in range(NW)]
        S = sb("S", [128, 2, 8, 128]); BS = [[Buf(f"S{l}_{h}") for h in range(8)] for l in range(2)]
        Sb = sb("Sb", [128, 2, 8, 128], BF16); BSb = [[Buf(f"Sb{l}_{h}") for h in range(8)] for l in range(2)]
        BSst = [Buf("Sst0"), Buf("Sst1")]
        Bvo = [Buf("vo0"), Buf("vo1")]
        bcg = sb("bcg", [128, D]); bcb = sb("bcb", [128, D]); Bbc = Buf("bc")
        lbb = sb("lbb", [128, D]); oml = sb("oml", [128, D]); Blb = Buf("lb")
        WTs = sb("WTs", [128, 2, 2, 4, 128], BF16); BWT = Buf("WTs")
        bsbc = sb("bsbc", [128, 512]); Bbs = Buf("bsbc")
        mask8 = sb("mask8", [128, 512]); mru = sb("mru", [128, 256]); tril = sb("tril", [128, 128])
        identb = sb("identb", [128, 128], BF16); identf = sb("identf", [128, 128]); onesf = sb("onesf", [128, 128])
        esel = sb("esel", [128, 64])
        Bc = Buf("consts")
        mod = sb("mod", [128, 2, 24, 6]); Bmodl = [Buf("mod0"), Buf("mod1")]
        pvxs = sb("pvxs", [128, 2, 32, 6]); pvgs = sb("pvgs", [128, 24])
        cTs = sb("cTs", [128, 8, 6]); scb = sb("scb", [128, 8, 6], BF16); Bsc = Buf("scb")
        ATs = sb("ATs", [128, 4, 512], BF16); BAT = [Buf("ATs0"), Buf("ATs1")]
        U = sb("U", [128, 2, 8, 128]); BU = [Buf("U0"), Buf("U1")]
        ec = sb("ec", [128, 2, 8, 8]); Bec = Buf("ec")
        st6 = sb("st6", [128, 4, 2, 2, 3]); mv = sb("mv", [128, 4, 2]); rsv = sb("rsv", [128, 4]); Bst = Buf("st")
        pbank = [es.enter_context(nc.psum_tensor(f"pb{i}", [128, 512], F32)) for i in range(8)]
        Bpb = [Buf(f"pb{i}") for i in range(8)]
        sems = [es.enter_context(nc.semaphore(f"s{i}")) for i in range(60)]
        block = es.enter_context(nc.Block())
        P = Prog(nc, sems)
        st = dict(pb=0, tf=0, ft=0, wb=0, ev=0, tile=0)

        held = set()
        bank_stamp = [0] * 8

        def nb(hold=False):
            cands = [i for i in range(8) if i not in held]
            i = min(cands, key=lambda j: bank_stamp[j])
            st["pb"] += 1
            bank_stamp[i] = st["pb"]
            if hold:
                held.add(i)
            return pbank[i], Bpb[i]

        def release(bank):
            i = pbank.index(bank)
            held.discard(i)
            st["pb"] += 1
            bank_stamp[i] = st["pb"]

        held_tf = set()

        def ntf(hold=False):
            i = st["tf"]
            while i in held_tf:
                i = (i + 1) % NTF
            st["tf"] = (i + 1) % NTF
            if hold:
                held_tf.add(i)
            return tf[i], Btf[i]

        def release_tf(buf):
            held_tf.discard(tf.index(buf))

        def nft():
            i = st["ft"]; st["ft"] = (i + 1) % 2
            return ft[i], Bft[i]

        Bscr = {}

        def wload(l, col0, ncols=512, keep=True):
            i = st["wb"]; st["wb"] = (i + 1) % NW
            key = (l, col0)
            if key in Bscr:
                assert ncols == 512 and col0 % 512 == 0
                P.dma("sp", wb[i][:, :, :].rearrange("p a b -> p (a b)"), wsc[l, col0 // 512], reads=[Bscr[key]], writes=[Bwb[i]])
            else:
                src = wall[l].rearrange("(kc p) n -> p kc n", p=128)[:, :, col0:col0 + ncols]
                P.dma("pool", wb[i][:, :, :ncols], src, writes=[Bwb[i]])
                if keep and (st["tile"] > 0 or (col0 // 512) % 2 == 0):
                    Bscr[key] = Buf(f"scr{l}_{col0}")
                    assert ncols == 512 and col0 % 512 == 0
                    P.dma("sp", wsc[l, col0 // 512], wb[i][:, :, :].rearrange("p a b -> p (a b)"), reads=[Bwb[i]], writes=[Bscr[key]], sembuf=Bwb[i])
            return wb[i], Bwb[i]

        def mm(out, lhsT, rhs, start, stop, reads, writes, inc=True, tp=None):
            if tp is None:
                P.op("pe", lambda e: e.matmul(out, lhsT=lhsT, rhs=rhs, start=start, stop=stop), reads, writes, inc)
            else:
                P.op("pe", lambda e: e.matmul(out, lhsT=lhsT, rhs=rhs, start=start, stop=stop, tile_position=tp), reads, writes, inc)

        def act(out, in_, func, reads, writes, scale=1.0, bias=0.0):
            P.op("act", lambda e: e.activation(out=out, in_=in_, func=func, bias=bias, scale=scale), reads, writes)

        def evac(out, in_, reads, writes):
            st["ev"] ^= 1
            if st["ev"]:
                P.op("act", lambda e: e.activation(out=out, in_=in_, func=AF.Copy), reads, writes)
            else:
                P.op("dve", lambda e: e.tensor_copy(out=out, in_=in_), reads, writes)

        def tt(out, in0, in1, op, reads, writes, eng="dve"):
            P.op(eng, lambda e: e.tensor_tensor(out=out, in0=in0, in1=in1, op=op), reads, writes)

        def ts(out, in0, s1, s2, op0, op1, reads, writes, eng="dve"):
            if s2 is None:
                P.op(eng, lambda e: e.tensor_scalar(out=out, in0=in0, scalar1=s1, scalar2=None, op0=op0), reads, writes)
            else:
                P.op(eng, lambda e: e.tensor_scalar(out=out, in0=in0, scalar1=s1, scalar2=s2, op0=op0, op1=op1), reads, writes)

        def stt(out, in0, scalar, in1, op0, op1, reads, writes, eng="dve"):
            P.op(eng, lambda e: e.scalar_tensor_tensor(out=out, in0=in0, scalar=scalar, in1=in1, op0=op0, op1=op1), reads, writes)

        for dst, src in ((identb, identb_d), (identf, identf_d), (mru, mru_d), (mask8, mask8_d), (tril, tril_d), (esel, esel_d)):
            P.dma("sp", dst[:], src[:, :], writes=[Bc])
        P.dma("sp", pvxs[:], pvx[:, :, :, :], writes=[Bc])
        P.dma("sp", pvgs[:], pvg[:, :], writes=[Bc])
        P.dma("sp", cTs[:], cT[:, :, :], writes=[Bc])
        P.op("dve", lambda e: e.memset(onesf[:], 1.0), writes=[Bc])
        P.op("dve", lambda e: e.memset(ATs[:], 0.0), writes=[BAT[0], BAT[1]])
        mask83 = mask8[:, :].rearrange("p (h t) -> p h t", h=8)
        for var in range(2):
            for l in range(2):
                t_, tB = nft()
                P.dma("sp", t_[:, 0:512].rearrange("p (g t) -> p g t", g=4), wst[var, l], writes=[tB])
                for g in range(4):
                    tt(WTs[:, var, l, g, :], t_[:, g * 128:(g + 1) * 128], tril[:], ALU.mult, [tB, Bc], [BWT])
        t0_, t0B = nft(); t1_, t1B = nft()
        P.dma("sp", t0_[:], lbr[0], writes=[t0B])
        P.dma("sp", t1_[:], lbr[1], writes=[t1B])
        tt(t1_[:], t1_[:], t0_[:], ALU.subtract, [t0B, t1B], [t1B])
        act(lbb[:], t1_[:], AF.Sigmoid, [t1B], [Blb])
        ts(oml[:], lbb[:], -1.0, 1.0, ALU.mult, ALU.add, [Blb], [Blb])
        act(scb[:], cTs[:], AF.Silu, [Bc], [Bsc])
        def mod_block(l, blk, pm, pmB):
            w_, wB = wload(l, C_ADA + blk * 512, keep=False)
            for j in range(4):
                jc = blk * 4 + j
                for kc in range(8):
                    mm(pm[:, jc * 6:jc * 6 + 6], w_[:, kc, j * 128:(j + 1) * 128], scb[:, kc, :],
                       kc == 0, kc == 7, [wB, Bsc], [pmB], inc=(kc == 7))

        def mod_finish(l, pm, pmB):
            release(pm)
            tt(mod[:, l].rearrange("p a b -> p (a b)"), pm[:, 0:144],
               pvxs[:, l, 0:24, :].rearrange("p a b -> p (a b)"), ALU.add, [pmB, Bc], [Bmodl[l]])
            stt(mod[:, l, 8:16, :], mod[:, l, 8:16, :], 1.0, pvxs[:, l, 24:32, :], ALU.add, ALU.mult, [Bmodl[l], Bc], [Bmodl[l]])

        pm0 = nb(hold=True)
        for blk in range(6):
            mod_block(0, blk, *pm0)
        mod_finish(0, *pm0)
        deferred = []
        pm1 = nb(hold=True)
        for blk in range(6):
            deferred.append(lambda blk=blk: mod_block(1, blk, *pm1))
        deferred.append(lambda: mod_finish(1, *pm1))

        def hook():
            if deferred:
                deferred.pop(0)()

        def stat_begin():
            return nb(hold=True)

        def stat_sq(c, T):
            q_, qB = ntf()
            act(q_[:, :T], xT[:, c, :T], AF.Square, [BxTc[c]], [qB])
            return (c, q_, qB)

        def stat_mm(pend, T, pn, pnB):
            c, q_, qB = pend
            mm(pn[:, :T], onesf[:], q_[:, :T], c == 0, c == 7, [qB, Bc], [pnB])

        def stat_finish(T, pn, pnB):
            release(pn)
            a_, aB = ntf()
            act(a_[:, :T], pn[:, :T], AF.Ln, [pnB], [aB], scale=1.0 / D, bias=EPS)
            r_, rB = ntf(hold=True)
            act(r_[:, :T], a_[:, :T], AF.Exp, [aB], [rB], scale=-0.5)
            return r_, rB

        def layer(l, tile, rstat, after_hgrn=None):
            T = tile["T"]; NB = T // 128; NCH = T // 64
            if l == 1:
                while deferred:
                    hook()
            var = 0 if tile["kind"] == "p" else 1
            if tile["kind"] == "p":
                segs = [(0, T, tile["seq"])]
            else:
                segs = [(i * 64, (i + 1) * 64, 2 + i) for i in range(4)]
            P.dma("pool", bcg[:], bct[l, 0], writes=[Bbc])
            P.dma("pool", bcb[:], bct[l, 1], writes=[Bbc])
            P.dma("pool", bsbc[:], bsr[var, l], writes=[Bbs])

            def s0_load(c):
                si = c % 2
                P.dma("pool", S[:, si], s0[l, c].rearrange("h k v -> k h v"), writes=[*BS[si]])

            if tile["kind"] == "s":
                s0_load(0)
                s0_load(1)
            P.tag = "S1"
            r_, rB = rstat
            for c in range(8):
                n_, nB = ntf()
                tt(n_[:, :T], xT[:, c, :T], r_[:, :T], ALU.mult, [BxTc[c], rB], [nB])
                for (c0, c1, sq) in segs:
                    act(hT[:, c, c0:c1], n_[:, c0:c1], AF.Identity, [nB, Bmodl[l]], [BhTc[c]],
                        scale=mod[:, l, 8 + c, sq:sq + 1], bias=mod[:, l, c, sq:sq + 1])
            release_tf(r_)
            P.tag = "S2v"
            hook()
            gvt = bufA
            for half in range(2):
                w_, wB = wload(l, C_V + half * 512)
                if half == 0:
                    bks = [nb() for tb in range(NB)]
                    for kc in range(8):
                        for tb in range(NB):
                            mm(bks[tb][0][:, :], hT[:, kc, tb * 128:(tb + 1) * 128], w_[:, kc, :], kc == 0, kc == 7,
                               [BhTc[kc], wB], [bks[tb][1]], inc=(kc == 7))
                    for tb in range(NB):
                        act(gvt[:, tb, 0:512], bks[tb][0][:, :], AF.Gelu_apprx_tanh, [bks[tb][1]], [BAt[tb]])
                    continue
                for tb in range(NB):
                    pv, pvB = nb()
                    for kc in range(8):
                        mm(pv[:, :], hT[:, kc, tb * 128:(tb + 1) * 128], w_[:, kc, :], kc == 0, kc == 7,
                           [*BhTc, wB], [pvB], inc=(kc == 7))
                    act(gvt[:, tb, half * 512:(half + 1) * 512], pv[:, :], AF.Gelu_apprx_tanh, [pvB], [BAt[tb]])
            for tb in range(NB):
                for half in range(2):
                    P.op("dve", lambda e, tb=tb, half=half: e.bn_stats(st6[:, tb, half], gvt[:, tb, half * 512:(half + 1) * 512]),
                         [BAt[tb]], [Bst])
                P.op("dve", lambda e, tb=tb: e.bn_aggr(mv[:, tb, :], st6[:, tb].rearrange("p k o j -> p (k o) j")), [Bst], [Bst])
            act(rsv[:, :NB], mv[:, :NB, 1], AF.Ln, [Bst], [Bst], scale=1.0, bias=EPS)
            act(rsv[:, :NB], rsv[:, :NB], AF.Exp, [Bst], [Bst], scale=-0.5)
            for tb in range(NB):
                stt(gvt[:, tb, :], gvt[:, tb, :], mv[:, tb, 0:1], bcg[:], ALU.subtract, ALU.mult, [BAt[tb], Bst, Bbc], [BAt[tb]])
                if var == 0:
                    stt(vtok[:, tb, :], gvt[:, tb, :], rsv[:, tb:tb + 1], bcb[:], ALU.mult, ALU.add, [BAt[tb], Bst, Bbc], [Bvt])
                else:
                    stt(gvt[:, tb, :], gvt[:, tb, :], rsv[:, tb:tb + 1], bcb[:], ALU.mult, ALU.add, [BAt[tb], Bst, Bbc], [BAt[tb]])
                    P.dma("pool", v_out[l, tb * 128:(tb + 1) * 128, :], gvt[:, tb, :], reads=[BAt[tb]], sembuf=Bvo[tb])
                    P.op("dve", lambda e, tb=tb: e.tensor_copy(out=vtok[:, tb, :], in_=gvt[:, tb, :]), [BAt[tb]], [Bvt])
            P.tag = "Hq"
            hook()

            def silu_proj(col, dst, dB, half, later=None):
                w_, wB = wload(l, col + half * 512)
                for j in range(4):
                    h = half * 4 + j
                    pq, pqB = nb(hold=(later is not None))
                    for kc in range(8):
                        mm(pq[:, :T], w_[:, kc, j * 128:(j + 1) * 128], hT[:, kc, :T], kc == 0, kc == 7,
                           [*BhTc, wB], [pqB], inc=(kc == 7))

                    def ev(h=h, pq=pq, pqB=pqB):
                        release(pq)
                        act(dst[:, h, :T], pq[:, :T], AF.Silu, [pqB], [dB])
                    if later is None:
                        ev()
                    else:
                        later.append(ev)

            P.tag = "Hf"
            hook()
            logf = bufA

            def i_proj(half):
                w_, wB = wload(l, C_I + half * 512)
                for tb in range(NB):
                    pi_, piB = nb()
                    for kc in range(8):
                        mm(pi_[:, :], hT[:, kc, tb * 128:(tb + 1) * 128], w_[:, kc, :], kc == 0, kc == 7,
                           [*BhTc, wB], [piB], inc=(kc == 7))
                    evac(itok[:, tb, half * 512:(half + 1) * 512], pi_[:, :], [piB], [Bit])

            fh = [(ft[i // 2][:, (i % 2) * 512:(i % 2 + 1) * 512], Bfh[i]) for i in range(4)]

            def f_proj(half):
                w_, wB = wload(l, C_F + half * 512)
                for tb in range(NB):
                    f_, fB = fh[tb]
                    pf, pfB = nb()
                    for kc in range(8):
                        mm(pf[:, :], hT[:, kc, tb * 128:(tb + 1) * 128], w_[:, kc, :], kc == 0, kc == 7,
                           [*BhTc, wB], [pfB], inc=(kc == 7))
                    act(f_, pf[:, :], AF.Sigmoid, [pfB], [fB])

            def f_post_a(half):
                hs = slice(half * 512, (half + 1) * 512)
                for tb in range(NB):
                    f_, fB = fh[tb]
                    if l == 1:
                        tt(f_, f_, oml[:, hs], ALU.mult, [fB, Blb], [fB])
                        tt(f_, f_, lbb[:, hs], ALU.add, [fB, Blb], [fB])
                    act(logf[:, tb, hs], f_, AF.Ln, [fB], [BAt[tb]])
                    ts(f_, f_, -1.0, 1.0, ALU.mult, ALU.add, [fB], [fB])

            def f_post_b(half):
                hs = slice(half * 512, (half + 1) * 512)
                for tb in range(NB):
                    f_, fB = fh[tb]
                    pr, prB = nb()
                    mm(pr[:, :], mru[:, 0:128], logf[:, tb, hs], True, True, [Bc, BAt[tb]], [prB])
                    e_, eB = ntf()
                    act(e_[:, :], pr[:, :], AF.Exp, [prB], [eB], scale=-1.0)
                    tt(ktT[:, tb, hs], f_, e_[:, :], ALU.mult, [fB, eB], [BkT])

            for half in range(2):
                f_proj(half)
                i_proj(half)
                f_post_a(half)
                later = []
                silu_proj(C_ZB, szb, Bsz, half, later=later)
                f_post_b(half)
                for ev in later:
                    ev()
            for half in range(2):
                silu_proj(C_Q, qsT, Bqs, half)
            P.tag = "HRB"
            hook()
            for h in range(8):
                erb_t, erbB = ft[h % 2], Bft[h % 2]
                erb = erb_t[:, 0:NB * 256].rearrange("p (n x) -> p n x", x=256)
                pRC = [nb() for _ in range((NB + 1) // 2)]
                for tb in range(NB):
                    pq_, pqB_ = pRC[tb // 2]
                    mm(pq_[:, (tb % 2) * 256:(tb % 2 + 1) * 256], logf[:, tb, h * 128:(h + 1) * 128], mru[:, 0:256], True, True,
                       [BAt[tb], Bc], [pqB_])
                for i2, (pq_, pqB_) in enumerate(pRC):
                    nb2 = min(2, NB - 2 * i2)
                    act(erb_t[:, i2 * 512:i2 * 512 + nb2 * 256], pq_[:, 0:nb2 * 256], AF.Exp, [pqB_], [erbB[i2] if NB > 2 else erbB])
                q3 = lambda t: t[:, h, :T].rearrange("p (n x) -> p n x", x=128)
                tt(q3(qtl), q3(qsT), erb[:, :, 0:128], ALU.mult, [Bqs, erbB], [Bqtc[h]])
                tt(q3(qdc), q3(qsT), erb[:, :, 128:256], ALU.mult, [Bqs, erbB], [Bqdc[h]])
                src = erb_t[:, 0:NB * 256].rearrange("p (n w c x) -> p w n c x", w=2, c=2, x=64)[:, :, :, :, 63]
                dst = ec[:, :, h, 0:NCH].rearrange("p w (n c) -> p w n c", c=2)
                P.op("dve", lambda e, src=src, dst=dst: e.tensor_copy(out=dst, in_=src), [erbB], [Bec])
            P.tag = "Hkt"
            hook()
            ktl = guT
            for h in range(8):
                pt, ptB = nb()
                ptb = pt[:, :].bitcast(BF16)
                for tb in range(NB):
                    P.op("pe", lambda e, h=h, tb=tb, ptb=ptb: e.transpose(ptb[:, tb * 128:(tb + 1) * 128], ktT[:, tb, h * 128:(h + 1) * 128], identb[:]),
                         [BkT, Bc], [ptB])
                P.op("dve", lambda e, h=h, ptb=ptb: e.tensor_copy(out=ktl[:, h, :T], in_=ptb[:, :T]), [ptB], [Bguc[h]])
            P.tag = "Hchunk"
            hook()
            oT = bufA[:, :, :].rearrange("p a b -> p (a b)").rearrange("p (h t) -> p h t", h=8)
            if tile["kind"] == "p" and tile["first"]:
                P.op("dve", lambda e: e.memset(S[:, l], 0.0), [], [*BS[l]])
                P.op("dve", lambda e: e.memset(Sb[:, l], 0.0), [], [*BSb[l]])

            def stage_a(c):
                tb = c // 2; r0 = (c % 2) * 64; rows = slice(r0, r0 + 64)
                cs = slice(c * 64, (c + 1) * 64); c0_ = c * 64
                tp = (0, 64) if r0 else None
                pa, paB = nb()
                for h in range(8):
                    mm(pa[rows, h * 64 + 32:(h + 1) * 64], ktl[:, h, cs], qtl[:, h, c0_ + 32:c0_ + 64], True, True,
                       [Bguc[h], Bqtc[h]], [paB], inc=False, tp=tp)
                    mm(pa[r0:r0 + 32, h * 64:h * 64 + 32], ktl[:, h, c0_:c0_ + 32], qtl[:, h, c0_:c0_ + 32], True, True,
                       [Bguc[h], Bqtc[h]], [paB], inc=(h == 7), tp=tp)
                pa3 = pa[:, :].rearrange("p (h t) -> p h t", h=8)
                at3 = ATs[:, tb, :].rearrange("p (h t) -> p h t", h=8)
                tt(at3[rows, :, 32:64], pa3[rows, :, 32:64], mask83[rows, :, 32:64], ALU.mult, [paB, Bc], [BAT[c % 2]])
                tt(at3[r0:r0 + 32, :, 0:32], pa3[r0:r0 + 32, :, 0:32], mask83[r0:r0 + 32, :, 0:32], ALU.mult, [paB, Bc], [BAT[c % 2]])
                pk = [nb(), nb()]
                for h in range(8):
                    pk_, pkB = pk[h // 4]
                    mm(pk_[:, (h % 4) * 128:(h % 4 + 1) * 128], ktT[rows, tb, h * 128:(h + 1) * 128],
                       itok[rows, tb, h * 128:(h + 1) * 128], True, True, [BkT, Bit], [pkB], inc=(h % 4 == 3))
                for h in range(8):
                    pk_, pkB = pk[h // 4]
                    act(U[:, c % 2, h, :], pk_[:, (h % 4) * 128:(h % 4 + 1) * 128], AF.Identity, [pkB, Bec], [BU[c % 2]],
                        scale=ec[:, 0, h, c:c + 1])

            def stage_b(c):
                tb = c // 2; r0 = (c % 2) * 64; rows = slice(r0, r0 + 64)
                cs = slice(c * 64, (c + 1) * 64)
                si = l if tile["kind"] == "p" else c % 2
                if tile["kind"] == "s":
                    P.op("act", lambda e: e.activation(out=Sb[:, si], in_=S[:, si], func=AF.Copy), [*BS[si]], [*BSb[si]])
                po, poB = nb()
                for h in range(8):
                    mm(po[:, h * 64:(h + 1) * 64], itok[rows, tb, h * 128:(h + 1) * 128], ATs[rows, tb, h * 64:(h + 1) * 64],
                       True, False, [Bit, BAT[c % 2]], [poB], inc=False)
                    mm(po[:, h * 64:(h + 1) * 64], Sb[:, si, h, :], qdc[:, h, cs], False, True, [BSb[si][h], Bqdc[h]], [poB], inc=True)
                evac(oT[:, :, cs], po[:, :].rearrange("p (h t) -> p h t", h=8), [poB], [*BAt])
                for h in range(8):
                    stt(S[:, si, h, :], S[:, si, h, :], ec[:, 1, h, c:c + 1], U[:, c % 2, h, :], ALU.mult, ALU.add,
                        [BS[si][h], Bec, BU[c % 2]], [BS[si][h]])
                    if tile["kind"] == "p":
                        P.op("act", lambda e, h=h: e.activation(out=Sb[:, si, h, :], in_=S[:, si, h, :], func=AF.Copy),
                             [BS[si][h]], [BSb[si][h]])
                if tile["kind"] == "s":
                    P.dma("pool", ss_out[l, c].rearrange("h k v -> k h v"), S[:, si], reads=[*BS[si]], sembuf=BSst[si])
                    if c + 2 < NCH:
                        s0_load(c + 2)

            stage_a(0)
            for c in range(NCH):
                if c + 1 < NCH:
                    stage_a(c + 1)
                stage_b(c)
            if tile["kind"] == "p" and tile["last"]:
                P.dma("pool", sp_out[l, tile["seq"]].rearrange("h k v -> k h v"), S[:, l], reads=[*BS[l]], sembuf=BSst[l])

            def gnorm_sq(h):
                q_, qB = ntf()
                tt(q_[:, :T], oT[:, h, :T], oT[:, h, :T], ALU.mult, [*BAt], [qB])
                return (h, q_, qB)

            def gnorm_mm(pend, gb):
                h, q_, qB = pend
                pn, pnB = nb(hold=True)
                mm(pn[:, :T], onesf[:], q_[:, :T], True, True, [qB, Bc], [pnB])
                gb[h] = (pn, pnB)

            def gnorm_finish(gb):
                rs_ = {}
                for h, (pn, pnB) in gb.items():
                    a_, aB = ntf()
                    release(pn)
                    act(a_[:, :T], pn[:, :T], AF.Ln, [pnB], [aB], scale=1.0 / 128, bias=EPS)
                    rs_[h] = (a_, aB)
                for h, (a_, aB) in rs_.items():
                    act(a_[:, :T], a_[:, :T], AF.Exp, [aB], [aB], scale=-0.5)
                for h, (a_, aB) in rs_.items():
                    tt(a_[:, :T], a_[:, :T], oT[:, h, :T], ALU.mult, [aB, *BAt], [aB])
                    stt(qsT[:, h, :T], a_[:, :T], pvgs[:, l * 8 + h:l * 8 + h + 1], szb[:, h, :T], ALU.mult, ALU.mult,
                        [aB, Bc, Bsz], [Bqs])

            P.tag = "S3a"
            hook()
            pq = []
            groups = [dict(), dict()]
            for half in range(2):
                w_, wB = wload(l, C_U + half * 512)
                for j in range(4):
                    fc = half * 4 + j
                    pu, puB = nb()
                    for kc in range(8):
                        mm(pu[:, :T], w_[:, kc, j * 128:(j + 1) * 128], hT[:, kc, :T], kc == 0, kc == 7,
                           [*BhTc, wB], [puB], inc=(kc == 7))
                    if len(pq) == 3:
                        pd = pq.pop(0)
                        gnorm_mm(pd, groups[pd[0] // 4])
                        if pd[0] == 3:
                            gnorm_finish(groups[0])
                    act(guT[:, fc, :T], pu[:, :T], AF.Gelu_apprx_tanh, [puB], [Bguc[fc]])
                    pq.append(gnorm_sq(fc))
            gn_tail = pq
            P.tag = "S3b"
            hook()
            for half in range(2):
                w_, wB = wload(l, C_ZA + half * 512)
                for j in range(4):
                    fc = half * 4 + j; g = fc // 2
                    pz, pzB = nb()
                    for kc in range(8):
                        mm(pz[:, :T], w_[:, kc, j * 128:(j + 1) * 128], hT[:, kc, :T], kc == 0, kc == 7,
                           [*BhTc, wB], [pzB], inc=(kc == 7))
                    if gn_tail:
                        gnorm_mm(gn_tail.pop(0), groups[1])
                    z_, zB = ntf()
                    act(z_[:, :T], pz[:, :T], AF.Silu, [pzB], [zB])
                    ps_, psB = nb()
                    for tb in range(NB):
                        mm(ps_[:, tb * 128:(tb + 1) * 128], vtok[:, tb, fc * 128:(fc + 1) * 128], WTs[:, var, l, g, :],
                           True, True, [Bvt, BWT], [psB], inc=(tb == NB - 1))
                    tt(z_[:, :T], z_[:, :T], guT[:, fc, :T], ALU.mult, [zB, Bguc[fc]], [zB])
                    s_, sB = ntf()
                    tt(s_[:, :T].rearrange("p (n t) -> p n t", t=128), ps_[:, :T].rearrange("p (n t) -> p n t", t=128),
                       bsbc[:, g * 128:(g + 1) * 128].unsqueeze(1).to_broadcast([128, NB, 128]), ALU.add, [psB, Bbs], [sB])
                    tt(guT[:, fc, :T], s_[:, :T], z_[:, :T], ALU.mult, [sB, zB], [Bguc[fc]])
            P.tag = "S5"
            gnorm_finish(groups[1])
            if after_hgrn is not None:
                after_hgrn()

            def merge_gates(dc):
                w_, wB = wload(l, C_MRG + dc * 512)
                gates = []
                for j in (2, 3):
                    pp, ppB = nb()
                    for kc in range(8):
                        mm(pp[:, :T], w_[:, kc, j * 128:(j + 1) * 128], hT[:, kc, :T], kc == 0, kc == 7,
                           [*BhTc, wB], [ppB], inc=(kc == 7))
                    g_, gB_ = ntf(hold=True)
                    act(g_[:, :T], pp[:, :T], AF.Sigmoid, [ppB], [gB_])
                    gates.append((g_, gB_))
                return w_, wB, gates

            def merge_rest(dc, pre):
                w_, wB, gates = pre
                for j, (src, sB) in ((0, (guT, Bguc)), (1, (qsT, [Bqs]))):
                    pp, ppB = nb()
                    for kc in range(8):
                        mm(pp[:, :T], w_[:, kc, j * 128:(j + 1) * 128], src[:, kc, :T], kc == 0, kc == 7,
                           [sB[kc] if len(sB) == 8 else sB[0], wB], [ppB], inc=(kc == 7))
                    g_, gB_ = gates[j]
                    tt(g_[:, :T], pp[:, :T], g_[:, :T], ALU.mult, [ppB, gB_], [gB_])
                (ga, gaB), (gb, gbB) = gates
                tt(qtl[:, dc, :T], ga[:, :T], gb[:, :T], ALU.add, [gaB, gbB], [Bqtc[dc]])
                release_tf(ga); release_tf(gb)

            pre = merge_gates(0)
            for dc in range(8):
                nxt = merge_gates(dc + 1) if dc + 1 < 8 else None
                merge_rest(dc, pre)
                pre = nxt
            P.tag = "S6"
            pnx, pnxB = stat_begin()
            pend = None
            for half in range(2):
                w_, wB = wload(l, C_O + half * 512)
                for j in range(4):
                    dcc = half * 4 + j
                    pp, ppB = nb()
                    for kc in range(8):
                        mm(pp[:, :T], w_[:, kc, j * 128:(j + 1) * 128], qtl[:, kc, :T], kc == 0, kc == 7,
                           [Bqtc[kc], wB], [ppB], inc=(kc == 7))
                    if pend is not None:
                        stat_mm(pend, T, pnx, pnxB)
                    for (c0, c1, sq) in segs:
                        stt(xT[:, dcc, c0:c1], pp[:, c0:c1], mod[:, l, 16 + dcc, sq:sq + 1], xT[:, dcc, c0:c1],
                            ALU.mult, ALU.add, [ppB, Bmodl[l], BxTc[dcc]], [BxTc[dcc]])
                    pend = stat_sq(dcc, T)
            stat_mm(pend, T, pnx, pnxB)
            return stat_finish(T, pnx, pnxB)


        def load_x(tile):
            T = tile["T"]; NB = T // 128; t0 = tile["tok0"]
            P.dma("pool", bufA[:, :NB, :], xin[t0:t0 + T].rearrange("(tb p) d -> p tb d", p=128), writes=[*BAt])

        load_x(tiles[0])
        for ti, tile in enumerate(tiles):
            st["tile"] = ti
            T = tile["T"]; NB = T // 128; t0 = tile["tok0"]
            xtok = bufA
            P.tag = "Xin"
            pn0, pn0B = stat_begin()
            pend = None
            for c in range(8):
                pt, ptB = nb()
                for tb in range(NB):
                    P.op("pe", lambda e, c=c, tb=tb, pt=pt: e.transpose(pt[:, tb * 128:(tb + 1) * 128], xtok[:, tb, c * 128:(c + 1) * 128], identf[:]),
                         [*BAt, Bc], [ptB])
                if pend is not None:
                    stat_mm(pend, T, pn0, pn0B)
                evac(xT[:, c, :T], pt[:, :T], [ptB], [BxTc[c]])
                pend = stat_sq(c, T)
            stat_mm(pend, T, pn0, pn0B)
            rs = stat_finish(T, pn0, pn0B)
            rs = layer(0, tile, rs)
            nxt = (lambda t=tiles[ti + 1]: load_x(t)) if ti + 1 < len(tiles) else None
            rs = layer(1, tile, rs, after_hgrn=nxt)
            P.tag = "Fin"
            r_, rB = rs
            for c in range(8):
                stt(xT[:, c, :T], xT[:, c, :T], pvgs[:, 16 + c:17 + c], r_[:, :T], ALU.mult, ALU.mult, [BxTc[c], Bc, rB], [BxTc[c]])
            release_tf(r_)
            for tb in range(NB):
                sg_, sgB = nft()
                for half in range(2):
                    pt, ptB = nb()
                    for j in range(4):
                        c = half * 4 + j
                        P.op("pe", lambda e, c=c, j=j, tb=tb, pt=pt: e.transpose(pt[:, j * 128:(j + 1) * 128], xT[:, c, tb * 128:(tb + 1) * 128], identf[:]),
                             [BxTc[c], Bc], [ptB])
                    evac(sg_[:, half * 512:(half + 1) * 512], pt[:, :], [ptB], [sgB[half]])
                P.dma("pool", y[t0 + tb * 128:t0 + (tb + 1) * 128, :], sg_[:], reads=[sgB], sembuf=sgB[0])
        final = []
        for b in (Bvo[0], Bvo[1], BSst[0], BSst[1], Bfh[0], Bfh[2]):
            if b.dsem is not None:
                for ent in b.dsem.values():
                    final.append((ent[0], ent[1]))
        P.wait_tokens("pool", final)
        P.emit(block)
        build_nc.last_prog = P
    return nc


def host_consts():
    identb = np.eye(128, dtype=np.float32).astype(ml_dtypes.bfloat16)
    identf = np.eye(128, dtype=np.float32)
    sp_, t_ = np.meshgrid(np.arange(128), np.arange(128), indexing="ij")
    same = (sp_ // 64) == (t_ // 64)
    sl, tl = sp_ % 64, t_ % 64
    U = (same & (sl <= tl)).astype(np.float32)
    mid = 31
    Mrel = np.where(same & (sl > mid) & (sl <= tl), 1.0, 0.0) - np.where(same & (sl <= mid) & (sl > tl), 1.0, 0.0)
    mru = np.concatenate([Mrel.astype(np.float32), U], axis=1)
    p_, t8 = np.meshgrid(np.arange(128), np.arange(512), indexing="ij")
    mask8 = ((p_ % 64) <= (t8 % 64)).astype(np.float32)
    tril = (sp_ <= t_).astype(np.float32)
    esel = np.zeros((128, 8, 8), np.float32)
    for h in range(8):
        esel[:, h, h] = 1.0
    return dict(identb=identb, identf=identf, mru=np.ascontiguousarray(mru), mask8=mask8, tril=tril,
                esel=esel.reshape(128, 64))


def host_weights(w_ada, b_ada, norm_g, w_in, ln_v_g, ln_v_b, w_s, b_s, lb_raw, gnorm_g, w_pa, w_pb, w_o, final_g):
    f = lambda a: np.asarray(a, dtype=np.float32)
    w_ada, b_ada, norm_g, w_in = f(w_ada), f(b_ada), f(norm_g), f(w_in)
    wall = np.empty((2, D, NCOL), np.float32)
    wall[:, :, 0:7168] = w_in[:, :, 0:7168]
    wall[:, :, C_ADA:C_ADA + 3072] = w_ada
    gA = w_in[:, :, 7168:8192]; gB = w_in[:, :, 8192:9216]
    for dc in range(8):
        s = slice(dc * 128, (dc + 1) * 128)
        o = C_MRG + dc * 512
        wall[:, :, o:o + 128] = f(w_pa)[:, :, s]
        wall[:, :, o + 128:o + 256] = f(w_pb)[:, :, s]
        wall[:, :, o + 256:o + 384] = gA[:, :, s]
        wall[:, :, o + 384:o + 512] = gB[:, :, s]
    wall[:, :, C_O:C_O + 1024] = f(w_o)
    fm = lambda v, n: np.ascontiguousarray(v.reshape(n, 128).T)
    pvx = np.empty((128, 2, 32, 6), np.float32)
    for l in range(2):
        pvx[:, l, 0:24, :] = fm(b_ada[l], 24)[:, :, None]
        pvx[:, l, 24:32, :] = fm(norm_g[l], 8)[:, :, None]
    pvg = np.concatenate([fm(f(gnorm_g)[0], 8), fm(f(gnorm_g)[1], 8), fm(f(final_g), 8)], axis=1)
    bct = np.empty((2, 2, 128, D), np.float32)
    bct[:, 0] = f(ln_v_g)[:, None, :]
    bct[:, 1] = f(ln_v_b)[:, None, :]
    lbr = np.ascontiguousarray(np.broadcast_to(f(lb_raw)[:, None, :], (2, 128, D)))
    w_s, b_s = f(w_s), f(b_s)
    wst = np.zeros((2, 2, 128, 4, 128), np.float32)
    wst[0] = np.transpose(w_s, (0, 3, 1, 2))
    blk = np.transpose(w_s[:, :, 0:64, 0:64], (0, 3, 1, 2))
    wst[1, :, 0:64, :, 0:64] = blk
    wst[1, :, 64:128, :, 64:128] = blk
    bsr = np.empty((2, 2, 128, 512), np.float32)
    bsr[0] = b_s.reshape(2, 1, 512)
    bsr[1] = np.concatenate([b_s[:, :, 0:64], b_s[:, :, 0:64]], axis=2).reshape(2, 1, 512)
    return dict(wall=wall, pvx=pvx, pvg=np.ascontiguousarray(pvg), bct=bct, lbr=lbr, wst=wst, bsr=bsr)


def core_inputs(core, x_prompt, x_sample, state_hgrn, c_prompt, c_sample, shared, npl=2048):
    xp = np.asarray(x_prompt[2 * core:2 * core + 2, :npl], np.float32).reshape(-1, D)
    xs = np.asarray(x_sample[4 * core:4 * core + 4], np.float32).reshape(-1, D)
    c6 = np.concatenate([np.asarray(c_prompt[2 * core:2 * core + 2], np.float32),
                         np.asarray(c_sample[4 * core:4 * core + 4], np.float32)], axis=0)
    cT = np.ascontiguousarray(c6.reshape(6, 8, 128).transpose(2, 1, 0))
    s0 = np.ascontiguousarray(np.asarray(state_hgrn[:, 4 * core:4 * core + 4], np.float32))
    m = dict(shared)
    m.update(xin=np.ascontiguousarray(np.concatenate([xp, xs], axis=0)), cT=cT, s0=s0)
    return m


def kernel(x_prompt, x_sample, state_hgrn, c_prompt, c_sample, w_ada, b_ada, norm_g, w_in, ln_v_g, ln_v_b,
           w_s, b_s, lb_raw, gnorm_g, w_pa, w_pb, w_o, final_g):
    ncores = 8
    shared = host_consts()
    shared.update(host_weights(w_ada, b_ada, norm_g, w_in, ln_v_g, ln_v_b, w_s, b_s, lb_raw, gnorm_g,
                               w_pa, w_pb, w_o, final_g))
    in_maps = [core_inputs(c, x_prompt, x_sample, state_hgrn, c_prompt, c_sample, shared) for c in range(ncores)]
    nc = build_nc(default_tiles(), 4352)
    res = run_bass_kernel_spmd(nc, in_maps, core_ids=list(range(ncores)))
    R = res.results
    y_prompt = np.concatenate([r["y"][:4096].reshape(2, 2048, D) for r in R], axis=0)
    y_sample = np.concatenate([r["y"][4096:].reshape(4, 64, D) for r in R], axis=0)
    sp = np.concatenate([r["sp_out"] for r in R], axis=1)
    ss = np.concatenate([r["ss_out"] for r in R], axis=1)
    vs = np.concatenate([r["v_out"].reshape(2, 4, 64, D) for r in R], axis=1)
    return (y_prompt.astype(np.float32), y_sample.astype(np.float32), sp.astype(np.float32),
            ss.astype(np.float32), vs.astype(np.float32))
```

```python
import numpy as np
import ml_dtypes
from contextlib import ExitStack
import concourse.bass as bass
import concourse.mybir as mybir
from concourse.bass_utils import run_bass_kernel_spmd

F32 = mybir.dt.float32
BF16 = mybir.dt.bfloat16
AF = mybir.ActivationFunctionType
ALU = mybir.AluOpType
ENGS = ("pe", "act", "dve", "pool", "sp")
EPS = 1e-6
D = 1024
NCOL = 15360
C_U, C_V, C_ZA, C_Q, C_F, C_I, C_ZB = 0, 1024, 2048, 3072, 4096, 5120, 6144
C_ADA, C_MRG, C_O = 7168, 10240, 14336


class Buf:
    __slots__ = ("name", "w", "r", "dsem", "dcnt")

    def __init__(self, name):
        self.name = name
        self.w = None
        self.r = []
        self.dsem = None
        self.dcnt = 0


class Prog:
    def __init__(self, nc, sems):
        self.nc = nc
        self.free_sems = list(sems)
        self.esem = {e: self.free_sems.pop() for e in ("pe", "act", "dve", "pool")}
        self.cnt = {e: 0 for e in ("pe", "act", "dve", "pool")}
        self.known = {e: {} for e in ENGS}
        self.q = {e: [] for e in ENGS}
        self.tag = ""
        self.pe_log = []

    def _need(self, eng, toks):
        best = {}
        for t in toks:
            if t is None:
                continue
            sem, val = t
            if self.known[eng].get(id(sem), 0) >= val:
                continue
            if best.get(id(sem), (None, 0))[1] < val:
                best[id(sem)] = (sem, val)
        out = []
        for sem, val in best.values():
            self.known[eng][id(sem)] = val
            out.append((sem, val))
        return out

    @staticmethod
    def _deps(reads, writes):
        toks = []
        for b in reads:
            toks.append(b.w)
        for b in writes:
            toks.append(b.w)
            toks.extend(b.r)
        return toks

    @staticmethod
    def _commit(tok, reads, writes):
        for b in reads:
            if b not in writes:
                b.r.append(tok)
                if len(b.r) > 64:
                    best = {}
                    for s, v in b.r:
                        if best.get(id(s), (None, 0))[1] < v:
                            best[id(s)] = (s, v)
                    b.r = list(best.values())
        for b in writes:
            b.w = tok
            b.r = []

    @staticmethod
    def _flat(bufs):
        out = []
        for b in bufs:
            if isinstance(b, (list, tuple)):
                out.extend(Prog._flat(b))
            elif b not in out:
                out.append(b)
        return out

    def op(self, eng, fn, reads=(), writes=(), inc=True):
        reads, writes = self._flat(reads), self._flat(writes)
        deps = self._deps(reads, writes)
        if eng == "pe":
            deps = [t for t in deps if t is not None and t[0] is not self.esem["pe"]]
        waits = self._need(eng, deps)
        if inc:
            self.cnt[eng] += 1
            tok = (self.esem[eng], self.cnt[eng])
        else:
            tok = (self.esem[eng], self.cnt[eng] + 1)
        self._commit(tok, reads, writes)
        self.q[eng].append((fn, waits, self.esem[eng] if inc else None))
        if eng == "pe":
            self.pe_log.append((self.tag, [(getattr(w[0], "name", str(w[0])), w[1]) for w in waits]))
        return tok

    def dma(self, qeng, out_ap, in_ap, reads=(), writes=(), sembuf=None):
        reads, writes = self._flat(reads), self._flat(writes)
        sb = sembuf if sembuf is not None else (writes[0] if writes else reads[0])
        if sb.dsem is None:
            sb.dsem = {}
        if qeng not in sb.dsem:
            sb.dsem[qeng] = [self.free_sems.pop(), 0]
        ent = sb.dsem[qeng]
        waits = self._need(qeng, self._deps(reads, writes))
        ent[1] += 16
        tok = (ent[0], ent[1])
        self._commit(tok, reads, writes)

        def fn(e, out_ap=out_ap, in_ap=in_ap):
            return e.dma_start(out=out_ap, in_=in_ap)
        self.q[qeng].append((fn, waits, (ent[0], 16)))
        return tok

    def wait_tokens(self, eng, toks):
        waits = self._need(eng, toks)
        if waits:
            self.q[eng].append((None, waits, None))

    def emit(self, block):
        hmap = {"pe": "tensor", "act": "scalar", "dve": "vector", "pool": "gpsimd", "sp": "sync"}

        def make(eng):
            def body(e):
                for fn, waits, inc in self.q[eng]:
                    if fn is None:
                        for sem, val in waits:
                            e.wait_ge(sem, val)
                        continue
                    if eng == "pe":
                        for sem, val in waits:
                            e.wait_ge(sem, val)
                        ins = fn(e)
                    else:
                        for sem, val in waits[1:]:
                            e.wait_ge(sem, val)
                        ins = fn(e)
                        if waits:
                            ins._wait_ge(waits[0][0], waits[0][1])
                    if inc is not None:
                        if isinstance(inc, tuple):
                            ins.then_inc(inc[0], inc[1])
                        else:
                            ins.then_inc(inc, 1)
            return body
        for eng in ENGS:
            getattr(block, hmap[eng])(make(eng))


def default_tiles():
    tiles = []
    for p in range(2):
        for k in range(4):
            tiles.append(dict(kind="p", tok0=p * 2048 + k * 512, T=512, seq=p, first=(k == 0), last=(k == 3)))
    tiles.append(dict(kind="s", tok0=4096, T=256, seq=None, first=True, last=True))
    return tiles


def build_nc(tiles, ntok, nprompt=2):
    nc = bass.Bass("TRN2", target_bir_lowering=False)
    dram_in = lambda name, shape, dt=F32: nc.dram_tensor(name, list(shape), dt, kind="ExternalInput").ap()
    dram_out = lambda name, shape, dt=F32: nc.dram_tensor(name, list(shape), dt, kind="ExternalOutput").ap()
    xin = dram_in("xin", [ntok, D])
    cT = dram_in("cT", [128, 8, 6])
    s0 = dram_in("s0", [2, 4, 8, 128, 128])
    wall = dram_in("wall", [2, D, NCOL])
    pvx = dram_in("pvx", [128, 2, 32, 6])
    pvg = dram_in("pvg", [128, 24])
    bct = dram_in("bct", [2, 2, 128, D])
    lbr = dram_in("lbr", [2, 128, D])
    wst = dram_in("wst", [2, 2, 128, 4, 128])
    bsr = dram_in("bsr", [2, 2, 128, 512])
    identb_d = dram_in("identb", [128, 128], BF16)
    identf_d = dram_in("identf", [128, 128])
    mru_d = dram_in("mru", [128, 256])
    mask8_d = dram_in("mask8", [128, 512])
    tril_d = dram_in("tril", [128, 128])
    esel_d = dram_in("esel", [128, 64])
    y = dram_out("y", [ntok, D])
    sp_out = dram_out("sp_out", [2, nprompt, 8, 128, 128])
    ss_out = dram_out("ss_out", [2, 4, 8, 128, 128])
    v_out = dram_out("v_out", [2, 256, D])
    wsc = nc.dram_tensor("wsc", [2, NCOL // 512, 128, 8 * 512], BF16).ap()

    with ExitStack() as es:
        def sb(name, shape, dt=F32):
            return es.enter_context(nc.sbuf_tensor("sb_" + name, list(shape), dt))
        xT = sb("xT", [128, 8, 512]); BxTc = [Buf(f"xT{c}") for c in range(8)]
        hT = sb("hT", [128, 8, 512], BF16); BhTc = [Buf(f"hT{c}") for c in range(8)]
        bufA = sb("bufA", [128, 4, 1024]); BAt = [Buf(f"bufA{t}") for t in range(4)]
        guT = sb("guT", [128, 8, 512], BF16); Bguc = [Buf(f"gu{c}") for c in range(8)]
        vtok = sb("vtok", [128, 4, 1024], BF16); Bvt = Buf("vtok")
        itok = sb("itok", [128, 4, 1024], BF16); Bit = Buf("itok")
        ktT = sb("ktT", [128, 4, 1024], BF16); BkT = Buf("ktT")
        qsT = sb("qsT", [128, 8, 512], BF16); Bqs = Buf("qs")
        qtl = sb("qtl", [128, 8, 512], BF16); Bqtc = [Buf(f"qtl{c}") for c in range(8)]
        qdc = sb("qdc", [128, 8, 512], BF16); Bqdc = [Buf(f"qdc{c}") for c in range(8)]
        szb = sb("szb", [128, 8, 512], BF16); Bsz = Buf("szb")
        NTF = 8
        tf = [sb(f"tf{i}", [128, 512]) for i in range(NTF)]; Btf = [Buf(f"tf{i}") for i in range(NTF)]
        ft = [sb(f"ft{i}", [128, 1024]) for i in range(2)]
        Bfh = [Buf(f"fh{i}") for i in range(4)]
        Bft = [[Bfh[0], Bfh[1]], [Bfh[2], Bfh[3]]]
        NW = 3
        wb = [sb(f"wb{i}", [128, 8, 512], BF16) for i in range(NW)]; Bwb = [Buf(f"wb{i}") for i in range(NW)]
        S = sb("S", [128, 2, 8, 128]); BS = [[Buf(f"S{l}_{h}") for h in range(8)] for l in range(2)]
        Sb = sb("Sb", [128, 2, 8, 128], BF16); BSb = [[Buf(f"Sb{l}_{h}") for h in range(8)] for l in range(2)]
        BSst = [Buf("Sst0"), Buf("Sst1")]
        Bvo = [Buf("vo0"), Buf("vo1")]
        bcg = sb("bcg", [128, D]); bcb = sb("bcb", [128, D]); Bbc = Buf("bc")
        lbb = sb("lbb", [128, D]); oml = sb("oml", [128, D]); Blb = Buf("lb")
        WTs = sb("WTs", [128, 2, 2, 4, 128], BF16); BWT = Buf("WTs")
        bsbc = sb("bsbc", [128, 512]); Bbs = Buf("bsbc")
        mask8 = sb("mask8", [128, 512]); mru = sb("mru", [128, 256]); tril = sb("tril", [128, 128])
        identb = sb("identb", [128, 128], BF16); identf = sb("identf", [128, 128]); onesf = sb("onesf", [128, 128])
        esel = sb("esel", [128, 64])
        Bc = Buf("consts")
        mod = sb("mod", [128, 2, 24, 6]); Bmodl = [Buf("mod0"), Buf("mod1")]
        pvxs = sb("pvxs", [128, 2, 32, 6]); pvgs = sb("pvgs", [128, 24])
        cTs = sb("cTs", [128, 8, 6]); scb = sb("scb", [128, 8, 6], BF16); Bsc = Buf("scb")
        ATs = sb("ATs", [128, 4, 512], BF16); BAT = [Buf("ATs0"), Buf("ATs1")]
        U = sb("U", [128, 2, 8, 128]); BU = [Buf("U0"), Buf("U1")]
        ec = sb("ec", [128, 2, 8, 8]); Bec = Buf("ec")
        st6 = sb("st6", [128, 4, 2, 2, 3]); mv = sb("mv", [128, 4, 2]); rsv = sb("rsv", [128, 4]); Bst = Buf("st")
        pbank = [es.enter_context(nc.psum_tensor(f"pb{i}", [128, 512], F32)) for i in range(8)]
        Bpb = [Buf(f"pb{i}") for i in range(8)]
        sems = [es.enter_context(nc.semaphore(f"s{i}")) for i in range(60)]
        block = es.enter_context(nc.Block())
        P = Prog(nc, sems)
        st = dict(pb=0, tf=0, ft=0, wb=0, ev=0, tile=0)

        held = set()
        bank_stamp = [0] * 8

        def nb(hold=False):
            cands = [i for i in range(8) if i not in held]
            i = min(cands, key=lambda j: bank_stamp[j])
            st["pb"] += 1
            bank_stamp[i] = st["pb"]
            if hold:
                held.add(i)
            return pbank[i], Bpb[i]

        def release(bank):
            i = pbank.index(bank)
            held.discard(i)
            st["pb"] += 1
            bank_stamp[i] = st["pb"]

        held_tf = set()

        def ntf(hold=False):
            i = st["tf"]
            while i in held_tf:
                i = (i + 1) % NTF
            st["tf"] = (i + 1) % NTF
            if hold:
                held_tf.add(i)
            return tf[i], Btf[i]

        def release_tf(buf):
            held_tf.discard(tf.index(buf))

        def nft():
            i = st["ft"]; st["ft"] = (i + 1) % 2
            return ft[i], Bft[i]

        Bscr = {}

        def wload(l, col0, ncols=512, keep=True):
            i = st["wb"]; st["wb"] = (i + 1) % NW
            key = (l, col0)
            if key in Bscr:
                assert ncols == 512 and col0 % 512 == 0
                P.dma("sp", wb[i][:, :, :].rearrange("p a b -> p (a b)"), wsc[l, col0 // 512], reads=[Bscr[key]], writes=[Bwb[i]])
            else:
                src = wall[l].rearrange("(kc p) n -> p kc n", p=128)[:, :, col0:col0 + ncols]
                P.dma("pool", wb[i][:, :, :ncols], src, writes=[Bwb[i]])
                if keep and (st["tile"] > 0 or (col0 // 512) % 2 == 0):
                    Bscr[key] = Buf(f"scr{l}_{col0}")
                    assert ncols == 512 and col0 % 512 == 0
                    P.dma("sp", wsc[l, col0 // 512], wb[i][:, :, :].rearrange("p a b -> p (a b)"), reads=[Bwb[i]], writes=[Bscr[key]], sembuf=Bwb[i])
            return wb[i], Bwb[i]

        def mm(out, lhsT, rhs, start, stop, reads, writes, inc=True, tp=None):
            if tp is None:
                P.op("pe", lambda e: e.matmul(out, lhsT=lhsT, rhs=rhs, start=start, stop=stop), reads, writes, inc)
            else:
                P.op("pe", lambda e: e.matmul(out, lhsT=lhsT, rhs=rhs, start=start, stop=stop, tile_position=tp), reads, writes, inc)

        def act(out, in_, func, reads, writes, scale=1.0, bias=0.0):
            P.op("act", lambda e: e.activation(out=out, in_=in_, func=func, bias=bias, scale=scale), reads, writes)

        def evac(out, in_, reads, writes):
            st["ev"] ^= 1
            if st["ev"]:
                P.op("act", lambda e: e.activation(out=out, in_=in_, func=AF.Copy), reads, writes)
            else:
                P.op("dve", lambda e: e.tensor_copy(out=out, in_=in_), reads, writes)

        def tt(out, in0, in1, op, reads, writes, eng="dve"):
            P.op(eng, lambda e: e.tensor_tensor(out=out, in0=in0, in1=in1, op=op), reads, writes)

        def ts(out, in0, s1, s2, op0, op1, reads, writes, eng="dve"):
            if s2 is None:
                P.op(eng, lambda e: e.tensor_scalar(out=out, in0=in0, scalar1=s1, scalar2=None, op0=op0), reads, writes)
            else:
                P.op(eng, lambda e: e.tensor_scalar(out=out, in0=in0, scalar1=s1, scalar2=s2, op0=op0, op1=op1), reads, writes)

        def stt(out, in0, scalar, in1, op0, op1, reads, writes, eng="dve"):
            P.op(eng, lambda e: e.scalar_tensor_tensor(out=out, in0=in0, scalar=scalar, in1=in1, op0=op0, op1=op1), reads, writes)

        for dst, src in ((identb, identb_d), (identf, identf_d), (mru, mru_d), (mask8, mask8_d), (tril, tril_d), (esel, esel_d)):
            P.dma("sp", dst[:], src[:, :], writes=[Bc])
        P.dma("sp", pvxs[:], pvx[:, :, :, :], writes=[Bc])
        P.dma("sp", pvgs[:], pvg[:, :], writes=[Bc])
        P.dma("sp", cTs[:], cT[:, :, :], writes=[Bc])
        P.op("dve", lambda e: e.memset(onesf[:], 1.0), writes=[Bc])
        P.op("dve", lambda e: e.memset(ATs[:], 0.0), writes=[BAT[0], BAT[1]])
        mask83 = mask8[:, :].rearrange("p (h t) -> p h t", h=8)
        for var in range(2):
            for l in range(2):
                t_, tB = nft()
                P.dma("sp", t_[:, 0:512].rearrange("p (g t) -> p g t", g=4), wst[var, l], writes=[tB])
                for g in range(4):
                    tt(WTs[:, var, l, g, :], t_[:, g * 128:(g + 1) * 128], tril[:], ALU.mult, [tB, Bc], [BWT])
        t0_, t0B = nft(); t1_, t1B = nft()
        P.dma("sp", t0_[:], lbr[0], writes=[t0B])
        P.dma("sp", t1_[:], lbr[1], writes=[t1B])
        tt(t1_[:], t1_[:], t0_[:], ALU.subtract, [t0B, t1B], [t1B])
        act(lbb[:], t1_[:], AF.Sigmoid, [t1B], [Blb])
        ts(oml[:], lbb[:], -1.0, 1.0, ALU.mult, ALU.add, [Blb], [Blb])
        act(scb[:], cTs[:], AF.Silu, [Bc], [Bsc])
        def mod_block(l, blk, pm, pmB):
            w_, wB = wload(l, C_ADA + blk * 512, keep=False)
            for j in range(4):
                jc = blk * 4 + j
                for kc in range(8):
                    mm(pm[:, jc * 6:jc * 6 + 6], w_[:, kc, j * 128:(j + 1) * 128], scb[:, kc, :],
                       kc == 0, kc == 7, [wB, Bsc], [pmB], inc=(kc == 7))

        def mod_finish(l, pm, pmB):
            release(pm)
            tt(mod[:, l].rearrange("p a b -> p (a b)"), pm[:, 0:144],
               pvxs[:, l, 0:24, :].rearrange("p a b -> p (a b)"), ALU.add, [pmB, Bc], [Bmodl[l]])
            stt(mod[:, l, 8:16, :], mod[:, l, 8:16, :], 1.0, pvxs[:, l, 24:32, :], ALU.add, ALU.mult, [Bmodl[l], Bc], [Bmodl[l]])

        pm0 = nb(hold=True)
        for blk in range(6):
            mod_block(0, blk, *pm0)
        mod_finish(0, *pm0)
        deferred = []
        pm1 = nb(hold=True)
        for blk in range(6):
            deferred.append(lambda blk=blk: mod_block(1, blk, *pm1))
        deferred.append(lambda: mod_finish(1, *pm1))

        def hook():
            if deferred:
                deferred.pop(0)()

        def stat_begin():
            return nb(hold=True)

        def stat_sq(c, T):
            q_, qB = ntf()
            act(q_[:, :T], xT[:, c, :T], AF.Square, [BxTc[c]], [qB])
            return (c, q_, qB)

        def stat_mm(pend, T, pn, pnB):
            c, q_, qB = pend
            mm(pn[:, :T], onesf[:], q_[:, :T], c == 0, c == 7, [qB, Bc], [pnB])

        def stat_finish(T, pn, pnB):
            release(pn)
            a_, aB = ntf()
            act(a_[:, :T], pn[:, :T], AF.Ln, [pnB], [aB], scale=1.0 / D, bias=EPS)
            r_, rB = ntf(hold=True)
            act(r_[:, :T], a_[:, :T], AF.Exp, [aB], [rB], scale=-0.5)
            return r_, rB

        def layer(l, tile, rstat, after_hgrn=None):
            T = tile["T"]; NB = T // 128; NCH = T // 64
            if l == 1:
                while deferred:
                    hook()
            var = 0 if tile["kind"] == "p" else 1
            if tile["kind"] == "p":
                segs = [(0, T, tile["seq"])]
            else:
                segs = [(i * 64, (i + 1) * 64, 2 + i) for i in range(4)]
            P.dma("pool", bcg[:], bct[l, 0], writes=[Bbc])
            P.dma("pool", bcb[:], bct[l, 1], writes=[Bbc])
            P.dma("pool", bsbc[:], bsr[var, l], writes=[Bbs])

            def s0_load(c):
                si = c % 2
                P.dma("pool", S[:, si], s0[l, c].rearrange("h k v -> k h v"), writes=[*BS[si]])

            if tile["kind"] == "s":
                s0_load(0)
                s0_load(1)
            P.tag = "S1"
            r_, rB = rstat
            for c in range(8):
                n_, nB = ntf()
                tt(n_[:, :T], xT[:, c, :T], r_[:, :T], ALU.mult, [BxTc[c], rB], [nB])
                for (c0, c1, sq) in segs:
                    act(hT[:, c, c0:c1], n_[:, c0:c1], AF.Identity, [nB, Bmodl[l]], [BhTc[c]],
                        scale=mod[:, l, 8 + c, sq:sq + 1], bias=mod[:, l, c, sq:sq + 1])
            release_tf(r_)
            P.tag = "S2v"
            hook()
            gvt = bufA
            for half in range(2):
                w_, wB = wload(l, C_V + half * 512)
                if half == 0:
                    bks = [nb() for tb in range(NB)]
                    for kc in range(8):
                        for tb in range(NB):
                            mm(bks[tb][0][:, :], hT[:, kc, tb * 128:(tb + 1) * 128], w_[:, kc, :], kc == 0, kc == 7,
                               [BhTc[kc], wB], [bks[tb][1]], inc=(kc == 7))
                    for tb in range(NB):
                        act(gvt[:, tb, 0:512], bks[tb][0][:, :], AF.Gelu_apprx_tanh, [bks[tb][1]], [BAt[tb]])
                    continue
                for tb in range(NB):
                    pv, pvB = nb()
                    for kc in range(8):
                        mm(pv[:, :], hT[:, kc, tb * 128:(tb + 1) * 128], w_[:, kc, :], kc == 0, kc == 7,
                           [*BhTc, wB], [pvB], inc=(kc == 7))
                    act(gvt[:, tb, half * 512:(half + 1) * 512], pv[:, :], AF.Gelu_apprx_tanh, [pvB], [BAt[tb]])
            for tb in range(NB):
                for half in range(2):
                    P.op("dve", lambda e, tb=tb, half=half: e.bn_stats(st6[:, tb, half], gvt[:, tb, half * 512:(half + 1) * 512]),
                         [BAt[tb]], [Bst])
                P.op("dve", lambda e, tb=tb: e.bn_aggr(mv[:, tb, :], st6[:, tb].rearrange("p k o j -> p (k o) j")), [Bst], [Bst])
            act(rsv[:, :NB], mv[:, :NB, 1], AF.Ln, [Bst], [Bst], scale=1.0, bias=EPS)
            act(rsv[:, :NB], rsv[:, :NB], AF.Exp, [Bst], [Bst], scale=-0.5)
            for tb in range(NB):
                stt(gvt[:, tb, :], gvt[:, tb, :], mv[:, tb, 0:1], bcg[:], ALU.subtract, ALU.mult, [BAt[tb], Bst, Bbc], [BAt[tb]])
                if var == 0:
                    stt(vtok[:, tb, :], gvt[:, tb, :], rsv[:, tb:tb + 1], bcb[:], ALU.mult, ALU.add, [BAt[tb], Bst, Bbc], [Bvt])
                else:
                    stt(gvt[:, tb, :], gvt[:, tb, :], rsv[:, tb:tb + 1], bcb[:], ALU.mult, ALU.add, [BAt[tb], Bst, Bbc], [BAt[tb]])
                    P.dma("pool", v_out[l, tb * 128:(tb + 1) * 128, :], gvt[:, tb, :], reads=[BAt[tb]], sembuf=Bvo[tb])
                    P.op("dve", lambda e, tb=tb: e.tensor_copy(out=vtok[:, tb, :], in_=gvt[:, tb, :]), [BAt[tb]], [Bvt])
            P.tag = "Hq"
            hook()

            def silu_proj(col, dst, dB, half, later=None):
                w_, wB = wload(l, col + half * 512)
                for j in range(4):
                    h = half * 4 + j
                    pq, pqB = nb(hold=(later is not None))
                    for kc in range(8):
                        mm(pq[:, :T], w_[:, kc, j * 128:(j + 1) * 128], hT[:, kc, :T], kc == 0, kc == 7,
                           [*BhTc, wB], [pqB], inc=(kc == 7))

                    def ev(h=h, pq=pq, pqB=pqB):
                        release(pq)
                        act(dst[:, h, :T], pq[:, :T], AF.Silu, [pqB], [dB])
                    if later is None:
                        ev()
                    else:
                        later.append(ev)

            P.tag = "Hf"
            hook()
            logf = bufA

            def i_proj(half):
                w_, wB = wload(l, C_I + half * 512)
                for tb in range(NB):
                    pi_, piB = nb()
                    for kc in range(8):
                        mm(pi_[:, :], hT[:, kc, tb * 128:(tb + 1) * 128], w_[:, kc, :], kc == 0, kc == 7,
                           [*BhTc, wB], [piB], inc=(kc == 7))
                    evac(itok[:, tb, half * 512:(half + 1) * 512], pi_[:, :], [piB], [Bit])

            fh = [(ft[i // 2][:, (i % 2) * 512:(i % 2 + 1) * 512], Bfh[i]) for i in range(4)]

            def f_proj(half):
                w_, wB = wload(l, C_F + half * 512)
                for tb in range(NB):
                    f_, fB = fh[tb]
                    pf, pfB = nb()
                    for kc in range(8):
                        mm(pf[:, :], hT[:, kc, tb * 128:(tb + 1) * 128], w_[:, kc, :], kc == 0, kc == 7,
                           [*BhTc, wB], [pfB], inc=(kc == 7))
                    act(f_, pf[:, :], AF.Sigmoid, [pfB], [fB])

            def f_post_a(half):
                hs = slice(half * 512, (half + 1) * 512)
                for tb in range(NB):
                    f_, fB = fh[tb]
                    if l == 1:
                        tt(f_, f_, oml[:, hs], ALU.mult, [fB, Blb], [fB])
                        tt(f_, f_, lbb[:, hs], ALU.add, [fB, Blb], [fB])
                    act(logf[:, tb, hs], f_, AF.Ln, [fB], [BAt[tb]])
                    ts(f_, f_, -1.0, 1.0, ALU.mult, ALU.add, [fB], [fB])

            def f_post_b(half):
                hs = slice(half * 512, (half + 1) * 512)
                for tb in range(NB):
                    f_, fB = fh[tb]
                    pr, prB = nb()
                    mm(pr[:, :], mru[:, 0:128], logf[:, tb, hs], True, True, [Bc, BAt[tb]], [prB])
                    e_, eB = ntf()
                    act(e_[:, :], pr[:, :], AF.Exp, [prB], [eB], scale=-1.0)
                    tt(ktT[:, tb, hs], f_, e_[:, :], ALU.mult, [fB, eB], [BkT])

            for half in range(2):
                f_proj(half)
                i_proj(half)
                f_post_a(half)
                later = []
                silu_proj(C_ZB, szb, Bsz, half, later=later)
                f_post_b(half)
                for ev in later:
                    ev()
            for half in range(2):
                silu_proj(C_Q, qsT, Bqs, half)
            P.tag = "HRB"
            hook()
            for h in range(8):
                erb_t, erbB = ft[h % 2], Bft[h % 2]
                erb = erb_t[:, 0:NB * 256].rearrange("p (n x) -> p n x", x=256)
                pRC = [nb() for _ in range((NB + 1) // 2)]
                for tb in range(NB):
                    pq_, pqB_ = pRC[tb // 2]
                    mm(pq_[:, (tb % 2) * 256:(tb % 2 + 1) * 256], logf[:, tb, h * 128:(h + 1) * 128], mru[:, 0:256], True, True,
                       [BAt[tb], Bc], [pqB_])
                for i2, (pq_, pqB_) in enumerate(pRC):
                    nb2 = min(2, NB - 2 * i2)
                    act(erb_t[:, i2 * 512:i2 * 512 + nb2 * 256], pq_[:, 0:nb2 * 256], AF.Exp, [pqB_], [erbB[i2] if NB > 2 else erbB])
                q3 = lambda t: t[:, h, :T].rearrange("p (n x) -> p n x", x=128)
                tt(q3(qtl), q3(qsT), erb[:, :, 0:128], ALU.mult, [Bqs, erbB], [Bqtc[h]])
                tt(q3(qdc), q3(qsT), erb[:, :, 128:256], ALU.mult, [Bqs, erbB], [Bqdc[h]])
                src = erb_t[:, 0:NB * 256].rearrange("p (n w c x) -> p w n c x", w=2, c=2, x=64)[:, :, :, :, 63]
                dst = ec[:, :, h, 0:NCH].rearrange("p w (n c) -> p w n c", c=2)
                P.op("dve", lambda e, src=src, dst=dst: e.tensor_copy(out=dst, in_=src), [erbB], [Bec])
            P.tag = "Hkt"
            hook()
            ktl = guT
            for h in range(8):
                pt, ptB = nb()
                ptb = pt[:, :].bitcast(BF16)
                for tb in range(NB):
                    P.op("pe", lambda e, h=h, tb=tb, ptb=ptb: e.transpose(ptb[:, tb * 128:(tb + 1) * 128], ktT[:, tb, h * 128:(h + 1) * 128], identb[:]),
                         [BkT, Bc], [ptB])
                P.op("dve", lambda e, h=h, ptb=ptb: e.tensor_copy(out=ktl[:, h, :T], in_=ptb[:, :T]), [ptB], [Bguc[h]])
            P.tag = "Hchunk"
            hook()
            oT = bufA[:, :, :].rearrange("p a b -> p (a b)").rearrange("p (h t) -> p h t", h=8)
            if tile["kind"] == "p" and tile["first"]:
                P.op("dve", lambda e: e.memset(S[:, l], 0.0), [], [*BS[l]])
                P.op("dve", lambda e: e.memset(Sb[:, l], 0.0), [], [*BSb[l]])

            def stage_a(c):
                tb = c // 2; r0 = (c % 2) * 64; rows = slice(r0, r0 + 64)
                cs = slice(c * 64, (c + 1) * 64); c0_ = c * 64
                tp = (0, 64) if r0 else None
                pa, paB = nb()
                for h in range(8):
                    mm(pa[rows, h * 64 + 32:(h + 1) * 64], ktl[:, h, cs], qtl[:, h, c0_ + 32:c0_ + 64], True, True,
                       [Bguc[h], Bqtc[h]], [paB], inc=False, tp=tp)
                    mm(pa[r0:r0 + 32, h * 64:h * 64 + 32], ktl[:, h, c0_:c0_ + 32], qtl[:, h, c0_:c0_ + 32], True, True,
                       [Bguc[h], Bqtc[h]], [paB], inc=(h == 7), tp=tp)
                pa3 = pa[:, :].rearrange("p (h t) -> p h t", h=8)
                at3 = ATs[:, tb, :].rearrange("p (h t) -> p h t", h=8)
                tt(at3[rows, :, 32:64], pa3[rows, :, 32:64], mask83[rows, :, 32:64], ALU.mult, [paB, Bc], [BAT[c % 2]])
                tt(at3[r0:r0 + 32, :, 0:32], pa3[r0:r0 + 32, :, 0:32], mask83[r0:r0 + 32, :, 0:32], ALU.mult, [paB, Bc], [BAT[c % 2]])
                pk = [nb(), nb()]
                for h in range(8):
                    pk_, pkB = pk[h // 4]
                    mm(pk_[:, (h % 4) * 128:(h % 4 + 1) * 128], ktT[rows, tb, h * 128:(h + 1) * 128],
                       itok[rows, tb, h * 128:(h + 1) * 128], True, True, [BkT, Bit], [pkB], inc=(h % 4 == 3))
                for h in range(8):
                    pk_, pkB = pk[h // 4]
                    act(U[:, c % 2, h, :], pk_[:, (h % 4) * 128:(h % 4 + 1) * 128], AF.Identity, [pkB, Bec], [BU[c % 2]],
                        scale=ec[:, 0, h, c:c + 1])

            def stage_b(c):
                tb = c // 2; r0 = (c % 2) * 64; rows = slice(r0, r0 + 64)
                cs = slice(c * 64, (c + 1) * 64)
                si = l if tile["kind"] == "p" else c % 2
                if tile["kind"] == "s":
                    P.op("act", lambda e: e.activation(out=Sb[:, si], in_=S[:, si], func=AF.Copy), [*BS[si]], [*BSb[si]])
                po, poB = nb()
                for h in range(8):
                    mm(po[:, h * 64:(h + 1) * 64], itok[rows, tb, h * 128:(h + 1) * 128], ATs[rows, tb, h * 64:(h + 1) * 64],
                       True, False, [Bit, BAT[c % 2]], [poB], inc=False)
                    mm(po[:, h * 64:(h + 1) * 64], Sb[:, si, h, :], qdc[:, h, cs], False, True, [BSb[si][h], Bqdc[h]], [poB], inc=True)
                evac(oT[:, :, cs], po[:, :].rearrange("p (h t) -> p h t", h=8), [poB], [*BAt])
                for h in range(8):
                    stt(S[:, si, h, :], S[:, si, h, :], ec[:, 1, h, c:c + 1], U[:, c % 2, h, :], ALU.mult, ALU.add,
                        [BS[si][h], Bec, BU[c % 2]], [BS[si][h]])
                    if tile["kind"] == "p":
                        P.op("act", lambda e, h=h: e.activation(out=Sb[:, si, h, :], in_=S[:, si, h, :], func=AF.Copy),
                             [BS[si][h]], [BSb[si][h]])
                if tile["kind"] == "s":
                    P.dma("pool", ss_out[l, c].rearrange("h k v -> k h v"), S[:, si], reads=[*BS[si]], sembuf=BSst[si])
                    if c + 2 < NCH:
                        s0_load(c + 2)

            stage_a(0)
            for c in range(NCH):
                if c + 1 < NCH:
                    stage_a(c + 1)
                stage_b(c)
            if tile["kind"] == "p" and tile["last"]:
                P.dma("pool", sp_out[l, tile["seq"]].rearrange("h k v -> k h v"), S[:, l], reads=[*BS[l]], sembuf=BSst[l])

            def gnorm_sq(h):
                q_, qB = ntf()
                tt(q_[:, :T], oT[:, h, :T], oT[:, h, :T], ALU.mult, [*BAt], [qB])
                return (h, q_, qB)

            def gnorm_mm(pend, gb):
                h, q_, qB = pend
                pn, pnB = nb(hold=True)
                mm(pn[:, :T], onesf[:], q_[:, :T], True, True, [qB, Bc], [pnB])
                gb[h] = (pn, pnB)

            def gnorm_finish(gb):
                rs_ = {}
                for h, (pn, pnB) in gb.items():
                    a_, aB = ntf()
                    release(pn)
                    act(a_[:, :T], pn[:, :T], AF.Ln, [pnB], [aB], scale=1.0 / 128, bias=EPS)
                    rs_[h] = (a_, aB)
                for h, (a_, aB) in rs_.items():
                    act(a_[:, :T], a_[:, :T], AF.Exp, [aB], [aB], scale=-0.5)
                for h, (a_, aB) in rs_.items():
                    tt(a_[:, :T], a_[:, :T], oT[:, h, :T], ALU.mult, [aB, *BAt], [aB])
                    stt(qsT[:, h, :T], a_[:, :T], pvgs[:, l * 8 + h:l * 8 + h + 1], szb[:, h, :T], ALU.mult, ALU.mult,
                        [aB, Bc, Bsz], [Bqs])

            P.tag = "S3a"
            hook()
            pq = []
            groups = [dict(), dict()]
            for half in range(2):
                w_, wB = wload(l, C_U + half * 512)
                for j in range(4):
                    fc = half * 4 + j
                    pu, puB = nb()
                    for kc in range(8):
                        mm(pu[:, :T], w_[:, kc, j * 128:(j + 1) * 128], hT[:, kc, :T], kc == 0, kc == 7,
                           [*BhTc, wB], [puB], inc=(kc == 7))
                    if len(pq) == 3:
                        pd = pq.pop(0)
                        gnorm_mm(pd, groups[pd[0] // 4])
                        if pd[0] == 3:
                            gnorm_finish(groups[0])
                    act(guT[:, fc, :T], pu[:, :T], AF.Gelu_apprx_tanh, [puB], [Bguc[fc]])
                    pq.append(gnorm_sq(fc))
            gn_tail = pq
            P.tag = "S3b"
            hook()
            for half in range(2):
                w_, wB = wload(l, C_ZA + half * 512)
                for j in range(4):
                    fc = half * 4 + j; g = fc // 2
                    pz, pzB = nb()
                    for kc in range(8):
                        mm(pz[:, :T], w_[:, kc, j * 128:(j + 1) * 128], hT[:, kc, :T], kc == 0, kc == 7,
                           [*BhTc, wB], [pzB], inc=(kc == 7))
                    if gn_tail:
                        gnorm_mm(gn_tail.pop(0), groups[1])
                    z_, zB = ntf()
                    act(z_[:, :T], pz[:, :T], AF.Silu, [pzB], [zB])
                    ps_, psB = nb()
                    for tb in range(NB):
                        mm(ps_[:, tb * 128:(tb + 1) * 128], vtok[:, tb, fc * 128:(fc + 1) * 128], WTs[:, var, l, g, :],
                           True, True, [Bvt, BWT], [psB], inc=(tb == NB - 1))
                    tt(z_[:, :T], z_[:, :T], guT[:, fc, :T], ALU.mult, [zB, Bguc[fc]], [zB])
                    s_, sB = ntf()
                    tt(s_[:, :T].rearrange("p (n t) -> p n t", t=128), ps_[:, :T].rearrange("p (n t) -> p n t", t=128),
                       bsbc[:, g * 128:(g + 1) * 128].unsqueeze(1).to_broadcast([128, NB, 128]), ALU.add, [psB, Bbs], [sB])
                    tt(guT[:, fc, :T], s_[:, :T], z_[:, :T], ALU.mult, [sB, zB], [Bguc[fc]])
            P.tag = "S5"
            gnorm_finish(groups[1])
            if after_hgrn is not None:
                after_hgrn()

            def merge_gates(dc):
                w_, wB = wload(l, C_MRG + dc * 512)
                gates = []
                for j in (2, 3):
                    pp, ppB = nb()
                    for kc in range(8):
                        mm(pp[:, :T], w_[:, kc, j * 128:(j + 1) * 128], hT[:, kc, :T], kc == 0, kc == 7,
                           [*BhTc, wB], [ppB], inc=(kc == 7))
                    g_, gB_ = ntf(hold=True)
                    act(g_[:, :T], pp[:, :T], AF.Sigmoid, [ppB], [gB_])
                    gates.append((g_, gB_))
                return w_, wB, gates

            def merge_rest(dc, pre):
                w_, wB, gates = pre
                for j, (src, sB) in ((0, (guT, Bguc)), (1, (qsT, [Bqs]))):
                    pp, ppB = nb()
                    for kc in range(8):
                        mm(pp[:, :T], w_[:, kc, j * 128:(j + 1) * 128], src[:, kc, :T], kc == 0, kc == 7,
                           [sB[kc] if len(sB) == 8 else sB[0], wB], [ppB], inc=(kc == 7))
                    g_, gB_ = gates[j]
                    tt(g_[:, :T], pp[:, :T], g_[:, :T], ALU.mult, [ppB, gB_], [gB_])
                (ga, gaB), (gb, gbB) = gates
                tt(qtl[:, dc, :T], ga[:, :T], gb[:, :T], ALU.add, [gaB, gbB], [Bqtc[dc]])
                release_tf(ga); release_tf(gb)

            pre = merge_gates(0)
            for dc in range(8):
                nxt = merge_gates(dc + 1) if dc + 1 < 8 else None
                merge_rest(dc, pre)
                pre = nxt
            P.tag = "S6"
            pnx, pnxB = stat_begin()
            pend = []
            for half in range(2):
                w_, wB = wload(l, C_O + half * 512)
                for j in range(4):
                    dcc = half * 4 + j
                    pp, ppB = nb()
                    for kc in range(8):
                        mm(pp[:, :T], w_[:, kc, j * 128:(j + 1) * 128], qtl[:, kc, :T], kc == 0, kc == 7,
                           [Bqtc[kc], wB], [ppB], inc=(kc == 7))
                    if len(pend) == 2:
                        stat_mm(pend.pop(0), T, pnx, pnxB)
                    for (c0, c1, sq) in segs:
                        stt(xT[:, dcc, c0:c1], pp[:, c0:c1], mod[:, l, 16 + dcc, sq:sq + 1], xT[:, dcc, c0:c1],
                            ALU.mult, ALU.add, [ppB, Bmodl[l], BxTc[dcc]], [BxTc[dcc]])
                    pend.append(stat_sq(dcc, T))
            for pd in pend:
                stat_mm(pd, T, pnx, pnxB)
            return stat_finish(T, pnx, pnxB)


        def load_x(tile):
            T = tile["T"]; NB = T // 128; t0 = tile["tok0"]
            P.dma("pool", bufA[:, :NB, :], xin[t0:t0 + T].rearrange("(tb p) d -> p tb d", p=128), writes=[*BAt])

        load_x(tiles[0])
        for ti, tile in enumerate(tiles):
            st["tile"] = ti
            T = tile["T"]; NB = T // 128; t0 = tile["tok0"]
            xtok = bufA
            P.tag = "Xin"
            pn0, pn0B = stat_begin()
            pend = []
            for c in range(8):
                pt, ptB = nb()
                for tb in range(NB):
                    P.op("pe", lambda e, c=c, tb=tb, pt=pt: e.transpose(pt[:, tb * 128:(tb + 1) * 128], xtok[:, tb, c * 128:(c + 1) * 128], identf[:]),
                         [*BAt, Bc], [ptB])
                if len(pend) == 2:
                    stat_mm(pend.pop(0), T, pn0, pn0B)
                evac(xT[:, c, :T], pt[:, :T], [ptB], [BxTc[c]])
                pend.append(stat_sq(c, T))
            for pd in pend:
                stat_mm(pd, T, pn0, pn0B)
            rs = stat_finish(T, pn0, pn0B)
            rs = layer(0, tile, rs)
            nxt = (lambda t=tiles[ti + 1]: load_x(t)) if ti + 1 < len(tiles) else None
            rs = layer(1, tile, rs, after_hgrn=nxt)
            P.tag = "Fin"
            r_, rB = rs
            for c in range(8):
                stt(xT[:, c, :T], xT[:, c, :T], pvgs[:, 16 + c:17 + c], r_[:, :T], ALU.mult, ALU.mult, [BxTc[c], Bc, rB], [BxTc[c]])
            release_tf(r_)
            for tb in range(NB):
                sg_, sgB = nft()
                for half in range(2):
                    pt, ptB = nb()
                    for j in range(4):
                        c = half * 4 + j
                        P.op("pe", lambda e, c=c, j=j, tb=tb, pt=pt: e.transpose(pt[:, j * 128:(j + 1) * 128], xT[:, c, tb * 128:(tb + 1) * 128], identf[:]),
                             [BxTc[c], Bc], [ptB])
                    evac(sg_[:, half * 512:(half + 1) * 512], pt[:, :], [ptB], [sgB[half]])
                P.dma("pool", y[t0 + tb * 128:t0 + (tb + 1) * 128, :], sg_[:], reads=[sgB], sembuf=sgB[0])
        final = []
        for b in (Bvo[0], Bvo[1], BSst[0], BSst[1], Bfh[0], Bfh[2]):
            if b.dsem is not None:
                for ent in b.dsem.values():
                    final.append((ent[0], ent[1]))
        P.wait_tokens("pool", final)
        P.emit(block)
        build_nc.last_prog = P
    return nc


def host_consts():
    identb = np.eye(128, dtype=np.float32).astype(ml_dtypes.bfloat16)
    identf = np.eye(128, dtype=np.float32)
    sp_, t_ = np.meshgrid(np.arange(128), np.arange(128), indexing="ij")
    same = (sp_ // 64) == (t_ // 64)
    sl, tl = sp_ % 64, t_ % 64
    U = (same & (sl <= tl)).astype(np.float32)
    mid = 31
    Mrel = np.where(same & (sl > mid) & (sl <= tl), 1.0, 0.0) - np.where(same & (sl <= mid) & (sl > tl), 1.0, 0.0)
    mru = np.concatenate([Mrel.astype(np.float32), U], axis=1)
    p_, t8 = np.meshgrid(np.arange(128), np.arange(512), indexing="ij")
    mask8 = ((p_ % 64) <= (t8 % 64)).astype(np.float32)
    tril = (sp_ <= t_).astype(np.float32)
    esel = np.zeros((128, 8, 8), np.float32)
    for h in range(8):
        esel[:, h, h] = 1.0
    return dict(identb=identb, identf=identf, mru=np.ascontiguousarray(mru), mask8=mask8, tril=tril,
                esel=esel.reshape(128, 64))


def host_weights(w_ada, b_ada, norm_g, w_in, ln_v_g, ln_v_b, w_s, b_s, lb_raw, gnorm_g, w_pa, w_pb, w_o, final_g):
    f = lambda a: np.asarray(a, dtype=np.float32)
    w_ada, b_ada, norm_g, w_in = f(w_ada), f(b_ada), f(norm_g), f(w_in)
    wall = np.empty((2, D, NCOL), np.float32)
    wall[:, :, 0:7168] = w_in[:, :, 0:7168]
    wall[:, :, C_ADA:C_ADA + 3072] = w_ada
    gA = w_in[:, :, 7168:8192]; gB = w_in[:, :, 8192:9216]
    for dc in range(8):
        s = slice(dc * 128, (dc + 1) * 128)
        o = C_MRG + dc * 512
        wall[:, :, o:o + 128] = f(w_pa)[:, :, s]
        wall[:, :, o + 128:o + 256] = f(w_pb)[:, :, s]
        wall[:, :, o + 256:o + 384] = gA[:, :, s]
        wall[:, :, o + 384:o + 512] = gB[:, :, s]
    wall[:, :, C_O:C_O + 1024] = f(w_o)
    fm = lambda v, n: np.ascontiguousarray(v.reshape(n, 128).T)
    pvx = np.empty((128, 2, 32, 6), np.float32)
    for l in range(2):
        pvx[:, l, 0:24, :] = fm(b_ada[l], 24)[:, :, None]
        pvx[:, l, 24:32, :] = fm(norm_g[l], 8)[:, :, None]
    pvg = np.concatenate([fm(f(gnorm_g)[0], 8), fm(f(gnorm_g)[1], 8), fm(f(final_g), 8)], axis=1)
    bct = np.empty((2, 2, 128, D), np.float32)
    bct[:, 0] = f(ln_v_g)[:, None, :]
    bct[:, 1] = f(ln_v_b)[:, None, :]
    lbr = np.ascontiguousarray(np.broadcast_to(f(lb_raw)[:, None, :], (2, 128, D)))
    w_s, b_s = f(w_s), f(b_s)
    wst = np.zeros((2, 2, 128, 4, 128), np.float32)
    wst[0] = np.transpose(w_s, (0, 3, 1, 2))
    blk = np.transpose(w_s[:, :, 0:64, 0:64], (0, 3, 1, 2))
    wst[1, :, 0:64, :, 0:64] = blk
    wst[1, :, 64:128, :, 64:128] = blk
    bsr = np.empty((2, 2, 128, 512), np.float32)
    bsr[0] = b_s.reshape(2, 1, 512)
    bsr[1] = np.concatenate([b_s[:, :, 0:64], b_s[:, :, 0:64]], axis=2).reshape(2, 1, 512)
    return dict(wall=wall, pvx=pvx, pvg=np.ascontiguousarray(pvg), bct=bct, lbr=lbr, wst=wst, bsr=bsr)


def core_inputs(core, x_prompt, x_sample, state_hgrn, c_prompt, c_sample, shared, npl=2048):
    xp = np.asarray(x_prompt[2 * core:2 * core + 2, :npl], np.float32).reshape(-1, D)
    xs = np.asarray(x_sample[4 * core:4 * core + 4], np.float32).reshape(-1, D)
    c6 = np.concatenate([np.asarray(c_prompt[2 * core:2 * core + 2], np.float32),
                         np.asarray(c_sample[4 * core:4 * core + 4], np.float32)], axis=0)
    cT = np.ascontiguousarray(c6.reshape(6, 8, 128).transpose(2, 1, 0))
    s0 = np.ascontiguousarray(np.asarray(state_hgrn[:, 4 * core:4 * core + 4], np.float32))
    m = dict(shared)
    m.update(xin=np.ascontiguousarray(np.concatenate([xp, xs], axis=0)), cT=cT, s0=s0)
    return m


def kernel(x_prompt, x_sample, state_hgrn, c_prompt, c_sample, w_ada, b_ada, norm_g, w_in, ln_v_g, ln_v_b,
           w_s, b_s, lb_raw, gnorm_g, w_pa, w_pb, w_o, final_g):
    ncores = 8
    shared = host_consts()
    shared.update(host_weights(w_ada, b_ada, norm_g, w_in, ln_v_g, ln_v_b, w_s, b_s, lb_raw, gnorm_g,
                               w_pa, w_pb, w_o, final_g))
    in_maps = [core_inputs(c, x_prompt, x_sample, state_hgrn, c_prompt, c_sample, shared) for c in range(ncores)]
    nc = build_nc(default_tiles(), 4352)
    res = run_bass_kernel_spmd(nc, in_maps, core_ids=list(range(ncores)))
    R = res.results
    y_prompt = np.concatenate([r["y"][:4096].reshape(2, 2048, D) for r in R], axis=0)
    y_sample = np.concatenate([r["y"][4096:].reshape(4, 64, D) for r in R], axis=0)
    sp = np.concatenate([r["sp_out"] for r in R], axis=1)
    ss = np.concatenate([r["ss_out"] for r in R], axis=1)
    vs = np.concatenate([r["v_out"].reshape(2, 4, 64, D) for r in R], axis=1)
    return (y_prompt.astype(np.float32), y_sample.astype(np.float32), sp.astype(np.float32),
            ss.astype(np.float32), vs.astype(np.float32))
```

```python
import numpy as np
import ml_dtypes
from contextlib import ExitStack
import concourse.bass as bass
import concourse.mybir as mybir
from concourse.bass_utils import run_bass_kernel_spmd

F32 = mybir.dt.float32
BF16 = mybir.dt.bfloat16
AF = mybir.ActivationFunctionType
ALU = mybir.AluOpType
ENGS = ("pe", "act", "dve", "pool", "sp")
EPS = 1e-6
D = 1024
NCOL = 15360
C_U, C_V, C_ZA, C_Q, C_F, C_I, C_ZB = 0, 1024, 2048, 3072, 4096, 5120, 6144
C_ADA, C_MRG, C_O = 7168, 10240, 14336


class Buf:
    __slots__ = ("name", "w", "r", "dsem", "dcnt")

    def __init__(self, name):
        self.name = name
        self.w = None
        self.r = []
        self.dsem = None
        self.dcnt = 0


class Prog:
    def __init__(self, nc, sems):
        self.nc = nc
        self.free_sems = list(sems)
        self.esem = {e: self.free_sems.pop() for e in ("pe", "act", "dve", "pool")}
        self.cnt = {e: 0 for e in ("pe", "act", "dve", "pool")}
        self.known = {e: {} for e in ENGS}
        self.q = {e: [] for e in ENGS}
        self.tag = ""
        self.pe_log = []

    def _need(self, eng, toks):
        best = {}
        for t in toks:
            if t is None:
                continue
            sem, val = t
            if self.known[eng].get(id(sem), 0) >= val:
                continue
            if best.get(id(sem), (None, 0))[1] < val:
                best[id(sem)] = (sem, val)
        out = []
        for sem, val in best.values():
            self.known[eng][id(sem)] = val
            out.append((sem, val))
        return out

    @staticmethod
    def _deps(reads, writes):
        toks = []
        for b in reads:
            toks.append(b.w)
        for b in writes:
            toks.append(b.w)
            toks.extend(b.r)
        return toks

    @staticmethod
    def _commit(tok, reads, writes):
        for b in reads:
            if b not in writes:
                b.r.append(tok)
                if len(b.r) > 64:
                    best = {}
                    for s, v in b.r:
                        if best.get(id(s), (None, 0))[1] < v:
                            best[id(s)] = (s, v)
                    b.r = list(best.values())
        for b in writes:
            b.w = tok
            b.r = []

    @staticmethod
    def _flat(bufs):
        out = []
        for b in bufs:
            if isinstance(b, (list, tuple)):
                out.extend(Prog._flat(b))
            elif b not in out:
                out.append(b)
        return out

    def op(self, eng, fn, reads=(), writes=(), inc=True):
        reads, writes = self._flat(reads), self._flat(writes)
        deps = self._deps(reads, writes)
        if eng == "pe":
            deps = [t for t in deps if t is not None and t[0] is not self.esem["pe"]]
        waits = self._need(eng, deps)
        if inc:
            self.cnt[eng] += 1
            tok = (self.esem[eng], self.cnt[eng])
        else:
            tok = (self.esem[eng], self.cnt[eng] + 1)
        self._commit(tok, reads, writes)
        self.q[eng].append((fn, waits, self.esem[eng] if inc else None))
        if eng == "pe":
            self.pe_log.append((self.tag, [(getattr(w[0], "name", str(w[0])), w[1]) for w in waits]))
        return tok

    def dma(self, qeng, out_ap, in_ap, reads=(), writes=(), sembuf=None):
        reads, writes = self._flat(reads), self._flat(writes)
        sb = sembuf if sembuf is not None else (writes[0] if writes else reads[0])
        if sb.dsem is None:
            sb.dsem = {}
        if qeng not in sb.dsem:
            sb.dsem[qeng] = [self.free_sems.pop(), 0]
        ent = sb.dsem[qeng]
        waits = self._need(qeng, self._deps(reads, writes))
        ent[1] += 16
        tok = (ent[0], ent[1])
        self._commit(tok, reads, writes)

        def fn(e, out_ap=out_ap, in_ap=in_ap):
            return e.dma_start(out=out_ap, in_=in_ap)
        self.q[qeng].append((fn, waits, (ent[0], 16)))
        return tok

    def wait_tokens(self, eng, toks):
        waits = self._need(eng, toks)
        if waits:
            self.q[eng].append((None, waits, None))

    def emit(self, block):
        hmap = {"pe": "tensor", "act": "scalar", "dve": "vector", "pool": "gpsimd", "sp": "sync"}

        def make(eng):
            def body(e):
                for fn, waits, inc in self.q[eng]:
                    if fn is None:
                        for sem, val in waits:
                            e.wait_ge(sem, val)
                        continue
                    if eng == "pe":
                        for sem, val in waits:
                            e.wait_ge(sem, val)
                        ins = fn(e)
                    else:
                        for sem, val in waits[1:]:
                            e.wait_ge(sem, val)
                        ins = fn(e)
                        if waits:
                            ins._wait_ge(waits[0][0], waits[0][1])
                    if inc is not None:
                        if isinstance(inc, tuple):
                            ins.then_inc(inc[0], inc[1])
                        else:
                            ins.then_inc(inc, 1)
            return body
        for eng in ENGS:
            getattr(block, hmap[eng])(make(eng))


def default_tiles():
    tiles = []
    for p in range(2):
        for k in range(4):
            tiles.append(dict(kind="p", tok0=p * 2048 + k * 512, T=512, seq=p, first=(k == 0), last=(k == 3)))
    tiles.append(dict(kind="s", tok0=4096, T=256, seq=None, first=True, last=True))
    return tiles


def build_nc(tiles, ntok, nprompt=2):
    nc = bass.Bass("TRN2", target_bir_lowering=False)
    dram_in = lambda name, shape, dt=F32: nc.dram_tensor(name, list(shape), dt, kind="ExternalInput").ap()
    dram_out = lambda name, shape, dt=F32: nc.dram_tensor(name, list(shape), dt, kind="ExternalOutput").ap()
    xin = dram_in("xin", [ntok, D])
    cT = dram_in("cT", [128, 8, 6])
    s0 = dram_in("s0", [2, 4, 8, 128, 128])
    wall = dram_in("wall", [2, D, NCOL])
    pvx = dram_in("pvx", [128, 2, 32, 6])
    pvg = dram_in("pvg", [128, 24])
    bct = dram_in("bct", [2, 2, 128, D])
    lbr = dram_in("lbr", [2, 128, D])
    wst = dram_in("wst", [2, 2, 128, 4, 128])
    bsr = dram_in("bsr", [2, 2, 128, 512])
    identb_d = dram_in("identb", [128, 128], BF16)
    identf_d = dram_in("identf", [128, 128])
    mru_d = dram_in("mru", [128, 256])
    mask8_d = dram_in("mask8", [128, 512])
    tril_d = dram_in("tril", [128, 128])
    esel_d = dram_in("esel", [128, 64])
    y = dram_out("y", [ntok, D])
    sp_out = dram_out("sp_out", [2, nprompt, 8, 128, 128])
    ss_out = dram_out("ss_out", [2, 4, 8, 128, 128])
    v_out = dram_out("v_out", [2, 256, D])
    wsc = nc.dram_tensor("wsc", [2, NCOL // 512, 128, 8 * 512], BF16).ap()

    with ExitStack() as es:
        def sb(name, shape, dt=F32):
            return es.enter_context(nc.sbuf_tensor("sb_" + name, list(shape), dt))
        xT = sb("xT", [128, 8, 512]); BxTc = [Buf(f"xT{c}") for c in range(8)]
        hT = sb("hT", [128, 8, 512], BF16); BhTc = [Buf(f"hT{c}") for c in range(8)]
        bufA = sb("bufA", [128, 4, 1024]); BAt = [Buf(f"bufA{t}") for t in range(4)]
        guT = sb("guT", [128, 8, 512], BF16); Bguc = [Buf(f"gu{c}") for c in range(8)]
        vtok = sb("vtok", [128, 4, 1024], BF16); Bvt = Buf("vtok")
        itok = sb("itok", [128, 4, 1024], BF16); Bit = Buf("itok")
        ktT = sb("ktT", [128, 4, 1024], BF16); BkT = Buf("ktT")
        qsT = sb("qsT", [128, 8, 512], BF16); Bqs = Buf("qs")
        qtl = sb("qtl", [128, 8, 512], BF16); Bqtc = [Buf(f"qtl{c}") for c in range(8)]
        qdc = sb("qdc", [128, 8, 512], BF16); Bqdc = [Buf(f"qdc{c}") for c in range(8)]
        szb = sb("szb", [128, 8, 512], BF16); Bsz = Buf("szb")
        NTF = 8
        tf = [sb(f"tf{i}", [128, 512]) for i in range(NTF)]; Btf = [Buf(f"tf{i}") for i in range(NTF)]
        ft = [sb(f"ft{i}", [128, 1024]) for i in range(2)]
        Bfh = [Buf(f"fh{i}") for i in range(4)]
        Bft = [[Bfh[0], Bfh[1]], [Bfh[2], Bfh[3]]]
        NW = 3
        wb = [sb(f"wb{i}", [128, 8, 512], BF16) for i in range(NW)]; Bwb = [Buf(f"wb{i}") for i in range(NW)]
        S = sb("S", [128, 2, 8, 128]); BS = [[Buf(f"S{l}_{h}") for h in range(8)] for l in range(2)]
        Sb = sb("Sb", [128, 2, 8, 128], BF16); BSb = [[Buf(f"Sb{l}_{h}") for h in range(8)] for l in range(2)]
        BSst = [Buf("Sst0"), Buf("Sst1")]
        Bvo = [Buf("vo0"), Buf("vo1")]
        bcg = sb("bcg", [128, D]); bcb = sb("bcb", [128, D]); Bbc = Buf("bc")
        lbb = sb("lbb", [128, D]); oml = sb("oml", [128, D]); Blb = Buf("lb")
        WTs = sb("WTs", [128, 2, 2, 4, 128], BF16); BWT = Buf("WTs")
        bsbc = sb("bsbc", [128, 512]); Bbs = Buf("bsbc")
        mask8 = sb("mask8", [128, 512]); mru = sb("mru", [128, 256]); tril = sb("tril", [128, 128])
        identb = sb("identb", [128, 128], BF16); identf = sb("identf", [128, 128]); onesf = sb("onesf", [128, 128])
        esel = sb("esel", [128, 64])
        Bc = Buf("consts")
        mod = sb("mod", [128, 2, 24, 6]); Bmodl = [Buf("mod0"), Buf("mod1")]
        pvxs = sb("pvxs", [128, 2, 32, 6]); pvgs = sb("pvgs", [128, 24])
        cTs = sb("cTs", [128, 8, 6]); scb = sb("scb", [128, 8, 6], BF16); Bsc = Buf("scb")
        ATs = sb("ATs", [128, 4, 512], BF16); BAT = [Buf("ATs0"), Buf("ATs1")]
        U = sb("U", [128, 2, 8, 128]); BU = [Buf("U0"), Buf("U1")]
        ec = sb("ec", [128, 2, 8, 8]); Bec = Buf("ec")
        st6 = sb("st6", [128, 4, 2, 2, 3]); mv = sb("mv", [128, 4, 2]); rsv = sb("rsv", [128, 4]); Bst = Buf("st")
        pbank = [es.enter_context(nc.psum_tensor(f"pb{i}", [128, 512], F32)) for i in range(8)]
        Bpb = [Buf(f"pb{i}") for i in range(8)]
        sems = [es.enter_context(nc.semaphore(f"s{i}")) for i in range(60)]
        block = es.enter_context(nc.Block())
        P = Prog(nc, sems)
        st = dict(pb=0, tf=0, ft=0, wb=0, ev=0, tile=0)

        held = set()
        bank_stamp = [0] * 8

        def nb(hold=False):
            cands = [i for i in range(8) if i not in held]
            i = min(cands, key=lambda j: bank_stamp[j])
            st["pb"] += 1
            bank_stamp[i] = st["pb"]
            if hold:
                held.add(i)
            return pbank[i], Bpb[i]

        def release(bank):
            i = pbank.index(bank)
            held.discard(i)
            st["pb"] += 1
            bank_stamp[i] = st["pb"]

        held_tf = set()

        def ntf(hold=False):
            i = st["tf"]
            while i in held_tf:
                i = (i + 1) % NTF
            st["tf"] = (i + 1) % NTF
            if hold:
                held_tf.add(i)
            return tf[i], Btf[i]

        def release_tf(buf):
            held_tf.discard(tf.index(buf))

        def nft():
            i = st["ft"]; st["ft"] = (i + 1) % 2
            return ft[i], Bft[i]

        Bscr = {}

        def wload(l, col0, ncols=512, keep=True):
            i = st["wb"]; st["wb"] = (i + 1) % NW
            key = (l, col0)
            if key in Bscr:
                assert ncols == 512 and col0 % 512 == 0
                P.dma("sp", wb[i][:, :, :].rearrange("p a b -> p (a b)"), wsc[l, col0 // 512], reads=[Bscr[key]], writes=[Bwb[i]])
            else:
                src = wall[l].rearrange("(kc p) n -> p kc n", p=128)[:, :, col0:col0 + ncols]
                P.dma("pool", wb[i][:, :, :ncols], src, writes=[Bwb[i]])
                if keep and (st["tile"] > 0 or (col0 // 512) % 2 == 0):
                    Bscr[key] = Buf(f"scr{l}_{col0}")
                    assert ncols == 512 and col0 % 512 == 0
                    P.dma("sp", wsc[l, col0 // 512], wb[i][:, :, :].rearrange("p a b -> p (a b)"), reads=[Bwb[i]], writes=[Bscr[key]], sembuf=Bwb[i])
            return wb[i], Bwb[i]

        def mm(out, lhsT, rhs, start, stop, reads, writes, inc=True, tp=None):
            if tp is None:
                P.op("pe", lambda e: e.matmul(out, lhsT=lhsT, rhs=rhs, start=start, stop=stop), reads, writes, inc)
            else:
                P.op("pe", lambda e: e.matmul(out, lhsT=lhsT, rhs=rhs, start=start, stop=stop, tile_position=tp), reads, writes, inc)

        def act(out, in_, func, reads, writes, scale=1.0, bias=0.0):
            P.op("act", lambda e: e.activation(out=out, in_=in_, func=func, bias=bias, scale=scale), reads, writes)

        def evac(out, in_, reads, writes):
            st["ev"] ^= 1
            if st["ev"]:
                P.op("act", lambda e: e.activation(out=out, in_=in_, func=AF.Copy), reads, writes)
            else:
                P.op("dve", lambda e: e.tensor_copy(out=out, in_=in_), reads, writes)

        def tt(out, in0, in1, op, reads, writes, eng="dve"):
            P.op(eng, lambda e: e.tensor_tensor(out=out, in0=in0, in1=in1, op=op), reads, writes)

        def ts(out, in0, s1, s2, op0, op1, reads, writes, eng="dve"):
            if s2 is None:
                P.op(eng, lambda e: e.tensor_scalar(out=out, in0=in0, scalar1=s1, scalar2=None, op0=op0), reads, writes)
            else:
                P.op(eng, lambda e: e.tensor_scalar(out=out, in0=in0, scalar1=s1, scalar2=s2, op0=op0, op1=op1), reads, writes)

        def stt(out, in0, scalar, in1, op0, op1, reads, writes, eng="dve"):
            P.op(eng, lambda e: e.scalar_tensor_tensor(out=out, in0=in0, scalar=scalar, in1=in1, op0=op0, op1=op1), reads, writes)

        for dst, src in ((identb, identb_d), (identf, identf_d), (mru, mru_d), (mask8, mask8_d), (tril, tril_d), (esel, esel_d)):
            P.dma("sp", dst[:], src[:, :], writes=[Bc])
        P.dma("sp", pvxs[:], pvx[:, :, :, :], writes=[Bc])
        P.dma("sp", pvgs[:], pvg[:, :], writes=[Bc])
        P.dma("sp", cTs[:], cT[:, :, :], writes=[Bc])
        P.op("dve", lambda e: e.memset(onesf[:], 1.0), writes=[Bc])
        P.op("dve", lambda e: e.memset(ATs[:], 0.0), writes=[BAT[0], BAT[1]])
        mask83 = mask8[:, :].rearrange("p (h t) -> p h t", h=8)
        for var in range(2):
            for l in range(2):
                t_, tB = nft()
                P.dma("sp", t_[:, 0:512].rearrange("p (g t) -> p g t", g=4), wst[var, l], writes=[tB])
                for g in range(4):
                    tt(WTs[:, var, l, g, :], t_[:, g * 128:(g + 1) * 128], tril[:], ALU.mult, [tB, Bc], [BWT])
        t0_, t0B = nft(); t1_, t1B = nft()
        P.dma("sp", t0_[:], lbr[0], writes=[t0B])
        P.dma("sp", t1_[:], lbr[1], writes=[t1B])
        tt(t1_[:], t1_[:], t0_[:], ALU.subtract, [t0B, t1B], [t1B])
        act(lbb[:], t1_[:], AF.Sigmoid, [t1B], [Blb])
        ts(oml[:], lbb[:], -1.0, 1.0, ALU.mult, ALU.add, [Blb], [Blb])
        act(scb[:], cTs[:], AF.Silu, [Bc], [Bsc])
        def mod_block(l, blk, pm, pmB):
            w_, wB = wload(l, C_ADA + blk * 512, keep=False)
            for j in range(4):
                jc = blk * 4 + j
                for kc in range(8):
                    mm(pm[:, jc * 6:jc * 6 + 6], w_[:, kc, j * 128:(j + 1) * 128], scb[:, kc, :],
                       kc == 0, kc == 7, [wB, Bsc], [pmB], inc=(kc == 7))

        def mod_finish(l, pm, pmB):
            release(pm)
            tt(mod[:, l].rearrange("p a b -> p (a b)"), pm[:, 0:144],
               pvxs[:, l, 0:24, :].rearrange("p a b -> p (a b)"), ALU.add, [pmB, Bc], [Bmodl[l]])
            stt(mod[:, l, 8:16, :], mod[:, l, 8:16, :], 1.0, pvxs[:, l, 24:32, :], ALU.add, ALU.mult, [Bmodl[l], Bc], [Bmodl[l]])

        pm0 = nb(hold=True)
        for blk in range(6):
            mod_block(0, blk, *pm0)
        mod_finish(0, *pm0)
        deferred = []
        pm1 = nb(hold=True)
        for blk in range(6):
            deferred.append(lambda blk=blk: mod_block(1, blk, *pm1))
        deferred.append(lambda: mod_finish(1, *pm1))

        def hook():
            if deferred:
                deferred.pop(0)()

        def stat_begin():
            return nb(hold=True)

        def stat_sq(c, T):
            q_, qB = ntf()
            act(q_[:, :T], xT[:, c, :T], AF.Square, [BxTc[c]], [qB])
            return (c, q_, qB)

        def stat_mm(pend, T, pn, pnB):
            c, q_, qB = pend
            mm(pn[:, :T], onesf[:], q_[:, :T], c == 0, c == 7, [qB, Bc], [pnB])

        def stat_finish(T, pn, pnB):
            release(pn)
            a_, aB = ntf()
            act(a_[:, :T], pn[:, :T], AF.Ln, [pnB], [aB], scale=1.0 / D, bias=EPS)
            r_, rB = ntf(hold=True)
            act(r_[:, :T], a_[:, :T], AF.Exp, [aB], [rB], scale=-0.5)
            return r_, rB

        def layer(l, tile, rstat, after_hgrn=None):
            T = tile["T"]; NB = T // 128; NCH = T // 64
            if l == 1:
                while deferred:
                    hook()
            var = 0 if tile["kind"] == "p" else 1
            if tile["kind"] == "p":
                segs = [(0, T, tile["seq"])]
            else:
                segs = [(i * 64, (i + 1) * 64, 2 + i) for i in range(4)]
            P.dma("pool", bcg[:], bct[l, 0], writes=[Bbc])
            P.dma("pool", bcb[:], bct[l, 1], writes=[Bbc])
            P.dma("pool", bsbc[:], bsr[var, l], writes=[Bbs])

            def s0_load(c):
                si = c % 2
                P.dma("pool", S[:, si], s0[l, c].rearrange("h k v -> k h v"), writes=[*BS[si]])

            if tile["kind"] == "s":
                s0_load(0)
                s0_load(1)
            P.tag = "S1"
            r_, rB = rstat
            for c in range(8):
                n_, nB = ntf()
                tt(n_[:, :T], xT[:, c, :T], r_[:, :T], ALU.mult, [BxTc[c], rB], [nB])
                for (c0, c1, sq) in segs:
                    act(hT[:, c, c0:c1], n_[:, c0:c1], AF.Identity, [nB, Bmodl[l]], [BhTc[c]],
                        scale=mod[:, l, 8 + c, sq:sq + 1], bias=mod[:, l, c, sq:sq + 1])
            release_tf(r_)
            P.tag = "S2v"
            hook()
            gvt = bufA
            for half in range(2):
                w_, wB = wload(l, C_V + half * 512)
                if half == 0:
                    bks = [nb() for tb in range(NB)]
                    for kc in range(8):
                        for tb in range(NB):
                            mm(bks[tb][0][:, :], hT[:, kc, tb * 128:(tb + 1) * 128], w_[:, kc, :], kc == 0, kc == 7,
                               [BhTc[kc], wB], [bks[tb][1]], inc=(kc == 7))
                    for tb in range(NB):
                        act(gvt[:, tb, 0:512], bks[tb][0][:, :], AF.Gelu_apprx_tanh, [bks[tb][1]], [BAt[tb]])
                    continue
                for tb in range(NB):
                    pv, pvB = nb()
                    for kc in range(8):
                        mm(pv[:, :], hT[:, kc, tb * 128:(tb + 1) * 128], w_[:, kc, :], kc == 0, kc == 7,
                           [*BhTc, wB], [pvB], inc=(kc == 7))
                    act(gvt[:, tb, half * 512:(half + 1) * 512], pv[:, :], AF.Gelu_apprx_tanh, [pvB], [BAt[tb]])
            for tb in range(NB):
                for half in range(2):
                    P.op("dve", lambda e, tb=tb, half=half: e.bn_stats(st6[:, tb, half], gvt[:, tb, half * 512:(half + 1) * 512]),
                         [BAt[tb]], [Bst])
                P.op("dve", lambda e, tb=tb: e.bn_aggr(mv[:, tb, :], st6[:, tb].rearrange("p k o j -> p (k o) j")), [Bst], [Bst])
            act(rsv[:, :NB], mv[:, :NB, 1], AF.Ln, [Bst], [Bst], scale=1.0, bias=EPS)
            act(rsv[:, :NB], rsv[:, :NB], AF.Exp, [Bst], [Bst], scale=-0.5)
            for tb in range(NB):
                stt(gvt[:, tb, :], gvt[:, tb, :], mv[:, tb, 0:1], bcg[:], ALU.subtract, ALU.mult, [BAt[tb], Bst, Bbc], [BAt[tb]])
                if var == 0:
                    stt(vtok[:, tb, :], gvt[:, tb, :], rsv[:, tb:tb + 1], bcb[:], ALU.mult, ALU.add, [BAt[tb], Bst, Bbc], [Bvt])
                else:
                    stt(gvt[:, tb, :], gvt[:, tb, :], rsv[:, tb:tb + 1], bcb[:], ALU.mult, ALU.add, [BAt[tb], Bst, Bbc], [BAt[tb]])
                    P.dma("pool", v_out[l, tb * 128:(tb + 1) * 128, :], gvt[:, tb, :], reads=[BAt[tb]], sembuf=Bvo[tb])
                    P.op("dve", lambda e, tb=tb: e.tensor_copy(out=vtok[:, tb, :], in_=gvt[:, tb, :]), [BAt[tb]], [Bvt])
            P.tag = "Hq"
            hook()

            def silu_proj(col, dst, dB, half, later=None):
                w_, wB = wload(l, col + half * 512)
                for j in range(4):
                    h = half * 4 + j
                    pq, pqB = nb(hold=(later is not None))
                    for kc in range(8):
                        mm(pq[:, :T], w_[:, kc, j * 128:(j + 1) * 128], hT[:, kc, :T], kc == 0, kc == 7,
                           [*BhTc, wB], [pqB], inc=(kc == 7))

                    def ev(h=h, pq=pq, pqB=pqB):
                        release(pq)
                        act(dst[:, h, :T], pq[:, :T], AF.Silu, [pqB], [dB])
                    if later is None:
                        ev()
                    else:
                        later.append(ev)

            P.tag = "Hf"
            hook()
            logf = bufA

            def i_proj(half):
                w_, wB = wload(l, C_I + half * 512)
                for tb in range(NB):
                    pi_, piB = nb()
                    for kc in range(8):
                        mm(pi_[:, :], hT[:, kc, tb * 128:(tb + 1) * 128], w_[:, kc, :], kc == 0, kc == 7,
                           [*BhTc, wB], [piB], inc=(kc == 7))
                    evac(itok[:, tb, half * 512:(half + 1) * 512], pi_[:, :], [piB], [Bit])

            fh = [(ft[i // 2][:, (i % 2) * 512:(i % 2 + 1) * 512], Bfh[i]) for i in range(4)]

            def f_proj(half):
                w_, wB = wload(l, C_F + half * 512)
                for tb in range(NB):
                    f_, fB = fh[tb]
                    pf, pfB = nb()
                    for kc in range(8):
                        mm(pf[:, :], hT[:, kc, tb * 128:(tb + 1) * 128], w_[:, kc, :], kc == 0, kc == 7,
                           [*BhTc, wB], [pfB], inc=(kc == 7))
                    act(f_, pf[:, :], AF.Sigmoid, [pfB], [fB])

            def f_post_a(half):
                hs = slice(half * 512, (half + 1) * 512)
                for tb in range(NB):
                    f_, fB = fh[tb]
                    if l == 1:
                        tt(f_, f_, oml[:, hs], ALU.mult, [fB, Blb], [fB])
                        tt(f_, f_, lbb[:, hs], ALU.add, [fB, Blb], [fB])
                    act(logf[:, tb, hs], f_, AF.Ln, [fB], [BAt[tb]])
                    ts(f_, f_, -1.0, 1.0, ALU.mult, ALU.add, [fB], [fB])

            def f_post_b(half):
                hs = slice(half * 512, (half + 1) * 512)
                for tb in range(NB):
                    f_, fB = fh[tb]
                    pr, prB = nb()
                    mm(pr[:, :], mru[:, 0:128], logf[:, tb, hs], True, True, [Bc, BAt[tb]], [prB])
                    e_, eB = ntf()
                    act(e_[:, :], pr[:, :], AF.Exp, [prB], [eB], scale=-1.0)
                    tt(ktT[:, tb, hs], f_, e_[:, :], ALU.mult, [fB, eB], [BkT])

            for half in range(2):
                f_proj(half)
                i_proj(half)
                f_post_a(half)
                later = []
                silu_proj(C_ZB, szb, Bsz, half, later=later)
                f_post_b(half)
                for ev in later:
                    ev()
            for half in range(2):
                silu_proj(C_Q, qsT, Bqs, half)
            P.tag = "HRB"
            hook()
            for h in range(8):
                erb_t, erbB = ft[h % 2], Bft[h % 2]
                erb = erb_t[:, 0:NB * 256].rearrange("p (n x) -> p n x", x=256)
                pRC = [nb() for _ in range((NB + 1) // 2)]
                for tb in range(NB):
                    pq_, pqB_ = pRC[tb // 2]
                    mm(pq_[:, (tb % 2) * 256:(tb % 2 + 1) * 256], logf[:, tb, h * 128:(h + 1) * 128], mru[:, 0:256], True, True,
                       [BAt[tb], Bc], [pqB_])
                for i2, (pq_, pqB_) in enumerate(pRC):
                    nb2 = min(2, NB - 2 * i2)
                    act(erb_t[:, i2 * 512:i2 * 512 + nb2 * 256], pq_[:, 0:nb2 * 256], AF.Exp, [pqB_], [erbB[i2] if NB > 2 else erbB])
                q3 = lambda t: t[:, h, :T].rearrange("p (n x) -> p n x", x=128)
                tt(q3(qtl), q3(qsT), erb[:, :, 0:128], ALU.mult, [Bqs, erbB], [Bqtc[h]])
                tt(q3(qdc), q3(qsT), erb[:, :, 128:256], ALU.mult, [Bqs, erbB], [Bqdc[h]])
                src = erb_t[:, 0:NB * 256].rearrange("p (n w c x) -> p w n c x", w=2, c=2, x=64)[:, :, :, :, 63]
                dst = ec[:, :, h, 0:NCH].rearrange("p w (n c) -> p w n c", c=2)
                P.op("dve", lambda e, src=src, dst=dst: e.tensor_copy(out=dst, in_=src), [erbB], [Bec])
            P.tag = "Hkt"
            hook()
            ktl = guT
            for h in range(8):
                pt, ptB = nb()
                ptb = pt[:, :].bitcast(BF16)
                for tb in range(NB):
                    P.op("pe", lambda e, h=h, tb=tb, ptb=ptb: e.transpose(ptb[:, tb * 128:(tb + 1) * 128], ktT[:, tb, h * 128:(h + 1) * 128], identb[:]),
                         [BkT, Bc], [ptB])
                P.op("dve", lambda e, h=h, ptb=ptb: e.tensor_copy(out=ktl[:, h, :T], in_=ptb[:, :T]), [ptB], [Bguc[h]])
            P.tag = "Hchunk"
            hook()
            oT = bufA[:, :, :].rearrange("p a b -> p (a b)").rearrange("p (h t) -> p h t", h=8)
            if tile["kind"] == "p" and tile["first"]:
                P.op("dve", lambda e: e.memset(S[:, l], 0.0), [], [*BS[l]])
                P.op("dve", lambda e: e.memset(Sb[:, l], 0.0), [], [*BSb[l]])

            def stage_a(c):
                tb = c // 2; r0 = (c % 2) * 64; rows = slice(r0, r0 + 64)
                cs = slice(c * 64, (c + 1) * 64); c0_ = c * 64
                tp = (0, 64) if r0 else None
                pa, paB = nb()
                for h in range(8):
                    mm(pa[rows, h * 64 + 32:(h + 1) * 64], ktl[:, h, cs], qtl[:, h, c0_ + 32:c0_ + 64], True, True,
                       [Bguc[h], Bqtc[h]], [paB], inc=False, tp=tp)
                    mm(pa[r0:r0 + 32, h * 64:h * 64 + 32], ktl[:, h, c0_:c0_ + 32], qtl[:, h, c0_:c0_ + 32], True, True,
                       [Bguc[h], Bqtc[h]], [paB], inc=(h == 7), tp=tp)
                pa3 = pa[:, :].rearrange("p (h t) -> p h t", h=8)
                at3 = ATs[:, tb, :].rearrange("p (h t) -> p h t", h=8)
                tt(at3[rows, :, 32:64], pa3[rows, :, 32:64], mask83[rows, :, 32:64], ALU.mult, [paB, Bc], [BAT[c % 2]])
                tt(at3[r0:r0 + 32, :, 0:32], pa3[r0:r0 + 32, :, 0:32], mask83[r0:r0 + 32, :, 0:32], ALU.mult, [paB, Bc], [BAT[c % 2]])
                pk = [nb(), nb()]
                for h in range(8):
                    pk_, pkB = pk[h // 4]
                    mm(pk_[:, (h % 4) * 128:(h % 4 + 1) * 128], ktT[rows, tb, h * 128:(h + 1) * 128],
                       itok[rows, tb, h * 128:(h + 1) * 128], True, True, [BkT, Bit], [pkB], inc=(h % 4 == 3))
                for h in range(8):
                    pk_, pkB = pk[h // 4]
                    act(U[:, c % 2, h, :], pk_[:, (h % 4) * 128:(h % 4 + 1) * 128], AF.Identity, [pkB, Bec], [BU[c % 2]],
                        scale=ec[:, 0, h, c:c + 1])

            def stage_b(c):
                tb = c // 2; r0 = (c % 2) * 64; rows = slice(r0, r0 + 64)
                cs = slice(c * 64, (c + 1) * 64)
                si = l if tile["kind"] == "p" else c % 2
                if tile["kind"] == "s":
                    P.op("act", lambda e: e.activation(out=Sb[:, si], in_=S[:, si], func=AF.Copy), [*BS[si]], [*BSb[si]])
                po, poB = nb()
                for h in range(8):
                    mm(po[:, h * 64:(h + 1) * 64], itok[rows, tb, h * 128:(h + 1) * 128], ATs[rows, tb, h * 64:(h + 1) * 64],
                       True, False, [Bit, BAT[c % 2]], [poB], inc=False)
                    mm(po[:, h * 64:(h + 1) * 64], Sb[:, si, h, :], qdc[:, h, cs], False, True, [BSb[si][h], Bqdc[h]], [poB], inc=True)
                evac(oT[:, :, cs], po[:, :].rearrange("p (h t) -> p h t", h=8), [poB], [*BAt])
                for h in range(8):
                    stt(S[:, si, h, :], S[:, si, h, :], ec[:, 1, h, c:c + 1], U[:, c % 2, h, :], ALU.mult, ALU.add,
                        [BS[si][h], Bec, BU[c % 2]], [BS[si][h]])
                    if tile["kind"] == "p":
                        P.op("act", lambda e, h=h: e.activation(out=Sb[:, si, h, :], in_=S[:, si, h, :], func=AF.Copy),
                             [BS[si][h]], [BSb[si][h]])
                if tile["kind"] == "s":
                    P.dma("pool", ss_out[l, c].rearrange("h k v -> k h v"), S[:, si], reads=[*BS[si]], sembuf=BSst[si])
                    if c + 2 < NCH:
                        s0_load(c + 2)

            stage_a(0)
            for c in range(NCH):
                if c + 1 < NCH:
                    stage_a(c + 1)
                stage_b(c)
            if tile["kind"] == "p" and tile["last"]:
                P.dma("pool", sp_out[l, tile["seq"]].rearrange("h k v -> k h v"), S[:, l], reads=[*BS[l]], sembuf=BSst[l])

            def gnorm_sq(h):
                q_, qB = ntf()
                tt(q_[:, :T], oT[:, h, :T], oT[:, h, :T], ALU.mult, [*BAt], [qB])
                return (h, q_, qB)

            def gnorm_mm(pend, gb):
                h, q_, qB = pend
                pn, pnB = nb(hold=True)
                mm(pn[:, :T], onesf[:], q_[:, :T], True, True, [qB, Bc], [pnB])
                gb[h] = (pn, pnB)

            def gnorm_finish(gb):
                rs_ = {}
                for h, (pn, pnB) in gb.items():
                    a_, aB = ntf()
                    release(pn)
                    act(a_[:, :T], pn[:, :T], AF.Ln, [pnB], [aB], scale=1.0 / 128, bias=EPS)
                    rs_[h] = (a_, aB)
                for h, (a_, aB) in rs_.items():
                    act(a_[:, :T], a_[:, :T], AF.Exp, [aB], [aB], scale=-0.5)
                for h, (a_, aB) in rs_.items():
                    tt(a_[:, :T], a_[:, :T], oT[:, h, :T], ALU.mult, [aB, *BAt], [aB])
                    stt(qsT[:, h, :T], a_[:, :T], pvgs[:, l * 8 + h:l * 8 + h + 1], szb[:, h, :T], ALU.mult, ALU.mult,
                        [aB, Bc, Bsz], [Bqs])

            P.tag = "S3a"
            hook()
            pq = []
            groups = [dict(), dict()]
            for half in range(2):
                w_, wB = wload(l, C_U + half * 512)
                for j in range(4):
                    fc = half * 4 + j
                    pu, puB = nb()
                    for kc in range(8):
                        mm(pu[:, :T], w_[:, kc, j * 128:(j + 1) * 128], hT[:, kc, :T], kc == 0, kc == 7,
                           [*BhTc, wB], [puB], inc=(kc == 7))
                    if len(pq) == 3:
                        pd = pq.pop(0)
                        gnorm_mm(pd, groups[pd[0] // 4])
                        if pd[0] == 3:
                            gnorm_finish(groups[0])
                    act(guT[:, fc, :T], pu[:, :T], AF.Gelu_apprx_tanh, [puB], [Bguc[fc]])
                    pq.append(gnorm_sq(fc))
            gn_tail = pq
            P.tag = "S3b"
            hook()
            for half in range(2):
                w_, wB = wload(l, C_ZA + half * 512)
                for j in range(4):
                    fc = half * 4 + j; g = fc // 2
                    pz, pzB = nb()
                    for kc in range(8):
                        mm(pz[:, :T], w_[:, kc, j * 128:(j + 1) * 128], hT[:, kc, :T], kc == 0, kc == 7,
                           [*BhTc, wB], [pzB], inc=(kc == 7))
                    if gn_tail:
                        gnorm_mm(gn_tail.pop(0), groups[1])
                    z_, zB = ntf()
                    act(z_[:, :T], pz[:, :T], AF.Silu, [pzB], [zB])
                    ps_, psB = nb()
                    for tb in range(NB):
                        mm(ps_[:, tb * 128:(tb + 1) * 128], vtok[:, tb, fc * 128:(fc + 1) * 128], WTs[:, var, l, g, :],
                           True, True, [Bvt, BWT], [psB], inc=(tb == NB - 1))
                    tt(z_[:, :T], z_[:, :T], guT[:, fc, :T], ALU.mult, [zB, Bguc[fc]], [zB])
                    s_, sB = ntf()
                    tt(s_[:, :T].rearrange("p (n t) -> p n t", t=128), ps_[:, :T].rearrange("p (n t) -> p n t", t=128),
                       bsbc[:, g * 128:(g + 1) * 128].unsqueeze(1).to_broadcast([128, NB, 128]), ALU.add, [psB, Bbs], [sB])
                    tt(guT[:, fc, :T], s_[:, :T], z_[:, :T], ALU.mult, [sB, zB], [Bguc[fc]])
            P.tag = "S5"
            gnorm_finish(groups[1])
            if after_hgrn is not None:
                after_hgrn()

            def merge_gates(dc):
                w_, wB = wload(l, C_MRG + dc * 512)
                gates = []
                for j in (2, 3):
                    pp, ppB = nb()
                    for kc in range(8):
                        mm(pp[:, :T], w_[:, kc, j * 128:(j + 1) * 128], hT[:, kc, :T], kc == 0, kc == 7,
                           [*BhTc, wB], [ppB], inc=(kc == 7))
                    g_, gB_ = ntf(hold=True)
                    act(g_[:, :T], pp[:, :T], AF.Sigmoid, [ppB], [gB_])
                    gates.append((g_, gB_))
                return w_, wB, gates

            def merge_rest(dc, pre):
                w_, wB, gates = pre
                for j, (src, sB) in ((0, (guT, Bguc)), (1, (qsT, [Bqs]))):
                    pp, ppB = nb()
                    for kc in range(8):
                        mm(pp[:, :T], w_[:, kc, j * 128:(j + 1) * 128], src[:, kc, :T], kc == 0, kc == 7,
                           [sB[kc] if len(sB) == 8 else sB[0], wB], [ppB], inc=(kc == 7))
                    g_, gB_ = gates[j]
                    tt(g_[:, :T], pp[:, :T], g_[:, :T], ALU.mult, [ppB, gB_], [gB_])
                (ga, gaB), (gb, gbB) = gates
                tt(qtl[:, dc, :T], ga[:, :T], gb[:, :T], ALU.add, [gaB, gbB], [Bqtc[dc]])
                release_tf(ga); release_tf(gb)

            pre = merge_gates(0)
            for dc in range(8):
                nxt = merge_gates(dc + 1) if dc + 1 < 8 else None
                merge_rest(dc, pre)
                pre = nxt
            P.tag = "S6"
            pnx, pnxB = stat_begin()
            pend = []
            for half in range(2):
                w_, wB = wload(l, C_O + half * 512)
                for j in range(4):
                    dcc = half * 4 + j
                    pp, ppB = nb()
                    for kc in range(8):
                        mm(pp[:, :T], w_[:, kc, j * 128:(j + 1) * 128], qtl[:, kc, :T], kc == 0, kc == 7,
                           [Bqtc[kc], wB], [ppB], inc=(kc == 7))
                    if len(pend) == 3:
                        stat_mm(pend.pop(0), T, pnx, pnxB)
                    for (c0, c1, sq) in segs:
                        stt(xT[:, dcc, c0:c1], pp[:, c0:c1], mod[:, l, 16 + dcc, sq:sq + 1], xT[:, dcc, c0:c1],
                            ALU.mult, ALU.add, [ppB, Bmodl[l], BxTc[dcc]], [BxTc[dcc]])
                    pend.append(stat_sq(dcc, T))
            for pd in pend:
                stat_mm(pd, T, pnx, pnxB)
            return stat_finish(T, pnx, pnxB)


        def load_x(tile):
            T = tile["T"]; NB = T // 128; t0 = tile["tok0"]
            P.dma("pool", bufA[:, :NB, :], xin[t0:t0 + T].rearrange("(tb p) d -> p tb d", p=128), writes=[*BAt])

        load_x(tiles[0])
        for ti, tile in enumerate(tiles):
            st["tile"] = ti
            T = tile["T"]; NB = T // 128; t0 = tile["tok0"]
            xtok = bufA
            P.tag = "Xin"
            pn0, pn0B = stat_begin()
            pend = []
            for c in range(8):
                pt, ptB = nb()
                for tb in range(NB):
                    P.op("pe", lambda e, c=c, tb=tb, pt=pt: e.transpose(pt[:, tb * 128:(tb + 1) * 128], xtok[:, tb, c * 128:(c + 1) * 128], identf[:]),
                         [*BAt, Bc], [ptB])
                if len(pend) == 3:
                    stat_mm(pend.pop(0), T, pn0, pn0B)
                evac(xT[:, c, :T], pt[:, :T], [ptB], [BxTc[c]])
                pend.append(stat_sq(c, T))
            for pd in pend:
                stat_mm(pd, T, pn0, pn0B)
            rs = stat_finish(T, pn0, pn0B)
            rs = layer(0, tile, rs)
            nxt = (lambda t=tiles[ti + 1]: load_x(t)) if ti + 1 < len(tiles) else None
            rs = layer(1, tile, rs, after_hgrn=nxt)
            P.tag = "Fin"
            r_, rB = rs
            for c in range(8):
                stt(xT[:, c, :T], xT[:, c, :T], pvgs[:, 16 + c:17 + c], r_[:, :T], ALU.mult, ALU.mult, [BxTc[c], Bc, rB], [BxTc[c]])
            release_tf(r_)
            for tb in range(NB):
                sg_, sgB = nft()
                for half in range(2):
                    pt, ptB = nb()
                    for j in range(4):
                        c = half * 4 + j
                        P.op("pe", lambda e, c=c, j=j, tb=tb, pt=pt: e.transpose(pt[:, j * 128:(j + 1) * 128], xT[:, c, tb * 128:(tb + 1) * 128], identf[:]),
                             [BxTc[c], Bc], [ptB])
                    evac(sg_[:, half * 512:(half + 1) * 512], pt[:, :], [ptB], [sgB[half]])
                P.dma("pool", y[t0 + tb * 128:t0 + (tb + 1) * 128, :], sg_[:], reads=[sgB], sembuf=sgB[0])
        final = []
        for b in (Bvo[0], Bvo[1], BSst[0], BSst[1], Bfh[0], Bfh[2]):
            if b.dsem is not None:
                for ent in b.dsem.values():
                    final.append((ent[0], ent[1]))
        P.wait_tokens("pool", final)
        P.emit(block)
        build_nc.last_prog = P
    return nc


def host_consts():
    identb = np.eye(128, dtype=np.float32).astype(ml_dtypes.bfloat16)
    identf = np.eye(128, dtype=np.float32)
    sp_, t_ = np.meshgrid(np.arange(128), np.arange(128), indexing="ij")
    same = (sp_ // 64) == (t_ // 64)
    sl, tl = sp_ % 64, t_ % 64
    U = (same & (sl <= tl)).astype(np.float32)
    mid = 31
    Mrel = np.where(same & (sl > mid) & (sl <= tl), 1.0, 0.0) - np.where(same & (sl <= mid) & (sl > tl), 1.0, 0.0)
    mru = np.concatenate([Mrel.astype(np.float32), U], axis=1)
    p_, t8 = np.meshgrid(np.arange(128), np.arange(512), indexing="ij")
    mask8 = ((p_ % 64) <= (t8 % 64)).astype(np.float32)
    tril = (sp_ <= t_).astype(np.float32)
    esel = np.zeros((128, 8, 8), np.float32)
    for h in range(8):
        esel[:, h, h] = 1.0
    return dict(identb=identb, identf=identf, mru=np.ascontiguousarray(mru), mask8=mask8, tril=tril,
                esel=esel.reshape(128, 64))


def host_weights(w_ada, b_ada, norm_g, w_in, ln_v_g, ln_v_b, w_s, b_s, lb_raw, gnorm_g, w_pa, w_pb, w_o, final_g):
    f = lambda a: np.asarray(a, dtype=np.float32)
    w_ada, b_ada, norm_g, w_in = f(w_ada), f(b_ada), f(norm_g), f(w_in)
    wall = np.empty((2, D, NCOL), np.float32)
    wall[:, :, 0:7168] = w_in[:, :, 0:7168]
    wall[:, :, C_ADA:C_ADA + 3072] = w_ada
    gA = w_in[:, :, 7168:8192]; gB = w_in[:, :, 8192:9216]
    for dc in range(8):
        s = slice(dc * 128, (dc + 1) * 128)
        o = C_MRG + dc * 512
        wall[:, :, o:o + 128] = f(w_pa)[:, :, s]
        wall[:, :, o + 128:o + 256] = f(w_pb)[:, :, s]
        wall[:, :, o + 256:o + 384] = gA[:, :, s]
        wall[:, :, o + 384:o + 512] = gB[:, :, s]
    wall[:, :, C_O:C_O + 1024] = f(w_o)
    fm = lambda v, n: np.ascontiguousarray(v.reshape(n, 128).T)
    pvx = np.empty((128, 2, 32, 6), np.float32)
    for l in range(2):
        pvx[:, l, 0:24, :] = fm(b_ada[l], 24)[:, :, None]
        pvx[:, l, 24:32, :] = fm(norm_g[l], 8)[:, :, None]
    pvg = np.concatenate([fm(f(gnorm_g)[0], 8), fm(f(gnorm_g)[1], 8), fm(f(final_g), 8)], axis=1)
    bct = np.empty((2, 2, 128, D), np.float32)
    bct[:, 0] = f(ln_v_g)[:, None, :]
    bct[:, 1] = f(ln_v_b)[:, None, :]
    lbr = np.ascontiguousarray(np.broadcast_to(f(lb_raw)[:, None, :], (2, 128, D)))
    w_s, b_s = f(w_s), f(b_s)
    wst = np.zeros((2, 2, 128, 4, 128), np.float32)
    wst[0] = np.transpose(w_s, (0, 3, 1, 2))
    blk = np.transpose(w_s[:, :, 0:64, 0:64], (0, 3, 1, 2))
    wst[1, :, 0:64, :, 0:64] = blk
    wst[1, :, 64:128, :, 64:128] = blk
    bsr = np.empty((2, 2, 128, 512), np.float32)
    bsr[0] = b_s.reshape(2, 1, 512)
    bsr[1] = np.concatenate([b_s[:, :, 0:64], b_s[:, :, 0:64]], axis=2).reshape(2, 1, 512)
    return dict(wall=wall, pvx=pvx, pvg=np.ascontiguousarray(pvg), bct=bct, lbr=lbr, wst=wst, bsr=bsr)


def core_inputs(core, x_prompt, x_sample, state_hgrn, c_prompt, c_sample, shared, npl=2048):
    xp = np.asarray(x_prompt[2 * core:2 * core + 2, :npl], np.float32).reshape(-1, D)
    xs = np.asarray(x_sample[4 * core:4 * core + 4], np.float32).reshape(-1, D)
    c6 = np.concatenate([np.asarray(c_prompt[2 * core:2 * core + 2], np.float32),
                         np.asarray(c_sample[4 * core:4 * core + 4], np.float32)], axis=0)
    cT = np.ascontiguousarray(c6.reshape(6, 8, 128).transpose(2, 1, 0))
    s0 = np.ascontiguousarray(np.asarray(state_hgrn[:, 4 * core:4 * core + 4], np.float32))
    m = dict(shared)
    m.update(xin=np.ascontiguousarray(np.concatenate([xp, xs], axis=0)), cT=cT, s0=s0)
    return m


def kernel(x_prompt, x_sample, state_hgrn, c_prompt, c_sample, w_ada, b_ada, norm_g, w_in, ln_v_g, ln_v_b,
           w_s, b_s, lb_raw, gnorm_g, w_pa, w_pb, w_o, final_g):
    ncores = 8
    shared = host_consts()
    shared.update(host_weights(w_ada, b_ada, norm_g, w_in, ln_v_g, ln_v_b, w_s, b_s, lb_raw, gnorm_g,
                               w_pa, w_pb, w_o, final_g))
    in_maps = [core_inputs(c, x_prompt, x_sample, state_hgrn, c_prompt, c_sample, shared) for c in range(ncores)]
    nc = build_nc(default_tiles(), 4352)
    res = run_bass_kernel_spmd(nc, in_maps, core_ids=list(range(ncores)))
    R = res.results
    y_prompt = np.concatenate([r["y"][:4096].reshape(2, 2048, D) for r in R], axis=0)
    y_sample = np.concatenate([r["y"][4096:].reshape(4, 64, D) for r in R], axis=0)
    sp = np.concatenate([r["sp_out"] for r in R], axis=1)
    ss = np.concatenate([r["ss_out"] for r in R], axis=1)
    vs = np.concatenate([r["v_out"].reshape(2, 4, 64, D) for r in R], axis=1)
    return (y_prompt.astype(np.float32), y_sample.astype(np.float32), sp.astype(np.float32),
            ss.astype(np.float32), vs.astype(np.float32))
```
